# Optimizing a Trainium2 kernel written in Bass

```python
import jax, jax.numpy as jnp
from jax import lax
import numpy as np

D_MODEL = 1024
BATCH = 2
SEQ = 8192
DEPTH = 1
DEC_BATCH = 32
DEC_SEQ = 1
PAST_LEN = 16384
PAGE_SIZE = 128

FOX_HEADS = 8
FOX_HEAD_DIM = 64
FOX_WIDTH = FOX_HEADS * FOX_HEAD_DIM
FOX_GATE_BIAS = 4.0
Q_BLOCK = 128
GLA_HEADS = 4
GLA_DK = 64
GLA_DV = 128
GLA_KW = GLA_HEADS * GLA_DK
GLA_VW = GLA_HEADS * GLA_DV
GLA_GATE_RANK = 16
GLA_GATE_NORM = 16.0
GLA_CHUNK = 32
MIX_WIDTH = FOX_WIDTH + GLA_VW
D_FF = 4 * D_MODEL
EPS = 1e-6
SPLIT_SIZES = (FOX_WIDTH, FOX_WIDTH, FOX_WIDTH, FOX_HEADS, GLA_KW, GLA_KW, GLA_VW, GLA_GATE_RANK, GLA_VW)
IN_COLS = FOX_WIDTH * 3 + FOX_HEADS + GLA_KW * 2 + GLA_VW * 2 + GLA_GATE_RANK

kernel_name = 'fox_gla_parallel_heads_decode_step'


def rmsnorm(x, g):
    xf = x.astype(jnp.float32)
    y = xf * lax.rsqrt(jnp.mean(xf * xf, axis=-1, keepdims=True) + EPS)
    return (y * g.astype(jnp.float32)).astype(x.dtype)


def project(xn, w_in, fox_b_f, gla_w_gate_up, gla_b_gate):
    B, L, _ = xn.shape
    z = xn @ w_in
    offs = np.cumsum(SPLIT_SIZES)[:-1].tolist()
    fq, fk, fv, ff, gq, gk, gv, glr, gg = jnp.split(z, offs, axis=-1)
    fox_shape = (B, L, FOX_HEADS, FOX_HEAD_DIM)
    fox_logf = jax.nn.log_sigmoid(ff.astype(jnp.float32) + fox_b_f.astype(jnp.float32))
    gla_log_a = jax.nn.log_sigmoid((glr @ gla_w_gate_up).astype(jnp.float32)
                                   + gla_b_gate.astype(jnp.float32)) / GLA_GATE_NORM
    gla_log_a = gla_log_a.reshape(B, L, GLA_HEADS, GLA_DK)
    return (fq.reshape(fox_shape), fk.reshape(fox_shape), fv.reshape(fox_shape), fox_logf,
            gq.reshape(B, L, GLA_HEADS, GLA_DK), gk.reshape(B, L, GLA_HEADS, GLA_DK),
            gv.reshape(B, L, GLA_HEADS, GLA_DV), gla_log_a, gg)


def fox_prompt(q, k, v, logf):
    B, S, H, dh = q.shape
    scale = dh ** -0.5
    c = jnp.cumsum(logf, axis=1).swapaxes(1, 2)
    nb = S // Q_BLOCK
    qb = q.reshape(B, nb, Q_BLOCK, H, dh).swapaxes(0, 1)
    cb = c.reshape(B, H, nb, Q_BLOCK).transpose(2, 0, 1, 3)
    kpos = jnp.arange(S)

    def one_block(args):
        i, q_i, c_i = args
        s = jnp.einsum('bqhd,bkhd->bhqk', q_i, k).astype(jnp.float32) * scale
        s = s + (c_i[..., :, None] - c[:, :, None, :])
        qpos = i * Q_BLOCK + jnp.arange(Q_BLOCK)
        s = jnp.where(kpos[None, :] <= qpos[:, None], s, -jnp.inf)
        p = jax.nn.softmax(s, axis=-1)
        return jnp.einsum('bhqk,bkhd->bqhd', p.astype(v.dtype), v)

    o = lax.map(one_block, (jnp.arange(nb), qb, cb))
    return o.swapaxes(0, 1).reshape(B, S, H, dh)


def fox_sample(q, k_new, v_new, logf_new, cache_k, cache_v, cache_logf, page_table):
    Bd, L, H, dh = q.shape
    n_pages = page_table.shape[1]
    past = n_pages * PAGE_SIZE
    scale = dh ** -0.5
    k_past = cache_k[page_table].reshape(Bd, past, H, dh).astype(q.dtype)
    v_past = cache_v[page_table].reshape(Bd, past, H, dh).astype(q.dtype)
    lf_past = cache_logf[page_table].reshape(Bd, past, H).astype(jnp.float32)
    k_all = jnp.concatenate([k_past, k_new], axis=1)
    v_all = jnp.concatenate([v_past, v_new], axis=1)
    c = jnp.cumsum(jnp.concatenate([lf_past, logf_new], axis=1), axis=1).swapaxes(1, 2)
    s = jnp.einsum('bqhd,bkhd->bhqk', q, k_all).astype(jnp.float32) * scale
    s = s + (c[:, :, past:, None] - c[:, :, None, :])
    qpos = past + jnp.arange(L)
    kpos = jnp.arange(past + L)
    s = jnp.where(kpos[None, :] <= qpos[:, None], s, -jnp.inf)
    p = jax.nn.softmax(s, axis=-1)
    return jnp.einsum('bhqk,bkhd->bqhd', p.astype(v_all.dtype), v_all)


def gla_chunked(q, k, v, log_a, s0):
    B, L, H, dk = q.shape
    dv = v.shape[-1]
    c = min(GLA_CHUNK, L)
    pad = (-L) % c
    q = q.astype(jnp.float32) * (dk ** -0.5)
    k = k.astype(jnp.float32)
    v = v.astype(jnp.float32)
    if pad:
        pw = ((0, 0), (0, pad), (0, 0), (0, 0))
        q, k, v, log_a = (jnp.pad(t, pw) for t in (q, k, v, log_a))
    n = (L + pad) // c

    def to_chunks(t):
        return t.reshape(B, n, c, *t.shape[2:]).swapaxes(0, 1)

    tril = jnp.tril(jnp.ones((c, c), dtype=bool))

    def step(S, inp):
        qc, kc, vc, ac = inp
        b = jnp.cumsum(ac, axis=1)
        b_ref = b[:, c // 2:c // 2 + 1]
        b_last = b[:, -1:]
        A = jnp.einsum('bihd,bjhd->bhij', qc * jnp.exp(b - b_ref), kc * jnp.exp(b_ref - b))
        A = jnp.where(tril, A, 0.0)
        o = (jnp.einsum('bhij,bjhv->bihv', A, vc)
             + jnp.einsum('bihd,bhdv->bihv', qc * jnp.exp(b), S))
        S = (S * jnp.exp(b_last[:, 0])[..., None]
             + jnp.einsum('bjhd,bjhv->bhdv', kc * jnp.exp(b_last - b), vc))
        return S, o

    S_fin, o = lax.scan(step, s0.astype(jnp.float32),
                        (to_chunks(q), to_chunks(k), to_chunks(v), to_chunks(log_a)))
    o = o.swapaxes(0, 1).reshape(B, n * c, H, dv)[:, :L]
    return o, S_fin


def merge_and_ffn(h, fox_o, gla_o, gla_g, gla_norm_g, w_o, norm2_g, w_up, w_down):
    B, L, _ = h.shape
    go = rmsnorm(gla_o, gla_norm_g).astype(h.dtype) * jax.nn.silu(gla_g).reshape(B, L, GLA_HEADS, GLA_DV)
    mixed = jnp.concatenate([fox_o.reshape(B, L, FOX_WIDTH).astype(h.dtype),
                             go.reshape(B, L, GLA_VW)], axis=-1)
    h = h + mixed @ w_o
    u = jnp.square(jax.nn.relu(rmsnorm(h, norm2_g) @ w_up))
    return h + u @ w_down


def setup_inputs(seed: int = 0) -> dict:
    key = jax.random.key(seed)
    ks = jax.random.split(key, 20)
    nrm = jax.random.normal
    n_pages = PAST_LEN // PAGE_SIZE
    n_phys = (DEC_BATCH * n_pages * 5) // 4
    page_table = jax.random.permutation(ks[6], n_phys)[:DEC_BATCH * n_pages]
    page_table = page_table.reshape(DEC_BATCH, n_pages).astype(jnp.int32)
    return {
        'x_prompt': nrm(ks[0], (BATCH, SEQ, D_MODEL), jnp.float32),
        'x_sample': nrm(ks[1], (DEC_BATCH, DEC_SEQ, D_MODEL), jnp.float32),
        'cache_k': nrm(ks[2], (DEPTH, n_phys, PAGE_SIZE, FOX_HEADS, FOX_HEAD_DIM), jnp.float32),
        'cache_v': nrm(ks[3], (DEPTH, n_phys, PAGE_SIZE, FOX_HEADS, FOX_HEAD_DIM), jnp.float32),
        'cache_logf': jax.nn.log_sigmoid(FOX_GATE_BIAS + nrm(ks[4], (DEPTH, n_phys, PAGE_SIZE, FOX_HEADS), jnp.float32)),
        'state_gla': 0.5 * nrm(ks[5], (DEPTH, DEC_BATCH, GLA_HEADS, GLA_DK, GLA_DV), jnp.float32),
        'page_table': page_table,
        'norm1_g': 1.0 + 0.02 * nrm(ks[7], (DEPTH, D_MODEL), jnp.float32),
        'w_in': nrm(ks[8], (DEPTH, D_MODEL, IN_COLS), jnp.float32) * D_MODEL ** -0.5,
        'fox_b_f': FOX_GATE_BIAS + 0.5 * nrm(ks[9], (DEPTH, FOX_HEADS), jnp.float32),
        'gla_w_gate_up': nrm(ks[10], (DEPTH, GLA_GATE_RANK, GLA_KW), jnp.float32) * GLA_GATE_RANK ** -0.5,
        'gla_b_gate': 0.1 * nrm(ks[11], (DEPTH, GLA_KW), jnp.float32),
        'gla_norm_g': 1.0 + 0.02 * nrm(ks[12], (DEPTH, GLA_DV), jnp.float32),
        'w_o': nrm(ks[13], (DEPTH, MIX_WIDTH, D_MODEL), jnp.float32) * MIX_WIDTH ** -0.5,
        'norm2_g': 1.0 + 0.02 * nrm(ks[14], (DEPTH, D_MODEL), jnp.float32),
        'w_up': nrm(ks[15], (DEPTH, D_MODEL, D_FF), jnp.float32) * D_MODEL ** -0.5,
        'w_down': nrm(ks[16], (DEPTH, D_FF, D_MODEL), jnp.float32) * D_FF ** -0.5,
        'final_g': 1.0 + 0.02 * nrm(ks[17], (D_MODEL,), jnp.float32),
    }


def reference(x_prompt, x_sample, cache_k, cache_v, cache_logf, state_gla, page_table,
              norm1_g, w_in, fox_b_f, gla_w_gate_up, gla_b_gate, gla_norm_g, w_o,
              norm2_g, w_up, w_down, final_g):
    hp, hs = x_prompt, x_sample
    kp_l, vp_l, fp_l, sp_l = [], [], [], []
    ks_l, vs_l, fs_l, ss_l = [], [], [], []
    for l in range(DEPTH):
        xn = rmsnorm(hp, norm1_g[l])
        fq, fk, fv, flf, gq, gk, gv, gla_a, gg = project(xn, w_in[l], fox_b_f[l], gla_w_gate_up[l], gla_b_gate[l])
        fox_o = fox_prompt(fq, fk, fv, flf)
        s_init = jnp.zeros((hp.shape[0], GLA_HEADS, GLA_DK, GLA_DV), jnp.float32)
        gla_o, s_p = gla_chunked(gq, gk, gv, gla_a, s_init)
        hp = merge_and_ffn(hp, fox_o, gla_o, gg, gla_norm_g[l], w_o[l], norm2_g[l], w_up[l], w_down[l])
        kp_l.append(fk); vp_l.append(fv); fp_l.append(flf); sp_l.append(s_p)
        xn = rmsnorm(hs, norm1_g[l])
        fq, fk, fv, flf, gq, gk, gv, gla_a, gg = project(xn, w_in[l], fox_b_f[l], gla_w_gate_up[l], gla_b_gate[l])
        fox_o = fox_sample(fq, fk, fv, flf, cache_k[l], cache_v[l], cache_logf[l], page_table)
        gla_o, s_s = gla_chunked(gq, gk, gv, gla_a, state_gla[l])
        hs = merge_and_ffn(hs, fox_o, gla_o, gg, gla_norm_g[l], w_o[l], norm2_g[l], w_up[l], w_down[l])
        ks_l.append(fk); vs_l.append(fv); fs_l.append(flf); ss_l.append(s_s)
    y_prompt = rmsnorm(hp, final_g)
    y_sample = rmsnorm(hs, final_g)
    return (y_prompt, y_sample,
            jnp.stack(kp_l), jnp.stack(vp_l), jnp.stack(fp_l), jnp.stack(sp_l),
            jnp.stack(ks_l), jnp.stack(vs_l), jnp.stack(fs_l), jnp.stack(ss_l))
```

```python
import numpy as np
import concourse.bass as bass
import concourse.mybir as mybir
from concourse.bass_utils import run_bass_kernel_spmd

F32 = mybir.dt.float32
F32R = mybir.dt.float32r
BF16 = mybir.dt.bfloat16
I32 = mybir.dt.int32
ALU = mybir.AluOpType
AF = mybir.ActivationFunctionType
AX = mybir.AxisListType


import types


def _freeze(fn):
    if fn is None or fn.__closure__ is None:
        return fn
    cells = []
    for c in fn.__closure__:
        try:
            cells.append(types.CellType(c.cell_contents))
        except ValueError:
            cells.append(c)
    return types.FunctionType(fn.__code__, fn.__globals__, fn.__name__, fn.__defaults__, tuple(cells))


class Buf:
    __slots__ = ("name", "w", "r", "excl")

    def __init__(self, name, excl=False):
        self.name = name
        self.excl = excl
        self.w = None
        self.r = {}


class Sched:
    COMPUTE = ("pe", "act", "dve", "pool")

    def __init__(self, nc, n_dma_sems=40):
        self.nc = nc
        self.engs = {"pe": nc.tensor, "act": nc.scalar, "dve": nc.vector, "pool": nc.gpsimd, "sp": nc.sync}
        self.prog = {e: [] for e in self.engs}
        self.sems = {}
        for e in self.COMPUTE:
            self.sems[e] = nc.alloc_semaphore(name="c_" + e)
        self.cnt = {e: 0 for e in self.COMPUTE}
        self.dsem = [nc.alloc_semaphore(name="d%d" % i) for i in range(n_dma_sems)]
        self.dcnt = [0] * n_dma_sems
        self.drr = 0
        self.waited = {e: {} for e in self.engs}
        self.n_ops = 0

    def _sem(self, key):
        return self.sems[key] if isinstance(key, str) else self.dsem[key]

    def _deps(self, reads, writes, eng=None):
        deps = {}

        def add(tok):
            if tok is None:
                return
            k, v = tok
            if deps.get(k, 0) < v:
                deps[k] = v

        for b in reads:
            add(b.w)
            if b.excl:
                for k, v in b.r.items():
                    if k != eng:
                        add((k, v))
        for b in writes:
            add(b.w)
            for k, v in b.r.items():
                add((k, v))
        return deps

    def _emit_waits(self, eng, deps):
        waits = []
        wd = self.waited[eng]
        for k, v in deps.items():
            if k == "pe" and eng == "pe":
                continue
            if wd.get(k, 0) >= v:
                continue
            wd[k] = v
            waits.append((self._sem(k), v))
        return waits

    def _commit(self, tok, reads, writes):
        k, v = tok
        for b in writes:
            b.w = tok
            b.r = {}
        for b in reads:
            if b.r.get(k, 0) < v:
                b.r[k] = v

    def op(self, eng, fn, reads=(), writes=()):
        deps = self._deps(reads, writes, eng)
        waits = self._emit_waits(eng, deps)
        self.cnt[eng] += 1
        tok = (eng, self.cnt[eng])
        sem = self.sems[eng]
        self.prog[eng].append((waits, _freeze(fn), sem, 1))
        self._commit(tok, reads, writes)
        self.n_ops += 1
        return tok

    def dma(self, eng, fn, reads=(), writes=()):
        deps = self._deps(reads, writes)
        k = self.drr
        self.drr = (self.drr + 1) % len(self.dsem)
        if self.dcnt[k] > 0:
            deps[k] = max(deps.get(k, 0), 16 * self.dcnt[k])
        waits = self._emit_waits(eng, deps)
        self.dcnt[k] += 1
        tok = (k, 16 * self.dcnt[k])
        self.prog[eng].append((waits, _freeze(fn), self.dsem[k], 16))
        self._commit(tok, reads, writes)
        self.n_ops += 1
        return tok

    def barrier(self):
        deps = {e: self.cnt[e] for e in self.COMPUTE if self.cnt[e] > 0}
        for k_, c_ in enumerate(self.dcnt):
            if c_ > 0:
                deps[k_] = 16 * c_
        for e in self.engs:
            waits = self._emit_waits(e, dict(deps))
            self.prog[e].append((waits, None, None, 0))

    def wait_all(self, eng, bufs):
        deps = self._deps(bufs, ())
        waits = self._emit_waits(eng, deps)
        self.prog[eng].append((waits, None, None, 0))

    def emit(self):
        nc = self.nc
        with nc.Block() as block:
            def mk(ename):
                def body(e):
                    for waits, fn, sem, inc in self.prog[ename]:
                        for s, v in waits:
                            e.wait_ge(s, v)
                        if fn is not None:
                            fn(e).then_inc(sem, inc)
                return body

            block.sync(mk("sp"))
            block.tensor(mk("pe"))
            block.scalar(mk("act"))
            block.vector(mk("dve"))
            block.gpsimd(mk("pool"))


class SbAlloc:
    def __init__(self, nc, base=16512, top=229312, prefix=""):
        self.prefix = prefix
        self.nc = nc
        self.off = base
        self.top = top
        self.n = 0
        self.peak = 0

    def mark(self):
        return self.off

    def release(self, mark):
        self.off = mark

    def tile(self, shape, dtype, name=None):
        esz = {F32: 4, F32R: 4, BF16: 2, I32: 4}[dtype]
        free = 1
        for s in shape[1:]:
            free *= s
        nbytes = (free * esz + 63) // 64 * 64
        off = self.off
        self.off += nbytes
        self.peak = max(self.peak, self.off)
        assert self.off <= self.top, "SBUF overflow: %d > %d (%s)" % (self.off, self.top, name)
        self.n += 1
        nm = "%s%s_%d" % (self.prefix, name or "t", self.n)
        return self.nc.alloc_sbuf_tensor_at(nm, list(shape), dtype, offset=off)


D = 1024
SEQ = 8192
NT = 16
NB = 64
H = 8
GH = 4
EPS = 1e-6
NOWN = 2052
P3T = 108
import os
STAGE = int(os.environ.get('K_STAGE', '6'))
DBG_NT = int(os.environ.get('K_NT', '16'))
DBG = os.environ.get('K_DBG', '')

C_FQ, C_FK, C_FV, C_FF = 0, 512, 1024, 1536
C_GQ, C_GK, C_GV, C_GLR, C_GG = 1544, 1800, 2056, 2568, 2584


def own_tiles(r):
    return [r, 7 - r, 8 + r, 15 - r]


class K:
    pass


def build_program():
    nc = bass.Bass("TRN2", target_bir_lowering=False)
    S = Sched(nc)
    sb = SbAlloc(nc)
    k = K()
    k.nc, k.S, k.sb = nc, S, sb

    k.used_inputs = []

    def din(name, shape, dt=F32):
        k.used_inputs.append(name)
        return nc.dram_tensor(name, list(shape), dt, kind="ExternalInput").ap()

    def dout(name, shape, dt=F32):
        return nc.dram_tensor(name, list(shape), dt, kind="ExternalOutput").ap()

    def dscr(name, shape, dt):
        return nc.dram_tensor(name, list(shape), dt, kind="Internal").ap()

    xT_full = din("xT_full", [D, SEQ])
    xT_own = din("xT_own", [D, NOWN])
    w1 = din("w1", [D, 2072])
    w2 = din("w2", [D, 1024])
    w_o = din("w_o", [D, D])
    w_up = din("w_up", [D, 4096])
    w_down = din("w_down", [4096, D])
    gains = din("gains", [128, 40])
    bfrep = din("bfrep", [128, 8])
    wg = din("wg", [16, 256])
    masks = din("masks", [4, 16, 128, 512], BF16)
    onehot = din("onehot", [4, 128, 64])
    idx_o = din("idx_o", [128, 16], I32)
    mbias = din("mbias", [128, 64])

    NPHYS = int(os.environ.get("K_NPHYS", "5120"))
    w_s = din("w_s", [D, 3096])
    ptab = din("ptab", [128, 4], I32)
    cache_k2 = din("cache_k2", [NPHYS * 32, 2048])
    cache_v2 = din("cache_v2", [NPHYS * 32, 2048])
    cache_lf = din("cache_lf", [NPHYS, 1024])
    sgla = din("sgla", [2, 128, 4, 128])
    o_ks = dout("o_ks", [4, 512]); o_vs = dout("o_vs", [4, 512]); o_lfs = dout("o_lfs", [4, 8])
    o_glas = dout("o_glas", [2, 128, 4, 128])
    o_y = dout("o_y", [D, NOWN])
    o_k = dout("o_k", [512, SEQ])
    o_v = dout("o_v", [SEQ, 512])
    o_lf = dout("o_lf", [128, NB * H])
    o_gla = dout("o_gla", [256, 128])

    kT_d = dscr("kT_d", [512, SEQ], BF16)
    v_d = dscr("v_d", [H, 128, NB, 65], BF16)
    og_d = dscr("og_d", [NT * GH * 128, 512], BF16)

    def cbuf(name):
        return Buf(name)

    ones_bf = sb.tile([128, 128], BF16, "ones_bf"); B_ones_bf = cbuf("ones_bf")
    ones_f = sb.tile([128, 128], F32, "ones_f"); B_ones_f = cbuf("ones_f")
    U_f = sb.tile([128, 128], F32, "U_f"); B_U = cbuf("U")
    tri_bf = sb.tile([128, 128], BF16, "tri_bf"); B_tri = cbuf("tri")
    gn = sb.tile([128, 40], F32, "gains"); B_gn = cbuf("gains")
    bfr = sb.tile([128, 8], F32, "bfr"); B_bfr = cbuf("bfr")
    wg_f = sb.tile([16, 256], F32, "wg_f"); wg_bf = sb.tile([16, 256], BF16, "wg_bf"); B_wg = cbuf("wg")
    lf_tab = sb.tile([128, H, NB], F32, "lf_tab"); B_lf = cbuf("lf")
    C_tab = sb.tile([128, H, NB], F32, "C_tab"); B_C = cbuf("C")

    S.op("pool", lambda e: e.memset(ones_bf[:], 1.0), writes=[B_ones_bf])
    S.op("pool", lambda e: e.memset(ones_f[:], 1.0), writes=[B_ones_f])
    S.op("pool", lambda e: e.memset(U_f[:], 1.0), writes=[B_U])
    S.op("pool", lambda e: e.affine_select(out=U_f[:], in_=U_f[:], pattern=[[1, 128]], compare_op=ALU.is_ge,
                                           fill=0.0, base=0, channel_multiplier=-1), writes=[B_U])
    S.op("pool", lambda e: e.tensor_copy(out=tri_bf[:], in_=U_f[:]), reads=[B_U], writes=[B_tri])
    S.dma("sp", lambda e: e.dma_start(out=gn[:], in_=gains), writes=[B_gn])
    S.dma("sp", lambda e: e.dma_start(out=bfr[:], in_=bfrep), writes=[B_bfr])
    S.dma("sp", lambda e: e.dma_start(out=wg_f[:], in_=wg), writes=[B_wg])
    S.op("dve", lambda e: e.tensor_copy(out=wg_bf[:], in_=wg_f[:]), reads=[B_wg], writes=[B_wg])

    ps = [nc.alloc_psum_tensor("ps%d" % i, [128, 512], F32) for i in range(7)]
    Bps = [Buf("ps%d" % i, excl=True) for i in range(7)]
    psT_bf = nc.alloc_psum_tensor("psT_bf", [128, 1024], BF16); B_psT = Buf("psT", excl=True)
    ident_bf = sb.tile([128, 128], BF16, "ident_bf"); B_ident = Buf("ident")
    ident_f = sb.tile([128, 128], F32, "ident_f")
    S.op("pool", lambda e: e.affine_select(out=ident_f[:], in_=U_f[:], pattern=[[-1, 128]], compare_op=ALU.is_ge,
                                           fill=0.0, base=0, channel_multiplier=1), reads=[B_U], writes=[B_ident])
    S.op("pool", lambda e: e.tensor_copy(out=ident_bf[:], in_=ident_f[:]), reads=[B_ident], writes=[B_ident])

    out_bufs = []

    def rms_rstd(xt, B_xt, n, sq, B_sq, rstd, B_rstd, pbank, nfeat=8, dim=1024.0):
        S.op("act", lambda e: e.activation(out=sq[:, 0:nfeat, 0:n], in_=xt[:, 0:nfeat, 0:n], func=AF.Square),
             reads=[B_xt], writes=[B_sq])
        for kc in range(nfeat):
            S.op("pe", lambda e, kc=kc: e.matmul(ps[pbank][:, 0:n], lhsT=ones_bf[:], rhs=sq[:, kc, 0:n],
                                                  start=(kc == 0), stop=(kc == nfeat - 1)),
                 reads=[B_sq, B_ones_bf], writes=[Bps[pbank]])
        S.op("act", lambda e: e.activation(out=rstd[:, 0:n], in_=ps[pbank][:, 0:n], func=AF.Ln, scale=1.0 / dim, bias=EPS),
             reads=[Bps[pbank]], writes=[B_rstd])
        S.op("act", lambda e: e.activation(out=rstd[:, 0:n], in_=rstd[:, 0:n], func=AF.Exp, scale=-0.5),
             reads=[B_rstd], writes=[B_rstd])

    m_p1 = sb.mark()
    rmask = sb.tile([128, 512], F32, "rmask"); B_rmask = cbuf("rmask")
    S.op("pool", lambda e: e.memset(rmask[:], 1.0), writes=[B_rmask])
    S.op("pool", lambda e: e.memset(rmask[:].rearrange("p (c t) -> p c t", t=128)[:, :, 0:1], 0.0), writes=[B_rmask])
    w1_bf = sb.tile([128, 8, 2072], BF16, "w1_bf"); B_w1 = Buf("w1")
    wst = [sb.tile([128, 2072], F32, "wst%d" % i) for i in range(2)]; B_wst = [Buf("wst0"), Buf("wst1")]
    for kc in range(8):
        i = kc % 2
        S.dma("sp", lambda e, kc=kc, i=i: e.dma_start(out=wst[i][:], in_=w1[kc * 128:(kc + 1) * 128, :]), writes=[B_wst[i]])
        eng = "dve" if kc % 2 == 0 else "pool"
        S.op(eng, lambda e, kc=kc, i=i: e.tensor_copy(out=w1_bf[:, kc, :], in_=wst[i][:]), reads=[B_wst[i]], writes=[B_w1])
    W_FK, W_GQ, W_GK, W_GLR, W_FV, W_GV, W_FF = 0, 512, 768, 1024, 1040, 1552, 2064

    xt = [sb.tile([128, 8, 512], F32, "xt%d" % i) for i in range(2)]; B_xt = [Buf("xt0"), Buf("xt1")]
    sq = sb.tile([128, 8, 512], BF16, "sq"); B_sq = Buf("sq")
    xn = sb.tile([128, 8, 512], BF16, "xn"); B_xn = Buf("xn")
    rstd = sb.tile([128, 512], F32, "rstd"); B_rstd = Buf("rstd")
    kst = [sb.tile([128, 512], F32, "kst%d" % i) for i in range(2)]; B_kst = [Buf("kst0"), Buf("kst1")]
    kbf = [sb.tile([128, 512], F32 if "F" in DBG else BF16, "kbf%d" % i) for i in range(2)]; B_kbf = [Buf("kbf0"), Buf("kbf1")]
    vst = [sb.tile([128, 512], F32, "vst%d" % i) for i in range(2)]; B_vst = [Buf("vst0"), Buf("vst1")]
    vaug = [sb.tile([128, H, 4, 65], BF16, "vaug%d" % i) for i in range(2)]; B_vaug = [Buf("vaug0"), Buf("vaug1")]
    for i in range(2):
        S.op("pool", lambda e, i=i: e.memset(vaug[i][:], 1.0), writes=[B_vaug[i]])
    zff = sb.tile([128, 8], F32, "zff"); B_zff = Buf("zff")
    gq_f = [sb.tile([128, 512], F32, "gq%d" % i) for i in range(2)]; B_gq = [Buf("gq0"), Buf("gq1")]
    gk_f = [sb.tile([128, 512], F32, "gk%d" % i) for i in range(2)]; B_gk = [Buf("gk0"), Buf("gk1")]
    glr_bf = sb.tile([16, 512], BF16, "glr"); B_glr = Buf("glr")
    Lb = [sb.tile([128, 512], F32, "L%d" % i) for i in range(2)]; B_L = [Buf("L0"), Buf("L1")]
    Bc = [sb.tile([128, 512], F32, "Bc%d" % i) for i in range(2)]; B_Bc = [Buf("Bc0"), Buf("Bc1")]
    dd = [sb.tile([128, 512], F32, "dd%d" % i) for i in range(2)]; B_dd = [Buf("dd0"), Buf("dd1")]
    ee = [sb.tile([128, 512], F32, "ee%d" % i) for i in range(2)]; B_ee = [Buf("ee0"), Buf("ee1")]
    e3 = [sb.tile([128, 512], F32, "e3%d" % i) for i in range(2)]; B_e3 = [Buf("e30"), Buf("e31")]
    qt_ = [sb.tile([128, 512], BF16, "qt%d" % i) for i in range(2)]; B_qt = [Buf("qt0"), Buf("qt1")]
    kt_ = [sb.tile([128, 512], BF16, "kt%d" % i) for i in range(2)]; B_kt = [Buf("kt0"), Buf("kt1")]
    qh_ = [sb.tile([128, 512], BF16, "qh%d" % i) for i in range(2)]; B_qh = [Buf("qh0"), Buf("qh1")]
    kh_ = [sb.tile([128, 512], BF16, "kh%d" % i) for i in range(2)]; B_kh = [Buf("kh0"), Buf("kh1")]
    gvb = [sb.tile([128, 512], BF16, "gvb%d" % i) for i in range(4)]; B_gvb = [Buf("gvb%d" % i) for i in range(4)]
    khT = [sb.tile([128, 128], BF16, "khT%d" % i) for i in range(2)]; B_khT = [Buf("khT0"), Buf("khT1")]
    AT = [sb.tile([128, 128], BF16, "AT%d" % i) for i in range(2)]; B_AT = [Buf("AT0"), Buf("AT1")]
    ost = [sb.tile([128, 512], BF16, "ost%d" % i) for i in range(4)]; B_ost = [Buf("ost%d" % i) for i in range(4)]
    Sst = [sb.tile([128, 128], F32, "Sst%d" % i) for i in range(2)]; B_Sst = [Buf("Sst0"), Buf("Sst1")]
    Sbf = [sb.tile([128, 128], BF16, "Sbf%d" % i) for i in range(2)]; B_Sbf = [Buf("Sbf0"), Buf("Sbf1")]
    Sbf2 = [[Sbf[i], sb.tile([128, 128], BF16, "Sbfb%d" % i)] for i in range(2)]
    B_Sbf2 = [[B_Sbf[i], Buf("Sbfb%d" % i)] for i in range(2)]
    for i in range(2):
        S.op("pool", lambda e, i=i: e.memset(Sst[i][:], 0.0), writes=[B_Sst[i]])
        for j in range(2):
            S.op("pool", lambda e: e.memset(Sbf2[i][j][:], 0.0), writes=[B_Sbf2[i][j]])

    B_kTd = Buf("kT_d"); B_vd = Buf("v_d"); B_ogd = Buf("og_d")
    B_ok = Buf("o_k"); B_ov = Buf("o_v"); B_olf = Buf("o_lf"); B_ogla = Buf("o_gla"); B_oy = Buf("o_y")
    out_bufs += [B_ok, B_ov, B_olf, B_ogla, B_oy]
    xT_full_v = xT_full.rearrange("(kc p) t -> p kc t", p=128)

    DO_GLA = STAGE >= 2

    def load_x(ti):
        i = ti % 2
        S.dma("sp", lambda e: e.dma_start(out=xt[i][:], in_=xT_full_v[:, :, ti * 512:(ti + 1) * 512]), writes=[B_xt[i]])

    load_x(0)
    for ti in range(DBG_NT):
        i = ti % 2
        if ti + 1 < DBG_NT:
            load_x(ti + 1)
        t0 = ti * 512
        if 'r' not in DBG:
            rms_rstd(xt[i], B_xt[i], 512, sq, B_sq, rstd, B_rstd, 0)
        else:
            S.op('pool', lambda e: e.memset(rstd[:], 1.0), writes=[B_rstd])
        for kc in range(0 if 'n' in DBG else 8):
            S.op("dve", lambda e, kc=kc: e.scalar_tensor_tensor(out=xn[:, kc, :], in0=xt[i][:, kc, :], scalar=gn[:, kc:kc + 1],
                                                                in1=rstd[:], op0=ALU.mult, op1=ALU.mult),
                 reads=[B_xt[i], B_gn, B_rstd], writes=[B_xn])
        for p in range(0 if 'a' in DBG else 4):
            pb = 1 + (p % 2)
            for kc in range(8):
                S.op("pe", lambda e, kc=kc, p=p, pb=pb: e.matmul(ps[pb][:, :], lhsT=w1_bf[:, kc, W_FK + p * 128:W_FK + (p + 1) * 128],
                                                                  rhs=xn[:, kc, :], start=(kc == 0), stop=(kc == 7)),
                     reads=[B_w1, B_xn], writes=[Bps[pb]])
            j = p % 2
            if 'e' in DBG:
                continue
            S.op("act", lambda e, pb=pb, j=j: e.copy(out=kst[j][:], in_=ps[pb][:, :]), reads=[Bps[pb]], writes=[B_kst[j]])
            if 'f' not in DBG:
                S.op("dve", lambda e, pb=pb, j=j: e.tensor_copy(out=kbf[j][:], in_=ps[pb][:, :]), reads=[Bps[pb]], writes=[B_kbf[j]])
            if 'g' not in DBG:
                S.dma("sp", lambda e, p=p, j=j: e.dma_start(out=o_k[p * 128:(p + 1) * 128, t0:t0 + 512], in_=kst[j][:]),
                      reads=[B_kst[j]], writes=[B_ok])
            if 'k' not in DBG:
                S.dma("sp", lambda e, p=p, j=j: e.dma_start(out=kT_d[p * 128:(p + 1) * 128, t0:t0 + 512], in_=kbf[j][:]),
                      reads=[B_kbf[j]], writes=[B_kTd])
        vi = ti % 2
        for blk in range(0 if 'b' in DBG else 4):
            pb = 3 + (blk % 2)
            for kc in range(8):
                S.op("pe", lambda e, kc=kc, blk=blk, pb=pb: e.matmul(ps[pb][:, :], lhsT=xn[:, kc, blk * 128:(blk + 1) * 128],
                                                                      rhs=w1_bf[:, kc, W_FV:W_FV + 512], start=(kc == 0), stop=(kc == 7)),
                     reads=[B_w1, B_xn], writes=[Bps[pb]])
            j = blk % 2
            S.op("act", lambda e, pb=pb, j=j: e.copy(out=vst[j][:], in_=ps[pb][:, :]), reads=[Bps[pb]], writes=[B_vst[j]])
            S.op("dve", lambda e, pb=pb, blk=blk: e.tensor_copy(out=vaug[vi][:, :, blk, 0:64],
                                                                in_=ps[pb][:, :].rearrange("p (h d) -> p h d", d=64)),
                 reads=[Bps[pb]], writes=[B_vaug[vi]])
            S.dma("sp", lambda e, blk=blk, j=j: e.dma_start(out=o_v[t0 + blk * 128:t0 + (blk + 1) * 128, :], in_=vst[j][:]),
                  reads=[B_vst[j]], writes=[B_ov])
            for kc in range(8):
                S.op("pe", lambda e, kc=kc, blk=blk: e.matmul(ps[5][:, 0:8], lhsT=xn[:, kc, blk * 128:(blk + 1) * 128],
                                                              rhs=w1_bf[:, kc, W_FF:W_FF + 8], start=(kc == 0), stop=(kc == 7)),
                     reads=[B_w1, B_xn], writes=[Bps[5]])
            nb = ti * 4 + blk
            S.op("dve", lambda e: e.tensor_tensor(out=zff[:], in0=ps[5][:, 0:8], in1=bfr[:], op=ALU.add),
                 reads=[Bps[5], B_bfr], writes=[B_zff])
            S.op("act", lambda e: e.activation(out=zff[:], in_=zff[:], func=AF.Exp, scale=-1.0), reads=[B_zff], writes=[B_zff])
            S.op("act", lambda e: e.activation(out=zff[:], in_=zff[:], func=AF.Ln, bias=1.0), reads=[B_zff], writes=[B_zff])
            S.op("dve", lambda e, nb=nb: e.tensor_scalar(out=lf_tab[:, :, nb], in0=zff[:], scalar1=-1.0, scalar2=None, op0=ALU.mult),
                 reads=[B_zff], writes=[B_lf])
        for h in range(0 if 'c' in DBG else H):
            S.dma("sp", lambda e, h=h: e.dma_start(out=v_d[h, :, ti * 4:(ti + 1) * 4, :], in_=vaug[vi][:, h, :, :]),
                  reads=[B_vaug[vi]], writes=[B_vd])
        if not DO_GLA:
            continue
        for kc in range(8):
            S.op("pe", lambda e, kc=kc: e.matmul(ps[5][0:16, :], lhsT=w1_bf[:, kc, W_GLR:W_GLR + 16], rhs=xn[:, kc, :],
                                                  start=(kc == 0), stop=(kc == 7)), reads=[B_w1, B_xn], writes=[Bps[5]])
        S.op("act", lambda e: e.copy(out=glr_bf[:], in_=ps[5][0:16, :]), reads=[Bps[5]], writes=[B_glr])
        for p in range(2):
            for which, (wofs, dst, Bd) in enumerate([(W_GQ, gq_f, B_gq), (W_GK, gk_f, B_gk)]):
                pb = 1 + which
                for kc in range(8):
                    S.op("pe", lambda e, kc=kc, p=p, wofs=wofs, pb=pb: e.matmul(ps[pb][:, :], lhsT=w1_bf[:, kc, wofs + p * 128:wofs + (p + 1) * 128],
                                                                                 rhs=xn[:, kc, :], start=(kc == 0), stop=(kc == 7)),
                         reads=[B_w1, B_xn], writes=[Bps[pb]])
                S.op("act", lambda e, pb=pb, dst=dst, p=p: e.copy(out=dst[p][:], in_=ps[pb][:, :]), reads=[Bps[pb]], writes=[Bd[p]])
            S.op("pe", lambda e, p=p: e.matmul(ps[5][:, :], lhsT=wg_bf[:, p * 128:(p + 1) * 128], rhs=glr_bf[:], start=True, stop=True),
                 reads=[B_wg, B_glr], writes=[Bps[5]])
            S.op("act", lambda e, p=p: e.activation(out=Lb[p][:], in_=ps[5][:, :], func=AF.Exp, scale=-1.0, bias=gn[:, 25 + p:26 + p]),
                 reads=[Bps[5], B_gn], writes=[B_L[p]])
            S.op("act", lambda e, p=p: e.activation(out=Lb[p][:], in_=Lb[p][:], func=AF.Ln, bias=1.0), reads=[B_L[p]], writes=[B_L[p]])
            S.op("dve", lambda e, p=p: e.tensor_tensor_scan(out=Bc[p][:], data0=rmask[:], data1=Lb[p][:], initial=0.0,
                                                            op0=ALU.mult, op1=ALU.add), reads=[B_rmask, B_L[p]], writes=[B_Bc[p]])
            Bc3 = Bc[p][:].rearrange("p (c t) -> p c t", t=128)
            dd3 = dd[p][:].rearrange("p (c t) -> p c t", t=128)
            S.op("dve", lambda e, Bc3=Bc3, dd3=dd3: e.tensor_tensor(out=dd3, in0=Bc3, in1=Bc3[:, :, 64:65].to_broadcast([128, 4, 128]),
                                                                    op=ALU.subtract), reads=[B_Bc[p]], writes=[B_dd[p]])
            S.op("act", lambda e, p=p: e.activation(out=ee[p][:], in_=dd[p][:], func=AF.Exp, scale=-1.0 / 16), reads=[B_dd[p]], writes=[B_ee[p]])
            S.op("dve", lambda e, p=p: e.scalar_tensor_tensor(out=qt_[p][:], in0=gq_f[p][:], scalar=0.125, in1=ee[p][:], op0=ALU.mult, op1=ALU.mult),
                 reads=[B_gq[p], B_ee[p]], writes=[B_qt[p]])
            S.op("act", lambda e, p=p: e.activation(out=ee[p][:], in_=dd[p][:], func=AF.Exp, scale=1.0 / 16), reads=[B_dd[p]], writes=[B_ee[p]])
            S.op("dve", lambda e, p=p: e.tensor_tensor(out=kt_[p][:], in0=gk_f[p][:], in1=ee[p][:], op=ALU.mult),
                 reads=[B_gk[p], B_ee[p]], writes=[B_kt[p]])
            S.op("act", lambda e, p=p: e.activation(out=e3[p][:], in_=Bc[p][:], func=AF.Exp, scale=-1.0 / 16), reads=[B_Bc[p]], writes=[B_e3[p]])
            S.op("dve", lambda e, p=p: e.scalar_tensor_tensor(out=qh_[p][:], in0=gq_f[p][:], scalar=0.125, in1=e3[p][:], op0=ALU.mult, op1=ALU.mult),
                 reads=[B_gq[p], B_e3[p]], writes=[B_qh[p]])
            S.op("dve", lambda e, Bc3=Bc3, dd3=dd3: e.tensor_tensor(out=dd3, in0=Bc3, in1=Bc3[:, :, 127:128].to_broadcast([128, 4, 128]),
                                                                    op=ALU.subtract), reads=[B_Bc[p]], writes=[B_dd[p]])
            S.op("act", lambda e, p=p: e.activation(out=ee[p][:], in_=dd[p][:], func=AF.Exp, scale=1.0 / 16), reads=[B_dd[p]], writes=[B_ee[p]])
            S.op("dve", lambda e, p=p: e.tensor_tensor(out=kh_[p][:], in0=gk_f[p][:], in1=ee[p][:], op=ALU.mult),
                 reads=[B_gk[p], B_ee[p]], writes=[B_kh[p]])
        for blk in range(4):
            pb = 3 + (blk % 2)
            for kc in range(8):
                S.op("pe", lambda e, kc=kc, blk=blk, pb=pb: e.matmul(ps[pb][:, :], lhsT=xn[:, kc, blk * 128:(blk + 1) * 128],
                                                                      rhs=w1_bf[:, kc, W_GV:W_GV + 512], start=(kc == 0), stop=(kc == 7)),
                     reads=[B_w1, B_xn], writes=[Bps[pb]])
            S.op("act", lambda e, pb=pb, blk=blk: e.copy(out=gvb[blk][:], in_=ps[pb][:, :]), reads=[Bps[pb]], writes=[B_gvb[blk]])
        ABANK = [6, 0]
        OBK = [2, 1]
        for c in range(4):
            cs = slice(c * 128, (c + 1) * 128)
            for p in range(2):
                S.op("pe", lambda e: e.transpose(out=psT_bf[:, 0:128], in_=kh_[p][:, cs], identity=ident_bf[:]),
                     reads=[B_kh[p], B_ident], writes=[B_psT])
                S.op("act", lambda e: e.copy(out=khT[p][:], in_=psT_bf[:, 0:128]), reads=[B_psT], writes=[B_khT[p]])
                for hh in range(2):
                    h = p * 2 + hh
                    S.op("pe", lambda e: e.matmul(ps[5][hh * 64:(hh + 1) * 64, 0:128], lhsT=khT[p][:, hh * 64:(hh + 1) * 64],
                                                  rhs=gvb[c][:, h * 128:(h + 1) * 128], start=True, stop=True),
                         reads=[B_khT[p], B_gvb[c]], writes=[Bps[5]])
                S.op("dve", lambda e: e.scalar_tensor_tensor(out=Sst[p][:], in0=Sst[p][:], scalar=e3[p][:, c * 128 + 127:c * 128 + 128],
                                                             in1=ps[5][:, 0:128], op0=ALU.mult, op1=ALU.add),
                     reads=[B_Sst[p], B_e3[p], Bps[5]], writes=[B_Sst[p]])
                S.op("act", lambda e: e.copy(out=Sbf2[p][c % 2][:], in_=Sst[p][:]), reads=[B_Sst[p]], writes=[B_Sbf2[p][c % 2]])
            for p in range(2):
                for hh in range(2):
                    h = p * 2 + hh
                    rs = slice(hh * 64, (hh + 1) * 64)
                    ab, ob2 = ABANK[hh], OBK[hh]
                    S.op("pe", lambda e: e.matmul(ps[ab][:, 0:128], lhsT=kt_[p][rs, cs], rhs=qt_[p][rs, cs], start=True, stop=True),
                         reads=[B_kt[p], B_qt[p]], writes=[Bps[ab]])
                    S.op("dve", lambda e: e.tensor_tensor(out=AT[hh][:], in0=ps[ab][:, 0:128], in1=tri_bf[:], op=ALU.mult),
                         reads=[Bps[ab], B_tri], writes=[B_AT[hh]])
                    S.op("pe", lambda e: e.matmul(ps[ob2][:, 0:128], lhsT=gvb[c][:, h * 128:(h + 1) * 128], rhs=AT[hh][:], start=True, stop=False),
                         reads=[B_gvb[c], B_AT[hh]], writes=[Bps[ob2]])
                    S.op("pe", lambda e: e.matmul(ps[ob2][:, 0:128], lhsT=Sbf2[p][(c - 1) % 2][rs, :], rhs=qh_[p][rs, cs], start=False, stop=True),
                         reads=[B_Sbf2[p][(c - 1) % 2], B_qh[p]], writes=[Bps[ob2]])
                    S.op("act", lambda e: e.copy(out=ost[h][:, cs], in_=ps[ob2][:, 0:128]), reads=[Bps[ob2]], writes=[B_ost[h]])
        for h in range(GH):
            S.dma("sp", lambda e, h=h: e.dma_start(out=og_d[(ti * GH + h) * 128:(ti * GH + h + 1) * 128, :], in_=ost[h][:]),
                  reads=[B_ost[h]], writes=[B_ogd])

    if DO_GLA:
        for p in range(2):
            S.dma("sp", lambda e, p=p: e.dma_start(out=o_gla[p * 128:(p + 1) * 128, :], in_=Sst[p][:]), reads=[B_Sst[p]], writes=[B_ogla])
    S.dma("sp", lambda e: e.dma_start(out=o_lf, in_=lf_tab[:].rearrange("p h n -> p (h n)")), reads=[B_lf], writes=[B_olf])


    if STAGE >= 3:
        S.barrier()
        sb.release(m_p1)
        mix = sb.tile([128, 8, NOWN], BF16, "mix"); B_mix = Buf("mix")
        S.op("pool", lambda e: e.memset(mix[:], 0.0), writes=[B_mix])
        m_p2 = sb.mark()
        xT_own_v = xT_own.rearrange("(kc p) t -> p kc t", p=128)
        if STAGE >= 6:
            B_oks, B_ovs, B_olfs, B_oglas = Buf("o_ks"), Buf("o_vs"), Buf("o_lfs"), Buf("o_glas")
            out_bufs += [B_oks, B_ovs, B_olfs, B_oglas]
            zsT = sb.tile([128, 25, 4], F32, "zsT"); B_zsT = Buf("zsT")
            xs = sb.tile([128, 8, 4], F32, "xs"); B_xs = Buf("xs")
            sqs = sb.tile([128, 8, 4], BF16, "sqs"); B_sqs = Buf("sqs")
            rss = sb.tile([128, 4], F32, "rss"); B_rss = Buf("rss")
            xns = sb.tile([128, 8, 4], F32, "xns"); B_xns = Buf("xns")
            fq_tm = sb.tile([4, 512], F32, "fq_tm"); B_fq_tm = Buf("fq_tm")
            fk_tm = sb.tile([4, 512], F32, "fk_tm"); B_fk_tm = Buf("fk_tm")
            fv_tm = sb.tile([4, 512], F32, "fv_tm"); B_fv_tm = Buf("fv_tm")
            gv_tm = sb.tile([4, 512], F32, "gv_tm"); B_gv_tm = Buf("gv_tm")
            lfs_tm = sb.tile([4, 8], F32, "lfs_tm"); B_lfs = Buf("lfs_tm")
            sel4 = sb.tile([4, 4, 128], F32, "sel4"); B_sel4 = Buf("sel4")
            bm = sb.tile([8, 8, 64], F32, "bm"); B_bm = Buf("bm")
            SL_f = sb.tile([128, 128], F32, "SL_f"); B_SL = Buf("SL")
            m128 = sb.tile([128, 1024], F32, "m128"); B_m128 = Buf("m128")
            pt_t = sb.tile([128, 4], I32, "pt_t"); B_pt = Buf("pt")
            Otm = sb.tile([4, 512], F32, "Otm"); B_Otm = Buf("Otm")
            dentm = sb.tile([4, 8], F32, "dentm"); B_dentm = Buf("dentm")
            S.op("dve", lambda e: e.tensor_copy(out=sel4[:], in_=ident_f[0:4, 0:4].unsqueeze(2).to_broadcast([4, 4, 128])), reads=[B_ident], writes=[B_sel4])
            S.op("dve", lambda e: e.tensor_copy(out=bm[:], in_=ident_f[0:8, 0:8].unsqueeze(2).to_broadcast([8, 8, 64])), reads=[B_ident], writes=[B_bm])
            S.op("dve", lambda e: e.tensor_tensor(out=SL_f[:], in0=ones_f[:], in1=U_f[:], op=ALU.subtract), reads=[B_ones_f, B_U], writes=[B_SL])
            S.op("pool", lambda e: e.memset(m128[:], 1.0), writes=[B_m128])
            S.op("pool", lambda e: e.memset(m128[:].rearrange("p (h i) -> p h i", i=128)[:, :, 0:1], 0.0), writes=[B_m128])
            S.dma("sp", lambda e: e.dma_start(out=pt_t[:], in_=ptab), writes=[B_pt])
            TOPBASE = 229312 - 100 * 1024
            sbt = SbAlloc(nc, base=TOPBASE, top=229312, prefix="T")
            win_f = sbt.tile([128, 8, 3096], F32, "win_f"); B_win = Buf("win_f")
            for kc in range(8):
                S.dma("sp", lambda e: e.dma_start(out=win_f[:, kc, :], in_=w_s[kc * 128:(kc + 1) * 128, :]), writes=[B_win])
            S.dma("sp", lambda e: e.dma_start(out=xs[:], in_=xT_own_v[:, :, 2048:2052]), writes=[B_xs])
            rms_rstd(xs, B_xs, 4, sqs, B_sqs, rss, B_rss, 0)
            for kc in range(8):
                S.op("dve", lambda e: e.scalar_tensor_tensor(out=xns[:, kc, :], in0=xs[:, kc, :], scalar=gn[:, kc:kc + 1], in1=rss[:],
                                                             op0=ALU.mult, op1=ALU.mult), reads=[B_xs, B_gn, B_rss], writes=[B_xns])
            for mc in range(25):
                mcols = 128 if mc < 24 else 24
                for kc in range(8):
                    S.op("pe", lambda e: e.matmul(ps[0][0:mcols, mc * 4:(mc + 1) * 4], lhsT=win_f[:, kc, mc * 128:mc * 128 + mcols], rhs=xns[:, kc, :],
                                                  start=(kc == 0), stop=(kc == 7)), reads=[B_win, B_xns], writes=[Bps[0]])
            S.op("act", lambda e: e.copy(out=zsT[:, 0:24, :], in_=ps[0][:, 0:96].rearrange("p (m t) -> p m t", t=4)), reads=[Bps[0]], writes=[B_zsT])
            S.op("act", lambda e: e.copy(out=zsT[0:24, 24, :], in_=ps[0][0:24, 96:100]), reads=[Bps[0]], writes=[B_zsT])
            def proj_tm(c0, n, dst, Bd, pb):
                for kc in range(8):
                    S.op("pe", lambda e: e.matmul(ps[pb][0:4, 0:n], lhsT=xns[:, kc, :], rhs=win_f[:, kc, c0:c0 + n], start=(kc == 0), stop=(kc == 7)),
                         reads=[B_win, B_xns], writes=[Bps[pb]])
                if dst is not None:
                    S.op("act", lambda e: e.copy(out=dst[:, 0:n], in_=ps[pb][0:4, 0:n]), reads=[Bps[pb]], writes=[Bd])
            proj_tm(0, 512, fq_tm, B_fq_tm, 1)
            proj_tm(512, 512, fk_tm, B_fk_tm, 2)
            proj_tm(1024, 512, fv_tm, B_fv_tm, 1)
            proj_tm(2048, 512, gv_tm, B_gv_tm, 2)
            proj_tm(3088, 8, None, None, 1)
            S.op("dve", lambda e: e.tensor_tensor(out=lfs_tm[:], in0=ps[1][0:4, 0:8], in1=bfr[0:4, :], op=ALU.add), reads=[Bps[1], B_bfr], writes=[B_lfs])
            S.op("act", lambda e: e.activation(out=lfs_tm[:], in_=lfs_tm[:], func=AF.Exp, scale=-1.0), reads=[B_lfs], writes=[B_lfs])
            S.op("act", lambda e: e.activation(out=lfs_tm[:], in_=lfs_tm[:], func=AF.Ln, bias=1.0), reads=[B_lfs], writes=[B_lfs])
            S.op("dve", lambda e: e.tensor_scalar(out=lfs_tm[:], in0=lfs_tm[:], scalar1=-1.0, scalar2=None, op0=ALU.mult), reads=[B_lfs], writes=[B_lfs])
            S.dma("sp", lambda e: e.dma_start(out=o_ks, in_=fk_tm[:]), reads=[B_fk_tm], writes=[B_oks])
            S.dma("sp", lambda e: e.dma_start(out=o_vs, in_=fv_tm[:]), reads=[B_fv_tm], writes=[B_ovs])
            S.dma("sp", lambda e: e.dma_start(out=o_lfs, in_=lfs_tm[:]), reads=[B_lfs], writes=[B_olfs])
            Sin = sb.tile([128, 4, 128], F32, "Sin"); B_Sin = Buf("Sin")
            Snew = sb.tile([128, 4, 128], F32, "Snew"); B_Snew = Buf("Snew")
            alT = sb.tile([128, 4], F32, "alT"); B_alT = Buf("alT")
            kvt = sb.tile([128, 128], F32, "kvt"); B_kvt = Buf("kvt")
            ogs = sb.tile([128, 1, 16], F32, "ogs"); B_ogs = Buf("ogs")
            for p in range(2):
                S.dma("sp", lambda e: e.dma_start(out=Sin[:], in_=sgla[p]), writes=[B_Sin])
                S.op("pe", lambda e: e.matmul(ps[5][:, 0:4], lhsT=wg_f[:, p * 128:(p + 1) * 128],
                                              rhs=zsT[0:16, 24, :], start=True, stop=True), reads=[B_wg, B_zsT], writes=[Bps[5]])
                S.op("act", lambda e: e.activation(out=alT[:], in_=ps[5][:, 0:4], func=AF.Exp, scale=-1.0, bias=gn[:, 25 + p:26 + p]),
                     reads=[Bps[5], B_gn], writes=[B_alT])
                S.op("act", lambda e: e.activation(out=alT[:], in_=alT[:], func=AF.Ln, bias=1.0), reads=[B_alT], writes=[B_alT])
                S.op("act", lambda e: e.activation(out=alT[:], in_=alT[:], func=AF.Exp, scale=-1.0 / 16), reads=[B_alT], writes=[B_alT])
                for bb in range(4):
                    for hh in range(2):
                        h = 2 * p + hh
                        S.op("pe", lambda e: e.matmul(ps[4][hh * 64:(hh + 1) * 64, 0:128], lhsT=sel4[:, bb, 0:64], rhs=gv_tm[:, h * 128:(h + 1) * 128],
                                                      start=True, stop=True), reads=[B_sel4, B_gv_tm], writes=[Bps[4]])
                    S.op("dve", lambda e: e.tensor_scalar(out=kvt[:], in0=ps[4][:, 0:128], scalar1=zsT[:, 14 + p, bb:bb + 1], scalar2=None, op0=ALU.mult),
                         reads=[Bps[4], B_zsT], writes=[B_kvt])
                    S.op("dve", lambda e: e.scalar_tensor_tensor(out=Snew[:, bb, :], in0=Sin[:, bb, :], scalar=alT[:, bb:bb + 1], in1=kvt[:],
                                                                 op0=ALU.mult, op1=ALU.add), reads=[B_Sin, B_alT, B_kvt], writes=[B_Snew])
                    for hh in range(2):
                        h = 2 * p + hh
                        rs = slice(hh * 64, (hh + 1) * 64)
                        S.op("pe", lambda e: e.matmul(ps[6][:, h * 4 + bb:h * 4 + bb + 1], lhsT=Snew[rs, bb, :], rhs=zsT[rs, 12 + p, bb:bb + 1],
                                                      start=True, stop=True), reads=[B_Snew, B_zsT], writes=[Bps[6]])
                S.dma("sp", lambda e: e.dma_start(out=o_glas[p], in_=Snew[:]), reads=[B_Snew], writes=[B_oglas])
            S.op("act", lambda e: e.mul(out=ogs[:, 0, :], in_=ps[6][:, 0:16], mul=0.125), reads=[Bps[6]], writes=[B_ogs])
            ogq = sb.tile([128, 1, 16], BF16, "ogq"); B_ogq = Buf("ogq")
            rso = sb.tile([128, 16], F32, "rso"); B_rso = Buf("rso")
            sgs = sb.tile([128, 4, 4], F32, "sgs"); B_sgs = Buf("sgs")
            t1s = sb.tile([128, 4, 4], F32, "t1s"); B_t1s = Buf("t1s")
            rms_rstd(ogs, B_ogs, 16, ogq, B_ogq, rso, B_rso, 5, nfeat=1, dim=128.0)
            S.op("act", lambda e: e.activation(out=sgs[:], in_=zsT[:, 20:24, :], func=AF.Exp, scale=-1.0), reads=[B_zsT], writes=[B_sgs])
            S.op("dve", lambda e: e.tensor_scalar(out=sgs[:], in0=sgs[:], scalar1=1.0, scalar2=None, op0=ALU.add), reads=[B_sgs], writes=[B_sgs])
            S.op("dve", lambda e: e.reciprocal(out=sgs[:], in_=sgs[:]), reads=[B_sgs], writes=[B_sgs])
            S.op("dve", lambda e: e.tensor_tensor(out=sgs[:], in0=zsT[:, 20:24, :], in1=sgs[:], op=ALU.mult), reads=[B_zsT, B_sgs], writes=[B_sgs])
            S.op("dve", lambda e: e.scalar_tensor_tensor(out=t1s[:].rearrange("p h b -> p (h b)"), in0=ogs[:, 0, :], scalar=gn[:, 24:25], in1=rso[:],
                                                         op0=ALU.mult, op1=ALU.mult), reads=[B_ogs, B_gn, B_rso], writes=[B_t1s])
            S.op("dve", lambda e: e.tensor_tensor(out=mix[:, 4:8, 2048:2052], in0=t1s[:], in1=sgs[:], op=ALU.mult), reads=[B_t1s, B_sgs], writes=[B_mix])
            qk = sb.tile([4, 8, 64], F32, "qk"); B_qk = Buf("qk")
            enew = sb.tile([4, 8], F32, "enew"); B_enew = Buf("enew")
            S.op("dve", lambda e: e.tensor_tensor(out=qk[:].rearrange("p h d -> p (h d)"), in0=fq_tm[:], in1=fk_tm[:], op=ALU.mult),
                 reads=[B_fq_tm, B_fk_tm], writes=[B_qk])
            S.op("dve", lambda e: e.tensor_reduce(out=enew[:], in_=qk[:], axis=AX.X, op=ALU.add), reads=[B_qk], writes=[B_enew])
            S.op("act", lambda e: e.activation(out=enew[:], in_=enew[:], func=AF.Exp, scale=0.125), reads=[B_enew], writes=[B_enew])
            qb = sb.tile([128, 4, 512], F32, "qb"); B_qb = Buf("qb")
            for bb in range(4):
                S.op("pe", lambda e: e.matmul(ps[2][:, :], lhsT=sel4[:, bb, :], rhs=fq_tm[:], start=True, stop=True), reads=[B_sel4, B_fq_tm], writes=[Bps[2]])
                S.op("act", lambda e: e.mul(out=qb[:, bb, :], in_=ps[2][:, :], mul=0.125), reads=[Bps[2]], writes=[B_qb])
            lfb = sb.tile([128, 4, 8], F32, "lfb"); B_lfb = Buf("lfb")
            for bb in range(4):
                S.op("pe", lambda e: e.matmul(ps[2][:, 0:8], lhsT=sel4[:, bb, :], rhs=lfs_tm[:], start=True, stop=True), reads=[B_sel4, B_lfs], writes=[Bps[2]])
                S.op("act", lambda e: e.copy(out=lfb[:, bb, :], in_=ps[2][:, 0:8]), reads=[Bps[2]], writes=[B_lfb])
            S.barrier()
        m64 = sb.tile([128, 512], F32, "m64"); B_m64 = Buf("m64")
        Iscan = sb.tile([128, 512], F32, "Iscan"); B_I = Buf("I")
        S.op("pool", lambda e: e.memset(m64[:], 1.0), writes=[B_m64])
        S.op("pool", lambda e: e.memset(m64[:].rearrange("p (h n) -> p h n", n=64)[:, :, 0:1], 0.0), writes=[B_m64])
        lf2 = lf_tab[:].rearrange("p h n -> p (h n)")
        S.op("dve", lambda e: e.tensor_tensor_scan(out=Iscan[:], data0=m64[:], data1=lf2, initial=0.0, op0=ALU.mult, op1=ALU.add),
             reads=[B_m64, B_lf], writes=[B_I])
        S.op("dve", lambda e: e.tensor_tensor(out=Iscan[:], in0=Iscan[:], in1=lf2, op=ALU.subtract), reads=[B_I, B_lf], writes=[B_I])
        S.op("pe", lambda e: e.matmul(ps[0][:, :], lhsT=U_f[:], rhs=lf2, start=True, stop=False), reads=[B_U, B_lf], writes=[Bps[0]])
        S.op("pe", lambda e: e.matmul(ps[0][:, :], lhsT=ones_f[:], rhs=Iscan[:], start=False, stop=True), reads=[B_ones_f, B_I], writes=[Bps[0]])
        S.op("act", lambda e: e.copy(out=C_tab[:].rearrange("p h n -> p (h n)"), in_=ps[0][:, :]), reads=[Bps[0]], writes=[B_C])
        oh = sb.tile([128, 4, 64], F32, "oh"); B_oh = Buf("oh")
        S.dma("sp", lambda e: e.dma_start(out=oh[:], in_=onehot.rearrange("s p n -> p s n")), writes=[B_oh])
        cref = sb.tile([128, 4, H], F32, "cref"); B_cref = Buf("cref")
        ctmp = sb.tile([128, H, NB], F32, "ctmp"); B_ctmp = Buf("ctmp")
        cpart = sb.tile([128, H], F32, "cpart"); B_cpart = Buf("cpart")
        for s_ in range(4):
            S.op("dve", lambda e: e.tensor_tensor(out=ctmp[:], in0=C_tab[:], in1=oh[:, s_:s_ + 1, :].to_broadcast([128, H, NB]), op=ALU.mult),
                 reads=[B_C, B_oh], writes=[B_ctmp])
            S.op("dve", lambda e: e.tensor_reduce(out=cpart[:], in_=ctmp[:], axis=AX.X, op=ALU.add), reads=[B_ctmp], writes=[B_cpart])
            S.op("pe", lambda e: e.matmul(ps[0][:, 0:H], lhsT=ones_f[:], rhs=cpart[:], start=True, stop=True), reads=[B_ones_f, B_cpart], writes=[Bps[0]])
            S.op("act", lambda e: e.copy(out=cref[:, s_, :], in_=ps[0][:, 0:H]), reads=[Bps[0]], writes=[B_cref])
        qown = [sb.tile([128, 2048], BF16, "qown%d" % p) for p in range(4)]; B_qown = [Buf("qown%d" % p) for p in range(4)]
        m_proj = sb.mark()
        w2_bf = sb.tile([128, 8, 1024], BF16, "w2_bf"); B_w2 = Buf("w2")
        wst2 = [sb.tile([128, 1024], F32, "wst2_%d" % i) for i in range(2)]; B_wst2 = [Buf("wst2_0"), Buf("wst2_1")]
        for kc in range(8):
            i = kc % 2
            S.dma("sp", lambda e: e.dma_start(out=wst2[i][:], in_=w2[kc * 128:(kc + 1) * 128, :]), writes=[B_wst2[i]])
            S.op("dve" if i == 0 else "pool", lambda e: e.tensor_copy(out=w2_bf[:, kc, :], in_=wst2[i][:]), reads=[B_wst2[i]], writes=[B_w2])
        xo = sb.tile([128, 8, 512], F32, "xo"); B_xo = Buf("xo")
        sq2 = sb.tile([128, 8, 512], BF16, "sq2"); B_sq2 = Buf("sq2")
        xn2 = sb.tile([128, 8, 512], BF16, "xn2"); B_xn2 = Buf("xn2")
        rstd2 = sb.tile([128, 512], F32, "rstd2"); B_rstd2 = Buf("rstd2")
        og = sb.tile([128, 512], BF16, "og"); B_og = Buf("og")
        ogsq = sb.tile([128, 1, 512], BF16, "ogsq"); B_ogsq = Buf("ogsq")
        rstdo = sb.tile([128, 512], F32, "rstdo"); B_rstdo = Buf("rstdo")
        sg = sb.tile([128, 512], F32, "sg"); B_sg = Buf("sg")
        t1 = sb.tile([128, 512], F32, "t1"); B_t1 = Buf("t1")
        idxo = sb.tile([128, 16], I32, "idxo"); B_idxo = Buf("idxo")
        S.dma("sp", lambda e: e.dma_start(out=idxo[:], in_=idx_o), writes=[B_idxo])
        for s_ in range(4):
            cs = slice(s_ * 512, (s_ + 1) * 512)
            S.dma("sp", lambda e: e.dma_start(out=xo[:], in_=xT_own_v[:, :, cs]), writes=[B_xo])
            rms_rstd(xo, B_xo, 512, sq2, B_sq2, rstd2, B_rstd2, 0)
            for kc in range(8):
                S.op("dve", lambda e: e.scalar_tensor_tensor(out=xn2[:, kc, :], in0=xo[:, kc, :], scalar=gn[:, kc:kc + 1],
                                                             in1=rstd2[:], op0=ALU.mult, op1=ALU.mult),
                     reads=[B_xo, B_gn, B_rstd2], writes=[B_xn2])
            for p in range(4):
                pb = 1 + (p % 2)
                for kc in range(8):
                    S.op("pe", lambda e: e.matmul(ps[pb][:, :], lhsT=w2_bf[:, kc, p * 128:(p + 1) * 128], rhs=xn2[:, kc, :],
                                                  start=(kc == 0), stop=(kc == 7)), reads=[B_w2, B_xn2], writes=[Bps[pb]])
                S.op("act", lambda e: e.mul(out=qown[p][:, cs], in_=ps[pb][:, :], mul=0.125), reads=[Bps[pb]], writes=[B_qown[p]])
            if STAGE >= 4:
                for h in range(GH):
                    pb = 3 + (h % 2)
                    for kc in range(8):
                        S.op("pe", lambda e: e.matmul(ps[pb][:, :], lhsT=w2_bf[:, kc, 512 + h * 128:512 + (h + 1) * 128], rhs=xn2[:, kc, :],
                                                      start=(kc == 0), stop=(kc == 7)), reads=[B_w2, B_xn2], writes=[Bps[pb]])
                    S.dma("pool", lambda e: e.indirect_dma_start(out=og[:], out_offset=None, in_=og_d,
                                                                  in_offset=bass.IndirectOffsetOnAxis(ap=idxo[:, s_ * 4 + h:s_ * 4 + h + 1], axis=0)),
                          reads=[B_ogd, B_idxo], writes=[B_og])
                    rms_rstd(og[:].rearrange("p (o n) -> p o n", o=1), B_og, 512, ogsq, B_ogsq, rstdo, B_rstdo, 5, nfeat=1, dim=128.0)
                    S.op("act", lambda e: e.activation(out=sg[:], in_=ps[pb][:, :], func=AF.Exp, scale=-1.0), reads=[Bps[pb]], writes=[B_sg])
                    S.op("dve", lambda e: e.tensor_scalar(out=sg[:], in0=sg[:], scalar1=1.0, scalar2=None, op0=ALU.add), reads=[B_sg], writes=[B_sg])
                    S.op("dve", lambda e: e.reciprocal(out=sg[:], in_=sg[:]), reads=[B_sg], writes=[B_sg])
                    S.op("dve", lambda e: e.tensor_tensor(out=sg[:], in0=ps[pb][:, :], in1=sg[:], op=ALU.mult), reads=[Bps[pb], B_sg], writes=[B_sg])
                    S.op("dve", lambda e: e.scalar_tensor_tensor(out=t1[:], in0=og[:], scalar=gn[:, 24:25], in1=rstdo[:], op0=ALU.mult, op1=ALU.mult),
                         reads=[B_og, B_gn, B_rstdo], writes=[B_t1])
                    S.op("dve", lambda e: e.tensor_tensor(out=mix[:, 4 + h, cs], in0=t1[:], in1=sg[:], op=ALU.mult), reads=[B_t1, B_sg], writes=[B_mix])
        S.barrier()
        sb.release(m_proj)
        if STAGE >= 6:
            CKT = 4
            NCK = 128 // CKT
            TOP2 = 229312 - 49 * 1024
            sbt2 = SbAlloc(nc, base=TOP2, top=229312, prefix="U")
            kc_t = [sbt2.tile([128, CKT * 512], F32, "kc_t%d" % i) for i in range(2)]; B_kc = [Buf("kc0"), Buf("kc1")]
            vc_t = [sbt2.tile([128, CKT * 512], F32, "vc_t%d" % i) for i in range(2)]; B_vc = [Buf("vc0"), Buf("vc1")]
            tmpk = sbt2.tile([128, CKT * 512], F32, "tmpk"); B_tmpk = Buf("tmpk")
            lfp = tmpk[:, 0:1024]; B_lfp = B_tmpk
            Ipre = sbt2.tile([128, 8, 128], F32, "Ipre"); B_Ipre = Buf("Ipre")
            scs = sbt2.tile([128, 128, 8], F32, "scs"); B_scs = Buf("scs")
            ps7 = psT_bf[:].bitcast(F32); Bps7 = B_psT
            totc = sb.tile([128, 8], F32, "totc"); B_totc = Buf("totc")
            tb = sb.tile([128, 8], F32, "tb"); B_tb = Buf("tb")
            dsum = sb.tile([128, 8], F32, "dsum"); B_dsum = Buf("dsum")
            tmpd = sb.tile([8, 8, 64], F32, "tmpd"); B_tmpd = Buf("tmpd")
            Od = sb.tile([8, 64], F32, "Od"); B_Od = Buf("Od")
            den8 = sb.tile([8, 1], F32, "den8"); B_den8 = Buf("den8")
            scr_o = dscr("scr_o", [4, 512], F32); B_scro = Buf("scr_o")
            scr_d = dscr("scr_d", [4, 8], F32); B_scrd = Buf("scr_d")
            idxk = sb.tile([128, 4, NCK], I32, "idxk"); B_idxk = Buf("idxk")
            for ck in range(NCK):
                S.op("dve", lambda e: e.tensor_scalar(out=idxk[:, :, ck], in0=pt_t[:], scalar1=float(NCK), scalar2=float(ck), op0=ALU.mult, op1=ALU.add),
                     reads=[B_pt], writes=[B_idxk])

            def decode_gen():
              for bb in range(4):
                  S.dma("pool", lambda e: e.indirect_dma_start(out=lfp, out_offset=None, in_=cache_lf,
                                                                in_offset=bass.IndirectOffsetOnAxis(ap=pt_t[:, bb:bb + 1], axis=0)),
                        reads=[B_pt], writes=[B_lfp])
                  S.op("dve", lambda e: e.tensor_tensor_scan(out=Ipre[:].rearrange("p h i -> p (h i)"), data0=m128[:], data1=lfp, initial=0.0,
                                                             op0=ALU.mult, op1=ALU.add), reads=[B_m128, B_lfp], writes=[B_Ipre])
                  S.op("dve", lambda e: e.tensor_copy(out=totc[:], in_=Ipre[:, :, 127]), reads=[B_Ipre], writes=[B_totc])
                  S.op("pe", lambda e: e.matmul(ps[0][:, 0:8], lhsT=SL_f[:], rhs=totc[:], start=True, stop=True), reads=[B_SL, B_totc], writes=[Bps[0]])
                  S.op("dve", lambda e: e.tensor_tensor(out=tb[:], in0=ps[0][:, 0:8], in1=totc[:], op=ALU.add), reads=[Bps[0], B_totc], writes=[B_tb])
                  S.op("dve", lambda e: e.tensor_tensor(out=tb[:], in0=tb[:], in1=lfb[:, bb, :], op=ALU.add), reads=[B_tb, B_lfb], writes=[B_tb])
                  S.op("dve", lambda e: e.tensor_tensor(out=Ipre[:], in0=tb[:].unsqueeze(2).to_broadcast([128, 8, 128]), in1=Ipre[:], op=ALU.subtract),
                       reads=[B_tb, B_Ipre], writes=[B_Ipre])
                  for ck in range(NCK):
                      i = ck % 2
                      S.dma("pool", lambda e: e.indirect_dma_start(out=kc_t[i][:], out_offset=None, in_=cache_k2,
                                                                    in_offset=bass.IndirectOffsetOnAxis(ap=idxk[:, bb, ck:ck + 1], axis=0)),
                            reads=[B_idxk], writes=[B_kc[i]])
                      S.op("dve", lambda e: e.tensor_tensor(out=tmpk[:].rearrange("p (t f) -> p t f", f=512), in0=kc_t[i][:].rearrange("p (t f) -> p t f", f=512),
                                                            in1=qb[:, bb:bb + 1, :].to_broadcast([128, CKT, 512]), op=ALU.mult),
                           reads=[B_kc[i], B_qb], writes=[B_tmpk])
                      S.op("dve", lambda e: e.tensor_reduce(out=scs[:, ck * CKT:(ck + 1) * CKT, :].rearrange("p t h -> p (t h)"),
                                                            in_=tmpk[:].rearrange("p (g d) -> p g d", d=64), axis=AX.X, op=ALU.add),
                           reads=[B_tmpk], writes=[B_scs])
                      yield
                  S.op("dve", lambda e: e.tensor_tensor(out=scs[:], in0=scs[:], in1=Ipre[:].rearrange("p h i -> p i h"), op=ALU.add),
                       reads=[B_scs, B_Ipre], writes=[B_scs])
                  S.op("act", lambda e: e.activation(out=scs[:], in_=scs[:], func=AF.Exp), reads=[B_scs], writes=[B_scs])
                  S.op("dve", lambda e: e.tensor_reduce(out=dsum[:], in_=scs[:].rearrange("p i h -> p h i"), axis=AX.X, op=ALU.add), reads=[B_scs], writes=[B_dsum])
                  for ck in range(NCK):
                      i = ck % 2
                      S.dma("pool", lambda e: e.indirect_dma_start(out=vc_t[i][:], out_offset=None, in_=cache_v2,
                                                                    in_offset=bass.IndirectOffsetOnAxis(ap=idxk[:, bb, ck:ck + 1], axis=0)),
                            reads=[B_idxk], writes=[B_vc[i]])
                      for il in range(CKT):
                          ii = ck * CKT + il
                          S.op("pe", lambda e: e.matmul(ps7[0:8, :], lhsT=scs[:, ii, :], rhs=vc_t[i][:, il * 512:(il + 1) * 512],
                                                        start=(ii == 0), stop=(ii == NCK * CKT - 1)), reads=[B_scs, B_vc[i]], writes=[Bps7])
                      yield
                  S.op("pe", lambda e: e.matmul(ps[0][0:8, 8:9], lhsT=dsum[:], rhs=ones_f[:, 0:1], start=True, stop=True), reads=[B_dsum, B_ones_f], writes=[Bps[0]])
                  S.op("dve", lambda e: e.tensor_tensor(out=tmpd[:].rearrange("p a d -> p (a d)"), in0=ps7[0:8, :], in1=bm[:].rearrange("p a d -> p (a d)"), op=ALU.mult),
                       reads=[Bps7, B_bm], writes=[B_tmpd])
                  S.op("dve", lambda e: e.tensor_reduce(out=Od[:], in_=tmpd[:].rearrange("p a d -> p d a"), axis=AX.X, op=ALU.add), reads=[B_tmpd], writes=[B_Od])
                  S.op("act", lambda e: e.copy(out=den8[:], in_=ps[0][0:8, 8:9]), reads=[Bps[0]], writes=[B_den8])
                  S.dma("sp", lambda e: e.dma_start(out=scr_o[bb:bb + 1, :].rearrange("o (h d) -> (o h) d", d=64), in_=Od[:]), reads=[B_Od], writes=[B_scro])
                  S.dma("sp", lambda e: e.dma_start(out=scr_d[bb:bb + 1, :].rearrange("o h -> h o"), in_=den8[:]), reads=[B_den8], writes=[B_scrd])
        kts = [sb.tile([128, SEQ], BF16, "kts%d" % i) for i in range(1)]; B_kts = [Buf("kts0")]
        vts = [sb.tile([128, NB, 65], BF16, "vts%d" % i) for i in range(2)]; B_vts = [Buf("vts0"), Buf("vts1")]
        mk = sb.tile([128, 16, 512], BF16, "mk"); B_mk = Buf("mk")
        bias_t = [sb.tile([128, NB], F32, "bias%d" % i) for i in range(2)]; B_bias = [Buf("bias0"), Buf("bias1")]
        NPT = 4
        Pt = [sb.tile([128, 512], BF16, "Pt%d" % i) for i in range(NPT)]; B_Pt = [Buf("Pt%d" % i) for i in range(NPT)]
        rcs = sb.tile([128, 512], F32, "rcs"); B_rcs = Buf("rcs")
        Osb = sb.tile([64, 512], F32, "Osb"); B_Osb = Buf("Osb")
        dec_it = decode_gen() if STAGE >= 6 else iter(())
        DEC_EVERY = 2
        SBANK = [1, 2, 4]
        OBANK = [3, 6]
        LA = 2
        unit = 0
        hcount = 0
        pcount = 0
        mb_t = sb.tile([128, 64], F32, "mb_t"); B_mb = Buf("mb")
        S.dma("sp", lambda e: e.dma_start(out=mb_t[:], in_=mbias), writes=[B_mb])
        for s_ in range(4):
            nk = 16 * s_ + 16
            cs = slice(s_ * 512, (s_ + 1) * 512)
            S.dma("sp", lambda e: e.dma_start(out=mk[:], in_=masks[s_].rearrange("j p q -> p j q")), writes=[B_mk])
            for p in range(4):
                kb_ = pcount % len(kts)
                pcount += 1
                S.dma("sp", lambda e: e.dma_start(out=kts[kb_][:, 0:nk * 128], in_=kT_d[p * 128:(p + 1) * 128, 0:nk * 128]),
                      reads=[B_kTd], writes=[B_kts[kb_]])
                for hh in range(2):
                    h = 2 * p + hh
                    vb = hcount % 2
                    ob = OBANK[hcount % 2]
                    hcount += 1
                    rs = slice(hh * 64, (hh + 1) * 64)
                    S.dma("sp", lambda e: e.dma_start(out=vts[vb][:, 0:nk, :], in_=v_d[h, :, 0:nk, :]), reads=[B_vd], writes=[B_vts[vb]])
                    S.op("dve", lambda e: e.tensor_scalar(out=bias_t[vb][:, 0:nk], in0=C_tab[:, h, 0:nk], scalar1=-1.0,
                                                          scalar2=cref[:, s_, h:h + 1], op0=ALU.mult, op1=ALU.add),
                         reads=[B_C, B_cref], writes=[B_bias[vb]])
                    S.op("dve", lambda e: e.tensor_tensor(out=bias_t[vb][:, 16 * s_:16 * s_ + 16], in0=bias_t[vb][:, 16 * s_:16 * s_ + 16],
                                                          in1=mb_t[:, 16 * s_:16 * s_ + 16], op=ALU.add),
                         reads=[B_bias[vb], B_mb], writes=[B_bias[vb]])
                    slot_of = {}
                    for n2 in range(nk + LA):
                        if n2 < nk:
                            n = n2
                            pb = SBANK[unit % 3]
                            pt = unit % NPT
                            unit += 1
                            if unit % DEC_EVERY == 0:
                                next(dec_it, None)
                            slot_of[n] = (pb, pt)
                            S.op("pe", lambda e: e.matmul(ps[pb][:, :], lhsT=kts[kb_][rs, n * 128:(n + 1) * 128], rhs=qown[p][rs, cs], start=True, stop=True),
                                 reads=[B_kts[kb_], B_qown[p]], writes=[Bps[pb]])
                        if n2 - LA >= 0:
                            n = n2 - LA
                            pb, pt = slot_of[n]
                            S.op("act", lambda e: e.activation(out=Pt[pt][:], in_=ps[pb][:, :], func=AF.Exp, bias=bias_t[vb][:, n:n + 1], scale=1.0),
                                 reads=[Bps[pb], B_bias[vb]], writes=[B_Pt[pt]])
                            if n >= 16 * s_:
                                S.op("pool", lambda e: e.tensor_tensor(out=Pt[pt][:], in0=Pt[pt][:], in1=mk[:, n - 16 * s_, :], op=ALU.mult),
                                     reads=[B_Pt[pt], B_mk], writes=[B_Pt[pt]])
                            S.op("pe", lambda e: e.matmul(ps[ob][0:65, :], lhsT=vts[vb][:, n, :], rhs=Pt[pt][:], start=(n == 0), stop=(n == nk - 1)),
                                 reads=[B_vts[vb], B_Pt[pt]], writes=[Bps[ob]])
                    S.op("dve", lambda e: e.reciprocal(out=rcs[64:65, :], in_=ps[ob][64:65, :]), reads=[Bps[ob]], writes=[B_rcs])
                    S.op("act", lambda e: e.copy(out=Osb[:], in_=ps[ob][0:64, :]), reads=[Bps[ob]], writes=[B_Osb])
                    S.op("pe", lambda e: e.matmul(ps[5][0:64, :], lhsT=ones_f[64:65, 0:64], rhs=rcs[64:65, :], start=True, stop=True),
                         reads=[B_ones_f, B_rcs], writes=[Bps[5]])
                    S.op("dve", lambda e: e.tensor_tensor(out=mix[rs, p, cs], in0=Osb[:], in1=ps[5][0:64, :], op=ALU.mult),
                         reads=[B_Osb, Bps[5]], writes=[B_mix])

        if STAGE >= 6:
            for _ in dec_it:
                pass
            S.dma("sp", lambda e: e.dma_start(out=Otm[:], in_=scr_o), reads=[B_scro], writes=[B_Otm])
            S.dma("sp", lambda e: e.dma_start(out=dentm[:], in_=scr_d), reads=[B_scrd], writes=[B_dentm])
            S.op("dve", lambda e: e.tensor_tensor(out=qk[:], in0=fv_tm[:].rearrange("p (h d) -> p h d", d=64), in1=enew[:].unsqueeze(2).to_broadcast([4, 8, 64]), op=ALU.mult),
                 reads=[B_fv_tm, B_enew], writes=[B_qk])
            S.op("dve", lambda e: e.tensor_tensor(out=Otm[:], in0=Otm[:], in1=qk[:].rearrange("p h d -> p (h d)"), op=ALU.add), reads=[B_Otm, B_qk], writes=[B_Otm])
            S.op("dve", lambda e: e.tensor_tensor(out=dentm[:], in0=dentm[:], in1=enew[:], op=ALU.add), reads=[B_dentm, B_enew], writes=[B_dentm])
            S.op("dve", lambda e: e.reciprocal(out=dentm[:], in_=dentm[:]), reads=[B_dentm], writes=[B_dentm])
            S.op("dve", lambda e: e.tensor_tensor(out=Otm[:].rearrange("p (h d) -> p h d", d=64), in0=Otm[:].rearrange("p (h d) -> p h d", d=64),
                                                  in1=dentm[:].unsqueeze(2).to_broadcast([4, 8, 64]), op=ALU.mult), reads=[B_Otm, B_dentm], writes=[B_Otm])
            for pr in range(4):
                S.op("pe", lambda e: e.transpose(out=ps[5][:, 0:4], in_=Otm[:, pr * 128:(pr + 1) * 128], identity=ident_f[0:4, 0:4]),
                     reads=[B_Otm, B_ident], writes=[Bps[5]])
                S.op("act", lambda e: e.copy(out=mix[:, pr, 2048:2052], in_=ps[5][:, 0:4]), reads=[Bps[5]], writes=[B_mix])
        assert sb.off <= TOP2, (sb.off, TOP2)
    if STAGE >= 3 and 'M' in DBG:
        o_mix = dout("o_mix", [128, 8, NOWN], BF16)
        B_omix = Buf("o_mix"); out_bufs.append(B_omix)
        S.dma("sp", lambda e: e.dma_start(out=o_mix, in_=mix[:]), reads=[B_mix], writes=[B_omix])
    if STAGE >= 5:
        S.barrier()
        sb.release(m_p2)
        wo_bf = sb.tile([128, 8, 1024], BF16, "wo_bf"); B_wo = Buf("wo")
        wup_bf = sb.tile([128, 8, 4096], BF16, "wup_bf"); B_wup = Buf("wup")
        wdn_bf = sb.tile([128, 32, 1024], BF16, "wdn_bf"); B_wdn = Buf("wdn")
        wst3 = [sb.tile([128, 512], F32, "wst3_%d" % i) for i in range(2)]; B_wst3 = [Buf("wst3_0"), Buf("wst3_1")]
        cnt3 = 0
        jobs = [(w_o[kc * 128:(kc + 1) * 128, q * 512:(q + 1) * 512], wo_bf[:, kc, q * 512:(q + 1) * 512], B_wo) for kc in range(8) for q in range(2)]
        jobs += [(w_up[kc * 128:(kc + 1) * 128, q * 512:(q + 1) * 512], wup_bf[:, kc, q * 512:(q + 1) * 512], B_wup)
                 for kc in range(8) for q in range(8)]
        jobs += [(w_down[f * 128:(f + 1) * 128, q * 512:(q + 1) * 512], wdn_bf[:, f, q * 512:(q + 1) * 512], B_wdn)
                 for f in range(32) for q in range(2)]
        for src, dst, Bd in jobs:
            i = cnt3 % 2
            cnt3 += 1
            S.dma("sp", lambda e: e.dma_start(out=wst3[i][:], in_=src), writes=[B_wst3[i]])
            eng3 = ["dve", "pool", "act"][cnt3 % 3]
            if eng3 == "act":
                S.op("act", lambda e: e.copy(out=dst, in_=wst3[i][:]), reads=[B_wst3[i]], writes=[Bd])
            else:
                S.op(eng3, lambda e: e.tensor_copy(out=dst, in_=wst3[i][:]), reads=[B_wst3[i]], writes=[Bd])
        NT3 = NOWN // P3T
        x3 = sb.tile([128, 8, P3T], F32, "x3"); B_x3 = Buf("x3")
        hT = sb.tile([128, 8, P3T], F32, "hT"); B_hT = Buf("hT")
        sq3 = sb.tile([128, 8, P3T], BF16, "sq3"); B_sq3 = Buf("sq3")
        hn = sb.tile([128, 8, P3T], BF16, "hn"); B_hn = Buf("hn")
        rs3 = sb.tile([128, P3T], F32, "rs3"); B_rs3 = Buf("rs3")
        rr = [sb.tile([128, P3T], F32, "rr%d" % i) for i in range(2)]; B_rr = [Buf("rr0"), Buf("rr1")]
        uT = sb.tile([128, 32, P3T], BF16, "uT"); B_uT = Buf("uT")
        yT = x3; B_yT = B_x3
        o_y_v = o_y.rearrange("(m p) t -> p m t", p=128)
        for tt in range(NT3):
            cs = slice(tt * P3T, (tt + 1) * P3T)
            S.dma("sp", lambda e: e.dma_start(out=x3[:], in_=xT_own_v[:, :, cs]), writes=[B_x3])
            for m in range(8):
                pb = 1 + (m % 2)
                for kc in range(8):
                    S.op("pe", lambda e: e.matmul(ps[pb][:, 0:P3T], lhsT=wo_bf[:, kc, m * 128:(m + 1) * 128], rhs=mix[:, kc, cs],
                                                  start=(kc == 0), stop=(kc == 7)), reads=[B_wo, B_mix], writes=[Bps[pb]])
                S.op("dve", lambda e: e.tensor_tensor(out=hT[:, m, :], in0=ps[pb][:, 0:P3T], in1=x3[:, m, :], op=ALU.add),
                     reads=[Bps[pb], B_x3], writes=[B_hT])
            rms_rstd(hT, B_hT, P3T, sq3, B_sq3, rs3, B_rs3, 0)
            for kc in range(8):
                S.op("dve", lambda e: e.scalar_tensor_tensor(out=hn[:, kc, :], in0=hT[:, kc, :], scalar=gn[:, 8 + kc:9 + kc], in1=rs3[:],
                                                             op0=ALU.mult, op1=ALU.mult), reads=[B_hT, B_gn, B_rs3], writes=[B_hn])
            for f in range(32):
                pb = 3 + (f % 2)
                for kc in range(8):
                    S.op("pe", lambda e: e.matmul(ps[pb][:, 0:P3T], lhsT=wup_bf[:, kc, f * 128:(f + 1) * 128], rhs=hn[:, kc, :],
                                                  start=(kc == 0), stop=(kc == 7)), reads=[B_wup, B_hn], writes=[Bps[pb]])
                j = f % 2
                S.op("act", lambda e: e.activation(out=rr[j][:], in_=ps[pb][:, 0:P3T], func=AF.Relu), reads=[Bps[pb]], writes=[B_rr[j]])
                S.op("dve" if j == 0 else "pool", lambda e: e.tensor_tensor(out=uT[:, f, :], in0=rr[j][:], in1=rr[j][:], op=ALU.mult),
                     reads=[B_rr[j]], writes=[B_uT])
            for m in range(8):
                pb = 1 + (m % 2)
                for f in range(32):
                    S.op("pe", lambda e: e.matmul(ps[pb][:, 0:P3T], lhsT=wdn_bf[:, f, m * 128:(m + 1) * 128], rhs=uT[:, f, :],
                                                  start=(f == 0), stop=(f == 31)), reads=[B_wdn, B_uT], writes=[Bps[pb]])
                S.op("dve", lambda e: e.tensor_tensor(out=hT[:, m, :], in0=ps[pb][:, 0:P3T], in1=hT[:, m, :], op=ALU.add),
                     reads=[Bps[pb], B_hT], writes=[B_hT])
            rms_rstd(hT, B_hT, P3T, sq3, B_sq3, rs3, B_rs3, 0)
            for m in range(8):
                S.op("dve", lambda e: e.scalar_tensor_tensor(out=yT[:, m, :], in0=hT[:, m, :], scalar=gn[:, 16 + m:17 + m], in1=rs3[:],
                                                             op0=ALU.mult, op1=ALU.mult), reads=[B_hT, B_gn, B_rs3], writes=[B_yT])
            S.dma("sp", lambda e: e.dma_start(out=o_y_v[:, :, cs], in_=yT[:]), reads=[B_yT], writes=[B_oy])

    k.out_bufs = out_bufs
    k.locals = locals()
    return k


def finish_program(k):
    S = k.S
    S.wait_all("sp", k.out_bufs)
    S.emit()
    return k.nc


def kernel(x_prompt, x_sample, cache_k, cache_v, cache_logf, state_gla, page_table,
           norm1_g, w_in, fox_b_f, gla_w_gate_up, gla_b_gate, gla_norm_g, w_o,
           norm2_g, w_up, w_down, final_g):
    f32 = np.float32
    import ml_dtypes
    x_prompt = np.asarray(x_prompt, f32); x_sample = np.asarray(x_sample, f32)
    w_in0 = np.asarray(w_in, f32)[0]
    k = build_program()
    nc = finish_program(k)
    cols1 = np.concatenate([np.arange(C_FK, C_FK + 512), np.arange(C_GQ, C_GQ + 256), np.arange(C_GK, C_GK + 256),
                            np.arange(C_GLR, C_GLR + 16), np.arange(C_FV, C_FV + 512), np.arange(C_GV, C_GV + 512),
                            np.arange(C_FF, C_FF + 8)])
    w1 = np.ascontiguousarray(w_in0[:, cols1])
    cols2 = np.concatenate([np.arange(C_FQ, C_FQ + 512), np.arange(C_GG, C_GG + 512)])
    w2 = np.ascontiguousarray(w_in0[:, cols2])
    gains = np.zeros((128, 40), f32)
    gains[:, 0:8] = np.asarray(norm1_g, f32)[0].reshape(8, 128).T
    gains[:, 8:16] = np.asarray(norm2_g, f32)[0].reshape(8, 128).T
    gains[:, 16:24] = np.asarray(final_g, f32).reshape(8, 128).T
    gains[:, 24] = np.asarray(gla_norm_g, f32)[0]
    gains[:, 25:27] = -np.asarray(gla_b_gate, f32)[0].reshape(2, 128).T
    bfrep = np.broadcast_to(np.asarray(fox_b_f, f32)[0][None, :], (128, 8)).copy()
    wg = np.asarray(gla_w_gate_up, f32)[0]
    perm_s = np.concatenate([np.arange(C_FQ, C_FQ + 512), np.arange(C_FK, C_FK + 512), np.arange(C_FV, C_FV + 512),
                             np.arange(C_GQ, C_GQ + 256), np.arange(C_GK, C_GK + 256), np.arange(C_GV, C_GV + 512),
                             np.arange(C_GG, C_GG + 512), np.arange(C_GLR, C_GLR + 16), np.arange(C_FF, C_FF + 8)])
    w_s = np.ascontiguousarray(w_in0[:, perm_s])
    if STAGE >= 6:
        ck2 = np.asarray(cache_k, f32)[0].reshape(-1, 2048)
        cv2 = np.asarray(cache_v, f32)[0].reshape(-1, 2048)
        clf = np.ascontiguousarray(np.asarray(cache_logf, f32)[0].transpose(0, 2, 1)).reshape(-1, 1024)
        pt_all = np.asarray(page_table, np.int32)
        sg_all = np.asarray(state_gla, f32)[0]
    in_maps = []
    for c in range(8):
        b, r = c // 4, c % 4
        tiles = own_tiles(r)
        xo = np.concatenate([x_prompt[b, t * 512:(t + 1) * 512, :] for t in tiles] + [x_sample[4 * c:4 * c + 4, 0, :]], axis=0)
        masks = np.zeros((4, 16, 128, 512), ml_dtypes.bfloat16)
        onehot = np.zeros((4, 128, 64), f32)
        idx_o = np.zeros((128, 16), np.int32)
        mbias = np.zeros((128, 64), f32)
        kp = np.arange(128)[:, None]; qp = np.arange(512)[None, :]
        for s_, t in enumerate(tiles):
            for j in range(16):
                kb = 16 * s_ + j
                if kb < 4 * t:
                    masks[s_, j] = 1
                elif kb < 4 * t + 4:
                    masks[s_, j] = ((kb - 4 * t) * 128 + kp <= qp)
                else:
                    mbias[:, s_ * 16 + j] = -30000.0
            if t > 0:
                onehot[s_, 127, 4 * t - 1] = 1.0
            for h in range(GH):
                idx_o[:, s_ * 4 + h] = (t * GH + h) * 128 + np.arange(128)
        m = {"xT_full": np.ascontiguousarray(x_prompt[b].T), "xT_own": np.ascontiguousarray(xo.T),
             "w1": w1, "w2": w2, "w_o": np.asarray(w_o, f32)[0], "w_up": np.asarray(w_up, f32)[0],
             "w_down": np.asarray(w_down, f32)[0], "gains": gains, "bfrep": bfrep, "wg": wg,
             "masks": masks, "onehot": onehot, "idx_o": idx_o, "mbias": mbias, "w_s": w_s}
        if STAGE >= 6:
            m["ptab"] = np.ascontiguousarray(pt_all[4 * c:4 * c + 4, :].T)
            m["cache_k2"] = ck2; m["cache_v2"] = cv2; m["cache_lf"] = clf
            sg = sg_all[4 * c:4 * c + 4].reshape(4, 2, 2, 64, 128).transpose(1, 2, 3, 0, 4).reshape(2, 128, 4, 128)
            m["sgla"] = np.ascontiguousarray(sg)
        in_maps.append(m)
    used = set(k.used_inputs)
    in_maps = [{n: v for n, v in m.items() if n in used} for m in in_maps]
    res = run_bass_kernel_spmd(nc, in_maps, core_ids=list(range(8)))
    R = res.results
    y_prompt = np.zeros((2, SEQ, D), f32); y_sample = np.zeros((32, 1, D), f32)
    new_k = np.zeros((1, 2, SEQ, H, 64), f32); new_v = np.zeros((1, 2, SEQ, H, 64), f32)
    new_lf = np.zeros((1, 2, SEQ, H), f32); new_gla = np.zeros((1, 2, GH, 64, 128), f32)
    ks = np.zeros((1, 32, 1, H, 64), f32); vs = np.zeros((1, 32, 1, H, 64), f32)
    lfs = np.zeros((1, 32, 1, H), f32); glas = np.zeros((1, 32, GH, 64, 128), f32)
    for c in range(8):
        b, r = c // 4, c % 4
        o = R[c]
        if r == 0:
            new_k[0, b] = o["o_k"].T.reshape(SEQ, H, 64)
            new_v[0, b] = o["o_v"].reshape(SEQ, H, 64)
            new_lf[0, b] = o["o_lf"].reshape(128, H, NB).transpose(2, 0, 1).reshape(SEQ, H)
            new_gla[0, b] = o["o_gla"].reshape(GH, 64, 128)
        if "o_y" in o and STAGE >= 5:
            yT = o["o_y"]
            for s_, t in enumerate(own_tiles(r)):
                y_prompt[b, t * 512:(t + 1) * 512] = yT[:, s_ * 512:(s_ + 1) * 512].T
            y_sample[4 * c:4 * c + 4, 0] = yT[:, 2048:2052].T
        if STAGE >= 6:
            ks[0, 4 * c:4 * c + 4, 0] = o["o_ks"].reshape(4, H, 64)
            vs[0, 4 * c:4 * c + 4, 0] = o["o_vs"].reshape(4, H, 64)
            lfs[0, 4 * c:4 * c + 4, 0] = o["o_lfs"]
            glas[0, 4 * c:4 * c + 4] = o["o_glas"].reshape(2, 2, 64, 4, 128).transpose(3, 0, 1, 2, 4).reshape(4, GH, 64, 128)
    kernel.last_results = R
    return (y_prompt, y_sample, new_k, new_v, new_lf, new_gla, ks, vs, lfs, glas)
```

```python
import numpy as np
import concourse.bass as bass
import concourse.mybir as mybir
from concourse.bass_utils import run_bass_kernel_spmd

F32 = mybir.dt.float32
F32R = mybir.dt.float32r
BF16 = mybir.dt.bfloat16
I32 = mybir.dt.int32
ALU = mybir.AluOpType
AF = mybir.ActivationFunctionType
AX = mybir.AxisListType


import types


def _freeze(fn):
    if fn is None or fn.__closure__ is None:
        return fn
    cells = []
    for c in fn.__closure__:
        try:
            cells.append(types.CellType(c.cell_contents))
        except ValueError:
            cells.append(c)
    return types.FunctionType(fn.__code__, fn.__globals__, fn.__name__, fn.__defaults__, tuple(cells))


class Buf:
    __slots__ = ("name", "w", "r", "excl")

    def __init__(self, name, excl=False):
        self.name = name
        self.excl = excl
        self.w = None
        self.r = {}


class Sched:
    COMPUTE = ("pe", "act", "dve", "pool")

    def __init__(self, nc, n_dma_sems=40):
        self.nc = nc
        self.engs = {"pe": nc.tensor, "act": nc.scalar, "dve": nc.vector, "pool": nc.gpsimd, "sp": nc.sync}
        self.prog = {e: [] for e in self.engs}
        self.sems = {}
        for e in self.COMPUTE:
            self.sems[e] = nc.alloc_semaphore(name="c_" + e)
        self.cnt = {e: 0 for e in self.COMPUTE}
        self.dsem = [nc.alloc_semaphore(name="d%d" % i) for i in range(n_dma_sems)]
        self.dcnt = [0] * n_dma_sems
        self.drr = 0
        self.waited = {e: {} for e in self.engs}
        self.n_ops = 0

    def _sem(self, key):
        return self.sems[key] if isinstance(key, str) else self.dsem[key]

    def _deps(self, reads, writes, eng=None):
        deps = {}

        def add(tok):
            if tok is None:
                return
            k, v = tok
            if deps.get(k, 0) < v:
                deps[k] = v

        for b in reads:
            add(b.w)
            if b.excl:
                for k, v in b.r.items():
                    if k != eng:
                        add((k, v))
        for b in writes:
            add(b.w)
            for k, v in b.r.items():
                add((k, v))
        return deps

    def _emit_waits(self, eng, deps):
        waits = []
        wd = self.waited[eng]
        for k, v in deps.items():
            if k == "pe" and eng == "pe":
                continue
            if wd.get(k, 0) >= v:
                continue
            wd[k] = v
            waits.append((self._sem(k), v))
        return waits

    def _commit(self, tok, reads, writes):
        k, v = tok
        for b in writes:
            b.w = tok
            b.r = {}
        for b in reads:
            if b.r.get(k, 0) < v:
                b.r[k] = v

    def op(self, eng, fn, reads=(), writes=()):
        deps = self._deps(reads, writes, eng)
        waits = self._emit_waits(eng, deps)
        self.cnt[eng] += 1
        tok = (eng, self.cnt[eng])
        sem = self.sems[eng]
        self.prog[eng].append((waits, _freeze(fn), sem, 1))
        self._commit(tok, reads, writes)
        self.n_ops += 1
        return tok

    def dma(self, eng, fn, reads=(), writes=()):
        deps = self._deps(reads, writes)
        k = self.drr
        self.drr = (self.drr + 1) % len(self.dsem)
        if self.dcnt[k] > 0:
            deps[k] = max(deps.get(k, 0), 16 * self.dcnt[k])
        waits = self._emit_waits(eng, deps)
        self.dcnt[k] += 1
        tok = (k, 16 * self.dcnt[k])
        self.prog[eng].append((waits, _freeze(fn), self.dsem[k], 16))
        self._commit(tok, reads, writes)
        self.n_ops += 1
        return tok

    def barrier(self):
        deps = {e: self.cnt[e] for e in self.COMPUTE if self.cnt[e] > 0}
        for k_, c_ in enumerate(self.dcnt):
            if c_ > 0:
                deps[k_] = 16 * c_
        for e in self.engs:
            waits = self._emit_waits(e, dict(deps))
            self.prog[e].append((waits, None, None, 0))

    def wait_all(self, eng, bufs):
        deps = self._deps(bufs, ())
        waits = self._emit_waits(eng, deps)
        self.prog[eng].append((waits, None, None, 0))

    def emit(self):
        nc = self.nc
        with nc.Block() as block:
            def mk(ename):
                def body(e):
                    for waits, fn, sem, inc in self.prog[ename]:
                        for s, v in waits:
                            e.wait_ge(s, v)
                        if fn is not None:
                            fn(e).then_inc(sem, inc)
                return body

            block.sync(mk("sp"))
            block.tensor(mk("pe"))
            block.scalar(mk("act"))
            block.vector(mk("dve"))
            block.gpsimd(mk("pool"))


class SbAlloc:
    def __init__(self, nc, base=16512, top=229312, prefix=""):
        self.prefix = prefix
        self.nc = nc
        self.off = base
        self.top = top
        self.n = 0
        self.peak = 0

    def mark(self):
        return self.off

    def release(self, mark):
        self.off = mark

    def tile(self, shape, dtype, name=None):
        esz = {F32: 4, F32R: 4, BF16: 2, I32: 4}[dtype]
        free = 1
        for s in shape[1:]:
            free *= s
        nbytes = (free * esz + 63) // 64 * 64
        off = self.off
        self.off += nbytes
        self.peak = max(self.peak, self.off)
        assert self.off <= self.top, "SBUF overflow: %d > %d (%s)" % (self.off, self.top, name)
        self.n += 1
        nm = "%s%s_%d" % (self.prefix, name or "t", self.n)
        return self.nc.alloc_sbuf_tensor_at(nm, list(shape), dtype, offset=off)


D = 1024
SEQ = 8192
NT = 16
NB = 64
H = 8
GH = 4
EPS = 1e-6
NOWN = 2052
P3T = 108
import os
STAGE = int(os.environ.get('K_STAGE', '6'))
DBG_NT = int(os.environ.get('K_NT', '16'))
DBG = os.environ.get('K_DBG', '')

C_FQ, C_FK, C_FV, C_FF = 0, 512, 1024, 1536
C_GQ, C_GK, C_GV, C_GLR, C_GG = 1544, 1800, 2056, 2568, 2584


def own_tiles(r):
    return [r, 7 - r, 8 + r, 15 - r]


class K:
    pass


def build_program():
    nc = bass.Bass("TRN2", target_bir_lowering=False)
    S = Sched(nc)
    sb = SbAlloc(nc)
    k = K()
    k.nc, k.S, k.sb = nc, S, sb

    k.used_inputs = []

    def din(name, shape, dt=F32):
        k.used_inputs.append(name)
        return nc.dram_tensor(name, list(shape), dt, kind="ExternalInput").ap()

    def dout(name, shape, dt=F32):
        return nc.dram_tensor(name, list(shape), dt, kind="ExternalOutput").ap()

    def dscr(name, shape, dt):
        return nc.dram_tensor(name, list(shape), dt, kind="Internal").ap()

    xT_full = din("xT_full", [D, SEQ])
    xT_own = din("xT_own", [D, NOWN])
    w1 = din("w1", [D, 2072])
    w2 = din("w2", [D, 1024])
    w_o = din("w_o", [D, D])
    w_up = din("w_up", [D, 4096])
    w_down = din("w_down", [4096, D])
    gains = din("gains", [128, 40])
    bfrep = din("bfrep", [128, 8])
    wg = din("wg", [16, 256])
    masks = din("masks", [4, 16, 128, 512], BF16)
    onehot = din("onehot", [4, 128, 64])
    idx_o = din("idx_o", [128, 16], I32)
    mbias = din("mbias", [128, 64])

    NPHYS = int(os.environ.get("K_NPHYS", "5120"))
    w_s = din("w_s", [D, 3096])
    ptab = din("ptab", [128, 4], I32)
    cache_k2 = din("cache_k2", [NPHYS * 32, 2048])
    cache_v2 = din("cache_v2", [NPHYS * 32, 2048])
    cache_lf = din("cache_lf", [NPHYS, 1024])
    sgla = din("sgla", [2, 128, 4, 128])
    o_ks = dout("o_ks", [4, 512]); o_vs = dout("o_vs", [4, 512]); o_lfs = dout("o_lfs", [4, 8])
    o_glas = dout("o_glas", [2, 128, 4, 128])
    o_y = dout("o_y", [D, NOWN])
    o_k = dout("o_k", [512, SEQ])
    o_v = dout("o_v", [SEQ, 512])
    o_lf = dout("o_lf", [128, NB * H])
    o_gla = dout("o_gla", [256, 128])

    kT_d = dscr("kT_d", [512, SEQ], BF16)
    v_d = dscr("v_d", [H, 128, NB, 65], BF16)
    og_d = dscr("og_d", [NT * GH * 128, 512], BF16)

    def cbuf(name):
        return Buf(name)

    ones_bf = sb.tile([128, 128], BF16, "ones_bf"); B_ones_bf = cbuf("ones_bf")
    ones_f = sb.tile([128, 128], F32, "ones_f"); B_ones_f = cbuf("ones_f")
    U_f = sb.tile([128, 128], F32, "U_f"); B_U = cbuf("U")
    tri_bf = sb.tile([128, 128], BF16, "tri_bf"); B_tri = cbuf("tri")
    gn = sb.tile([128, 40], F32, "gains"); B_gn = cbuf("gains")
    bfr = sb.tile([128, 8], F32, "bfr"); B_bfr = cbuf("bfr")
    wg_f = sb.tile([16, 256], F32, "wg_f"); wg_bf = sb.tile([16, 256], BF16, "wg_bf"); B_wg = cbuf("wg")
    lf_tab = sb.tile([128, H, NB], F32, "lf_tab"); B_lf = cbuf("lf")
    C_tab = sb.tile([128, H, NB], F32, "C_tab"); B_C = cbuf("C")

    S.op("pool", lambda e: e.memset(ones_bf[:], 1.0), writes=[B_ones_bf])
    S.op("pool", lambda e: e.memset(ones_f[:], 1.0), writes=[B_ones_f])
    S.op("pool", lambda e: e.memset(U_f[:], 1.0), writes=[B_U])
    S.op("pool", lambda e: e.affine_select(out=U_f[:], in_=U_f[:], pattern=[[1, 128]], compare_op=ALU.is_ge,
                                           fill=0.0, base=0, channel_multiplier=-1), writes=[B_U])
    S.op("pool", lambda e: e.tensor_copy(out=tri_bf[:], in_=U_f[:]), reads=[B_U], writes=[B_tri])
    S.dma("sp", lambda e: e.dma_start(out=gn[:], in_=gains), writes=[B_gn])
    S.dma("sp", lambda e: e.dma_start(out=bfr[:], in_=bfrep), writes=[B_bfr])
    S.dma("sp", lambda e: e.dma_start(out=wg_f[:], in_=wg), writes=[B_wg])
    S.op("dve", lambda e: e.tensor_copy(out=wg_bf[:], in_=wg_f[:]), reads=[B_wg], writes=[B_wg])

    ps = [nc.alloc_psum_tensor("ps%d" % i, [128, 512], F32) for i in range(7)]
    Bps = [Buf("ps%d" % i, excl=True) for i in range(7)]
    psT_bf = nc.alloc_psum_tensor("psT_bf", [128, 1024], BF16); B_psT = Buf("psT", excl=True)
    ident_bf = sb.tile([128, 128], BF16, "ident_bf"); B_ident = Buf("ident")
    ident_f = sb.tile([128, 128], F32, "ident_f")
    S.op("pool", lambda e: e.affine_select(out=ident_f[:], in_=U_f[:], pattern=[[-1, 128]], compare_op=ALU.is_ge,
                                           fill=0.0, base=0, channel_multiplier=1), reads=[B_U], writes=[B_ident])
    S.op("pool", lambda e: e.tensor_copy(out=ident_bf[:], in_=ident_f[:]), reads=[B_ident], writes=[B_ident])

    out_bufs = []

    def rms_rstd(xt, B_xt, n, sq, B_sq, rstd, B_rstd, pbank, nfeat=8, dim=1024.0):
        S.op("act", lambda e: e.activation(out=sq[:, 0:nfeat, 0:n], in_=xt[:, 0:nfeat, 0:n], func=AF.Square),
             reads=[B_xt], writes=[B_sq])
        for kc in range(nfeat):
            S.op("pe", lambda e, kc=kc: e.matmul(ps[pbank][:, 0:n], lhsT=ones_bf[:], rhs=sq[:, kc, 0:n],
                                                  start=(kc == 0), stop=(kc == nfeat - 1)),
                 reads=[B_sq, B_ones_bf], writes=[Bps[pbank]])
        S.op("act", lambda e: e.activation(out=rstd[:, 0:n], in_=ps[pbank][:, 0:n], func=AF.Ln, scale=1.0 / dim, bias=EPS),
             reads=[Bps[pbank]], writes=[B_rstd])
        S.op("act", lambda e: e.activation(out=rstd[:, 0:n], in_=rstd[:, 0:n], func=AF.Exp, scale=-0.5),
             reads=[B_rstd], writes=[B_rstd])

    m_p1 = sb.mark()
    rmask = sb.tile([128, 512], F32, "rmask"); B_rmask = cbuf("rmask")
    S.op("pool", lambda e: e.memset(rmask[:], 1.0), writes=[B_rmask])
    S.op("pool", lambda e: e.memset(rmask[:].rearrange("p (c t) -> p c t", t=128)[:, :, 0:1], 0.0), writes=[B_rmask])
    w1_bf = sb.tile([128, 8, 2072], BF16, "w1_bf"); B_w1 = Buf("w1")
    wst = [sb.tile([128, 2072], F32, "wst%d" % i) for i in range(2)]; B_wst = [Buf("wst0"), Buf("wst1")]
    for kc in range(8):
        i = kc % 2
        S.dma("sp", lambda e, kc=kc, i=i: e.dma_start(out=wst[i][:], in_=w1[kc * 128:(kc + 1) * 128, :]), writes=[B_wst[i]])
        eng = "dve" if kc % 2 == 0 else "pool"
        S.op(eng, lambda e, kc=kc, i=i: e.tensor_copy(out=w1_bf[:, kc, :], in_=wst[i][:]), reads=[B_wst[i]], writes=[B_w1])
    W_FK, W_GQ, W_GK, W_GLR, W_FV, W_GV, W_FF = 0, 512, 768, 1024, 1040, 1552, 2064

    xt = [sb.tile([128, 8, 512], F32, "xt%d" % i) for i in range(2)]; B_xt = [Buf("xt0"), Buf("xt1")]
    sq = sb.tile([128, 8, 512], BF16, "sq"); B_sq = Buf("sq")
    xn = sb.tile([128, 8, 512], BF16, "xn"); B_xn = Buf("xn")
    rstd = sb.tile([128, 512], F32, "rstd"); B_rstd = Buf("rstd")
    kst = [sb.tile([128, 512], F32, "kst%d" % i) for i in range(2)]; B_kst = [Buf("kst0"), Buf("kst1")]
    kbf = [sb.tile([128, 512], F32 if "F" in DBG else BF16, "kbf%d" % i) for i in range(2)]; B_kbf = [Buf("kbf0"), Buf("kbf1")]
    vst = [sb.tile([128, 512], F32, "vst%d" % i) for i in range(2)]; B_vst = [Buf("vst0"), Buf("vst1")]
    vaug = [sb.tile([128, H, 4, 65], BF16, "vaug%d" % i) for i in range(2)]; B_vaug = [Buf("vaug0"), Buf("vaug1")]
    for i in range(2):
        S.op("pool", lambda e, i=i: e.memset(vaug[i][:], 1.0), writes=[B_vaug[i]])
    zff = sb.tile([128, 8], F32, "zff"); B_zff = Buf("zff")
    gq_f = [sb.tile([128, 512], F32, "gq%d" % i) for i in range(2)]; B_gq = [Buf("gq0"), Buf("gq1")]
    gk_f = [sb.tile([128, 512], F32, "gk%d" % i) for i in range(2)]; B_gk = [Buf("gk0"), Buf("gk1")]
    glr_bf = sb.tile([16, 512], BF16, "glr"); B_glr = Buf("glr")
    Lb = [sb.tile([128, 512], F32, "L%d" % i) for i in range(2)]; B_L = [Buf("L0"), Buf("L1")]
    Bc = [sb.tile([128, 512], F32, "Bc%d" % i) for i in range(2)]; B_Bc = [Buf("Bc0"), Buf("Bc1")]
    dd = [sb.tile([128, 512], F32, "dd%d" % i) for i in range(2)]; B_dd = [Buf("dd0"), Buf("dd1")]
    ee = [sb.tile([128, 512], F32, "ee%d" % i) for i in range(2)]; B_ee = [Buf("ee0"), Buf("ee1")]
    e3 = [sb.tile([128, 512], F32, "e3%d" % i) for i in range(2)]; B_e3 = [Buf("e30"), Buf("e31")]
    qt_ = [sb.tile([128, 512], BF16, "qt%d" % i) for i in range(2)]; B_qt = [Buf("qt0"), Buf("qt1")]
    kt_ = [sb.tile([128, 512], BF16, "kt%d" % i) for i in range(2)]; B_kt = [Buf("kt0"), Buf("kt1")]
    qh_ = [sb.tile([128, 512], BF16, "qh%d" % i) for i in range(2)]; B_qh = [Buf("qh0"), Buf("qh1")]
    kh_ = [sb.tile([128, 512], BF16, "kh%d" % i) for i in range(2)]; B_kh = [Buf("kh0"), Buf("kh1")]
    gvb = [sb.tile([128, 512], BF16, "gvb%d" % i) for i in range(4)]; B_gvb = [Buf("gvb%d" % i) for i in range(4)]
    khT = [sb.tile([128, 128], BF16, "khT%d" % i) for i in range(2)]; B_khT = [Buf("khT0"), Buf("khT1")]
    AT = [sb.tile([128, 128], BF16, "AT%d" % i) for i in range(2)]; B_AT = [Buf("AT0"), Buf("AT1")]
    ost = [sb.tile([128, 512], BF16, "ost%d" % i) for i in range(4)]; B_ost = [Buf("ost%d" % i) for i in range(4)]
    Sst = [sb.tile([128, 128], F32, "Sst%d" % i) for i in range(2)]; B_Sst = [Buf("Sst0"), Buf("Sst1")]
    Sbf = [sb.tile([128, 128], BF16, "Sbf%d" % i) for i in range(2)]; B_Sbf = [Buf("Sbf0"), Buf("Sbf1")]
    for i in range(2):
        S.op("pool", lambda e, i=i: e.memset(Sst[i][:], 0.0), writes=[B_Sst[i]])
        S.op("pool", lambda e, i=i: e.memset(Sbf[i][:], 0.0), writes=[B_Sbf[i]])

    B_kTd = Buf("kT_d"); B_vd = Buf("v_d"); B_ogd = Buf("og_d")
    B_ok = Buf("o_k"); B_ov = Buf("o_v"); B_olf = Buf("o_lf"); B_ogla = Buf("o_gla"); B_oy = Buf("o_y")
    out_bufs += [B_ok, B_ov, B_olf, B_ogla, B_oy]
    xT_full_v = xT_full.rearrange("(kc p) t -> p kc t", p=128)

    DO_GLA = STAGE >= 2

    def load_x(ti):
        i = ti % 2
        S.dma("sp", lambda e: e.dma_start(out=xt[i][:], in_=xT_full_v[:, :, ti * 512:(ti + 1) * 512]), writes=[B_xt[i]])

    load_x(0)
    for ti in range(DBG_NT):
        i = ti % 2
        if ti + 1 < DBG_NT:
            load_x(ti + 1)
        t0 = ti * 512
        if 'r' not in DBG:
            rms_rstd(xt[i], B_xt[i], 512, sq, B_sq, rstd, B_rstd, 0)
        else:
            S.op('pool', lambda e: e.memset(rstd[:], 1.0), writes=[B_rstd])
        for kc in range(0 if 'n' in DBG else 8):
            S.op("dve", lambda e, kc=kc: e.scalar_tensor_tensor(out=xn[:, kc, :], in0=xt[i][:, kc, :], scalar=gn[:, kc:kc + 1],
                                                                in1=rstd[:], op0=ALU.mult, op1=ALU.mult),
                 reads=[B_xt[i], B_gn, B_rstd], writes=[B_xn])
        for p in range(0 if 'a' in DBG else 4):
            pb = 1 + (p % 2)
            for kc in range(8):
                S.op("pe", lambda e, kc=kc, p=p, pb=pb: e.matmul(ps[pb][:, :], lhsT=w1_bf[:, kc, W_FK + p * 128:W_FK + (p + 1) * 128],
                                                                  rhs=xn[:, kc, :], start=(kc == 0), stop=(kc == 7)),
                     reads=[B_w1, B_xn], writes=[Bps[pb]])
            j = p % 2
            if 'e' in DBG:
                continue
            S.op("act", lambda e, pb=pb, j=j: e.copy(out=kst[j][:], in_=ps[pb][:, :]), reads=[Bps[pb]], writes=[B_kst[j]])
            if 'f' not in DBG:
                S.op("dve", lambda e, pb=pb, j=j: e.tensor_copy(out=kbf[j][:], in_=ps[pb][:, :]), reads=[Bps[pb]], writes=[B_kbf[j]])
            if 'g' not in DBG:
                S.dma("sp", lambda e, p=p, j=j: e.dma_start(out=o_k[p * 128:(p + 1) * 128, t0:t0 + 512], in_=kst[j][:]),
                      reads=[B_kst[j]], writes=[B_ok])
            if 'k' not in DBG:
                S.dma("sp", lambda e, p=p, j=j: e.dma_start(out=kT_d[p * 128:(p + 1) * 128, t0:t0 + 512], in_=kbf[j][:]),
                      reads=[B_kbf[j]], writes=[B_kTd])
        vi = ti % 2
        for blk in range(0 if 'b' in DBG else 4):
            pb = 3 + (blk % 2)
            for kc in range(8):
                S.op("pe", lambda e, kc=kc, blk=blk, pb=pb: e.matmul(ps[pb][:, :], lhsT=xn[:, kc, blk * 128:(blk + 1) * 128],
                                                                      rhs=w1_bf[:, kc, W_FV:W_FV + 512], start=(kc == 0), stop=(kc == 7)),
                     reads=[B_w1, B_xn], writes=[Bps[pb]])
            j = blk % 2
            S.op("act", lambda e, pb=pb, j=j: e.copy(out=vst[j][:], in_=ps[pb][:, :]), reads=[Bps[pb]], writes=[B_vst[j]])
            S.op("dve", lambda e, pb=pb, blk=blk: e.tensor_copy(out=vaug[vi][:, :, blk, 0:64],
                                                                in_=ps[pb][:, :].rearrange("p (h d) -> p h d", d=64)),
                 reads=[Bps[pb]], writes=[B_vaug[vi]])
            S.dma("sp", lambda e, blk=blk, j=j: e.dma_start(out=o_v[t0 + blk * 128:t0 + (blk + 1) * 128, :], in_=vst[j][:]),
                  reads=[B_vst[j]], writes=[B_ov])
            for kc in range(8):
                S.op("pe", lambda e, kc=kc, blk=blk: e.matmul(ps[5][:, 0:8], lhsT=xn[:, kc, blk * 128:(blk + 1) * 128],
                                                              rhs=w1_bf[:, kc, W_FF:W_FF + 8], start=(kc == 0), stop=(kc == 7)),
                     reads=[B_w1, B_xn], writes=[Bps[5]])
            nb = ti * 4 + blk
            S.op("dve", lambda e: e.tensor_tensor(out=zff[:], in0=ps[5][:, 0:8], in1=bfr[:], op=ALU.add),
                 reads=[Bps[5], B_bfr], writes=[B_zff])
            S.op("act", lambda e: e.activation(out=zff[:], in_=zff[:], func=AF.Exp, scale=-1.0), reads=[B_zff], writes=[B_zff])
            S.op("act", lambda e: e.activation(out=zff[:], in_=zff[:], func=AF.Ln, bias=1.0), reads=[B_zff], writes=[B_zff])
            S.op("dve", lambda e, nb=nb: e.tensor_scalar(out=lf_tab[:, :, nb], in0=zff[:], scalar1=-1.0, scalar2=None, op0=ALU.mult),
                 reads=[B_zff], writes=[B_lf])
        for h in range(0 if 'c' in DBG else H):
            S.dma("sp", lambda e, h=h: e.dma_start(out=v_d[h, :, ti * 4:(ti + 1) * 4, :], in_=vaug[vi][:, h, :, :]),
                  reads=[B_vaug[vi]], writes=[B_vd])
        if not DO_GLA:
            continue
        for kc in range(8):
            S.op("pe", lambda e, kc=kc: e.matmul(ps[5][0:16, :], lhsT=w1_bf[:, kc, W_GLR:W_GLR + 16], rhs=xn[:, kc, :],
                                                  start=(kc == 0), stop=(kc == 7)), reads=[B_w1, B_xn], writes=[Bps[5]])
        S.op("act", lambda e: e.copy(out=glr_bf[:], in_=ps[5][0:16, :]), reads=[Bps[5]], writes=[B_glr])
        for p in range(2):
            for which, (wofs, dst, Bd) in enumerate([(W_GQ, gq_f, B_gq), (W_GK, gk_f, B_gk)]):
                pb = 1 + which
                for kc in range(8):
                    S.op("pe", lambda e, kc=kc, p=p, wofs=wofs, pb=pb: e.matmul(ps[pb][:, :], lhsT=w1_bf[:, kc, wofs + p * 128:wofs + (p + 1) * 128],
                                                                                 rhs=xn[:, kc, :], start=(kc == 0), stop=(kc == 7)),
                         reads=[B_w1, B_xn], writes=[Bps[pb]])
                S.op("act", lambda e, pb=pb, dst=dst, p=p: e.copy(out=dst[p][:], in_=ps[pb][:, :]), reads=[Bps[pb]], writes=[Bd[p]])
            S.op("pe", lambda e, p=p: e.matmul(ps[5][:, :], lhsT=wg_bf[:, p * 128:(p + 1) * 128], rhs=glr_bf[:], start=True, stop=True),
                 reads=[B_wg, B_glr], writes=[Bps[5]])
            S.op("act", lambda e, p=p: e.activation(out=Lb[p][:], in_=ps[5][:, :], func=AF.Exp, scale=-1.0, bias=gn[:, 25 + p:26 + p]),
                 reads=[Bps[5], B_gn], writes=[B_L[p]])
            S.op("act", lambda e, p=p: e.activation(out=Lb[p][:], in_=Lb[p][:], func=AF.Ln, bias=1.0), reads=[B_L[p]], writes=[B_L[p]])
            S.op("dve", lambda e, p=p: e.tensor_tensor_scan(out=Bc[p][:], data0=rmask[:], data1=Lb[p][:], initial=0.0,
                                                            op0=ALU.mult, op1=ALU.add), reads=[B_rmask, B_L[p]], writes=[B_Bc[p]])
            Bc3 = Bc[p][:].rearrange("p (c t) -> p c t", t=128)
            dd3 = dd[p][:].rearrange("p (c t) -> p c t", t=128)
            S.op("dve", lambda e, Bc3=Bc3, dd3=dd3: e.tensor_tensor(out=dd3, in0=Bc3, in1=Bc3[:, :, 64:65].to_broadcast([128, 4, 128]),
                                                                    op=ALU.subtract), reads=[B_Bc[p]], writes=[B_dd[p]])
            S.op("act", lambda e, p=p: e.activation(out=ee[p][:], in_=dd[p][:], func=AF.Exp, scale=-1.0 / 16), reads=[B_dd[p]], writes=[B_ee[p]])
            S.op("dve", lambda e, p=p: e.scalar_tensor_tensor(out=qt_[p][:], in0=gq_f[p][:], scalar=0.125, in1=ee[p][:], op0=ALU.mult, op1=ALU.mult),
                 reads=[B_gq[p], B_ee[p]], writes=[B_qt[p]])
            S.op("act", lambda e, p=p: e.activation(out=ee[p][:], in_=dd[p][:], func=AF.Exp, scale=1.0 / 16), reads=[B_dd[p]], writes=[B_ee[p]])
            S.op("dve", lambda e, p=p: e.tensor_tensor(out=kt_[p][:], in0=gk_f[p][:], in1=ee[p][:], op=ALU.mult),
                 reads=[B_gk[p], B_ee[p]], writes=[B_kt[p]])
            S.op("act", lambda e, p=p: e.activation(out=e3[p][:], in_=Bc[p][:], func=AF.Exp, scale=-1.0 / 16), reads=[B_Bc[p]], writes=[B_e3[p]])
            S.op("dve", lambda e, p=p: e.scalar_tensor_tensor(out=qh_[p][:], in0=gq_f[p][:], scalar=0.125, in1=e3[p][:], op0=ALU.mult, op1=ALU.mult),
                 reads=[B_gq[p], B_e3[p]], writes=[B_qh[p]])
            S.op("dve", lambda e, Bc3=Bc3, dd3=dd3: e.tensor_tensor(out=dd3, in0=Bc3, in1=Bc3[:, :, 127:128].to_broadcast([128, 4, 128]),
                                                                    op=ALU.subtract), reads=[B_Bc[p]], writes=[B_dd[p]])
            S.op("act", lambda e, p=p: e.activation(out=ee[p][:], in_=dd[p][:], func=AF.Exp, scale=1.0 / 16), reads=[B_dd[p]], writes=[B_ee[p]])
            S.op("dve", lambda e, p=p: e.tensor_tensor(out=kh_[p][:], in0=gk_f[p][:], in1=ee[p][:], op=ALU.mult),
                 reads=[B_gk[p], B_ee[p]], writes=[B_kh[p]])
        for blk in range(4):
            pb = 3 + (blk % 2)
            for kc in range(8):
                S.op("pe", lambda e, kc=kc, blk=blk, pb=pb: e.matmul(ps[pb][:, :], lhsT=xn[:, kc, blk * 128:(blk + 1) * 128],
                                                                      rhs=w1_bf[:, kc, W_GV:W_GV + 512], start=(kc == 0), stop=(kc == 7)),
                     reads=[B_w1, B_xn], writes=[Bps[pb]])
            S.op("act", lambda e, pb=pb, blk=blk: e.copy(out=gvb[blk][:], in_=ps[pb][:, :]), reads=[Bps[pb]], writes=[B_gvb[blk]])
        for c in range(4):
            cs = slice(c * 128, (c + 1) * 128)
            for p in range(2):
                S.op("pe", lambda e, p=p, cs=cs: e.transpose(out=psT_bf[:, 0:128], in_=kh_[p][:, cs], identity=ident_bf[:]),
                     reads=[B_kh[p], B_ident], writes=[B_psT])
                S.op("act", lambda e, p=p: e.copy(out=khT[p][:], in_=psT_bf[:, 0:128]), reads=[B_psT], writes=[B_khT[p]])
                for hh in range(2):
                    h = p * 2 + hh
                    rs = slice(hh * 64, (hh + 1) * 64)
                    S.op("pe", lambda e, p=p, rs=rs, cs=cs: e.matmul(ps[6][:, 0:128], lhsT=kt_[p][rs, cs], rhs=qt_[p][rs, cs], start=True, stop=True),
                         reads=[B_kt[p], B_qt[p]], writes=[Bps[6]])
                    S.op("dve", lambda e, hh=hh: e.tensor_tensor(out=AT[hh][:], in0=ps[6][:, 0:128], in1=tri_bf[:], op=ALU.mult),
                         reads=[Bps[6], B_tri], writes=[B_AT[hh]])
                    S.op("pe", lambda e, h=h, hh=hh, c=c: e.matmul(ps[2][:, 0:128], lhsT=gvb[c][:, h * 128:(h + 1) * 128], rhs=AT[hh][:], start=True, stop=False),
                         reads=[B_gvb[c], B_AT[hh]], writes=[Bps[2]])
                    S.op("pe", lambda e, p=p, rs=rs, cs=cs: e.matmul(ps[2][:, 0:128], lhsT=Sbf[p][rs, :], rhs=qh_[p][rs, cs], start=False, stop=True),
                         reads=[B_Sbf[p], B_qh[p]], writes=[Bps[2]])
                    S.op("act", lambda e, h=h, cs=cs: e.copy(out=ost[h][:, cs], in_=ps[2][:, 0:128]), reads=[Bps[2]], writes=[B_ost[h]])
                for hh in range(2):
                    h = p * 2 + hh
                    S.op("pe", lambda e, p=p, hh=hh, h=h, c=c: e.matmul(ps[5][hh * 64:(hh + 1) * 64, 0:128], lhsT=khT[p][:, hh * 64:(hh + 1) * 64],
                                                                         rhs=gvb[c][:, h * 128:(h + 1) * 128], start=True, stop=True),
                         reads=[B_khT[p], B_gvb[c]], writes=[Bps[5]])
                S.op("dve", lambda e, p=p, c=c: e.scalar_tensor_tensor(out=Sst[p][:], in0=Sst[p][:], scalar=e3[p][:, c * 128 + 127:c * 128 + 128],
                                                                      in1=ps[5][:, 0:128], op0=ALU.mult, op1=ALU.add),
                     reads=[B_Sst[p], B_e3[p], Bps[5]], writes=[B_Sst[p]])
                S.op("act", lambda e, p=p: e.copy(out=Sbf[p][:], in_=Sst[p][:]), reads=[B_Sst[p]], writes=[B_Sbf[p]])
        for h in range(GH):
            S.dma("sp", lambda e, h=h: e.dma_start(out=og_d[(ti * GH + h) * 128:(ti * GH + h + 1) * 128, :], in_=ost[h][:]),
                  reads=[B_ost[h]], writes=[B_ogd])

    if DO_GLA:
        for p in range(2):
            S.dma("sp", lambda e, p=p: e.dma_start(out=o_gla[p * 128:(p + 1) * 128, :], in_=Sst[p][:]), reads=[B_Sst[p]], writes=[B_ogla])
    S.dma("sp", lambda e: e.dma_start(out=o_lf, in_=lf_tab[:].rearrange("p h n -> p (h n)")), reads=[B_lf], writes=[B_olf])


    if STAGE >= 3:
        S.barrier()
        sb.release(m_p1)
        mix = sb.tile([128, 8, NOWN], BF16, "mix"); B_mix = Buf("mix")
        S.op("pool", lambda e: e.memset(mix[:], 0.0), writes=[B_mix])
        m_p2 = sb.mark()
        xT_own_v = xT_own.rearrange("(kc p) t -> p kc t", p=128)
        if STAGE >= 6:
            B_oks, B_ovs, B_olfs, B_oglas = Buf("o_ks"), Buf("o_vs"), Buf("o_lfs"), Buf("o_glas")
            out_bufs += [B_oks, B_ovs, B_olfs, B_oglas]
            zsT = sb.tile([128, 25, 4], F32, "zsT"); B_zsT = Buf("zsT")
            xs = sb.tile([128, 8, 4], F32, "xs"); B_xs = Buf("xs")
            sqs = sb.tile([128, 8, 4], BF16, "sqs"); B_sqs = Buf("sqs")
            rss = sb.tile([128, 4], F32, "rss"); B_rss = Buf("rss")
            xns = sb.tile([128, 8, 4], F32, "xns"); B_xns = Buf("xns")
            fq_tm = sb.tile([4, 512], F32, "fq_tm"); B_fq_tm = Buf("fq_tm")
            fk_tm = sb.tile([4, 512], F32, "fk_tm"); B_fk_tm = Buf("fk_tm")
            fv_tm = sb.tile([4, 512], F32, "fv_tm"); B_fv_tm = Buf("fv_tm")
            gv_tm = sb.tile([4, 512], F32, "gv_tm"); B_gv_tm = Buf("gv_tm")
            lfs_tm = sb.tile([4, 8], F32, "lfs_tm"); B_lfs = Buf("lfs_tm")
            sel4 = sb.tile([4, 4, 128], F32, "sel4"); B_sel4 = Buf("sel4")
            bm = sb.tile([8, 8, 64], F32, "bm"); B_bm = Buf("bm")
            SL_f = sb.tile([128, 128], F32, "SL_f"); B_SL = Buf("SL")
            m128 = sb.tile([128, 1024], F32, "m128"); B_m128 = Buf("m128")
            pt_t = sb.tile([128, 4], I32, "pt_t"); B_pt = Buf("pt")
            Otm = sb.tile([4, 512], F32, "Otm"); B_Otm = Buf("Otm")
            dentm = sb.tile([4, 8], F32, "dentm"); B_dentm = Buf("dentm")
            S.op("dve", lambda e: e.tensor_copy(out=sel4[:], in_=ident_f[0:4, 0:4].unsqueeze(2).to_broadcast([4, 4, 128])), reads=[B_ident], writes=[B_sel4])
            S.op("dve", lambda e: e.tensor_copy(out=bm[:], in_=ident_f[0:8, 0:8].unsqueeze(2).to_broadcast([8, 8, 64])), reads=[B_ident], writes=[B_bm])
            S.op("dve", lambda e: e.tensor_tensor(out=SL_f[:], in0=ones_f[:], in1=U_f[:], op=ALU.subtract), reads=[B_ones_f, B_U], writes=[B_SL])
            S.op("pool", lambda e: e.memset(m128[:], 1.0), writes=[B_m128])
            S.op("pool", lambda e: e.memset(m128[:].rearrange("p (h i) -> p h i", i=128)[:, :, 0:1], 0.0), writes=[B_m128])
            S.dma("sp", lambda e: e.dma_start(out=pt_t[:], in_=ptab), writes=[B_pt])
            TOPBASE = 229312 - 100 * 1024
            sbt = SbAlloc(nc, base=TOPBASE, top=229312, prefix="T")
            win_f = sbt.tile([128, 8, 3096], F32, "win_f"); B_win = Buf("win_f")
            for kc in range(8):
                S.dma("sp", lambda e: e.dma_start(out=win_f[:, kc, :], in_=w_s[kc * 128:(kc + 1) * 128, :]), writes=[B_win])
            S.dma("sp", lambda e: e.dma_start(out=xs[:], in_=xT_own_v[:, :, 2048:2052]), writes=[B_xs])
            rms_rstd(xs, B_xs, 4, sqs, B_sqs, rss, B_rss, 0)
            for kc in range(8):
                S.op("dve", lambda e: e.scalar_tensor_tensor(out=xns[:, kc, :], in0=xs[:, kc, :], scalar=gn[:, kc:kc + 1], in1=rss[:],
                                                             op0=ALU.mult, op1=ALU.mult), reads=[B_xs, B_gn, B_rss], writes=[B_xns])
            for mc in range(25):
                mcols = 128 if mc < 24 else 24
                for kc in range(8):
                    S.op("pe", lambda e: e.matmul(ps[0][0:mcols, mc * 4:(mc + 1) * 4], lhsT=win_f[:, kc, mc * 128:mc * 128 + mcols], rhs=xns[:, kc, :],
                                                  start=(kc == 0), stop=(kc == 7)), reads=[B_win, B_xns], writes=[Bps[0]])
            S.op("act", lambda e: e.copy(out=zsT[:, 0:24, :], in_=ps[0][:, 0:96].rearrange("p (m t) -> p m t", t=4)), reads=[Bps[0]], writes=[B_zsT])
            S.op("act", lambda e: e.copy(out=zsT[0:24, 24, :], in_=ps[0][0:24, 96:100]), reads=[Bps[0]], writes=[B_zsT])
            def proj_tm(c0, n, dst, Bd, pb):
                for kc in range(8):
                    S.op("pe", lambda e: e.matmul(ps[pb][0:4, 0:n], lhsT=xns[:, kc, :], rhs=win_f[:, kc, c0:c0 + n], start=(kc == 0), stop=(kc == 7)),
                         reads=[B_win, B_xns], writes=[Bps[pb]])
                if dst is not None:
                    S.op("act", lambda e: e.copy(out=dst[:, 0:n], in_=ps[pb][0:4, 0:n]), reads=[Bps[pb]], writes=[Bd])
            proj_tm(0, 512, fq_tm, B_fq_tm, 1)
            proj_tm(512, 512, fk_tm, B_fk_tm, 2)
            proj_tm(1024, 512, fv_tm, B_fv_tm, 1)
            proj_tm(2048, 512, gv_tm, B_gv_tm, 2)
            proj_tm(3088, 8, None, None, 1)
            S.op("dve", lambda e: e.tensor_tensor(out=lfs_tm[:], in0=ps[1][0:4, 0:8], in1=bfr[0:4, :], op=ALU.add), reads=[Bps[1], B_bfr], writes=[B_lfs])
            S.op("act", lambda e: e.activation(out=lfs_tm[:], in_=lfs_tm[:], func=AF.Exp, scale=-1.0), reads=[B_lfs], writes=[B_lfs])
            S.op("act", lambda e: e.activation(out=lfs_tm[:], in_=lfs_tm[:], func=AF.Ln, bias=1.0), reads=[B_lfs], writes=[B_lfs])
            S.op("dve", lambda e: e.tensor_scalar(out=lfs_tm[:], in0=lfs_tm[:], scalar1=-1.0, scalar2=None, op0=ALU.mult), reads=[B_lfs], writes=[B_lfs])
            S.dma("sp", lambda e: e.dma_start(out=o_ks, in_=fk_tm[:]), reads=[B_fk_tm], writes=[B_oks])
            S.dma("sp", lambda e: e.dma_start(out=o_vs, in_=fv_tm[:]), reads=[B_fv_tm], writes=[B_ovs])
            S.dma("sp", lambda e: e.dma_start(out=o_lfs, in_=lfs_tm[:]), reads=[B_lfs], writes=[B_olfs])
            Sin = sb.tile([128, 4, 128], F32, "Sin"); B_Sin = Buf("Sin")
            Snew = sb.tile([128, 4, 128], F32, "Snew"); B_Snew = Buf("Snew")
            alT = sb.tile([128, 4], F32, "alT"); B_alT = Buf("alT")
            kvt = sb.tile([128, 128], F32, "kvt"); B_kvt = Buf("kvt")
            ogs = sb.tile([128, 1, 16], F32, "ogs"); B_ogs = Buf("ogs")
            for p in range(2):
                S.dma("sp", lambda e: e.dma_start(out=Sin[:], in_=sgla[p]), writes=[B_Sin])
                S.op("pe", lambda e: e.matmul(ps[5][:, 0:4], lhsT=wg_f[:, p * 128:(p + 1) * 128],
                                              rhs=zsT[0:16, 24, :], start=True, stop=True), reads=[B_wg, B_zsT], writes=[Bps[5]])
                S.op("act", lambda e: e.activation(out=alT[:], in_=ps[5][:, 0:4], func=AF.Exp, scale=-1.0, bias=gn[:, 25 + p:26 + p]),
                     reads=[Bps[5], B_gn], writes=[B_alT])
                S.op("act", lambda e: e.activation(out=alT[:], in_=alT[:], func=AF.Ln, bias=1.0), reads=[B_alT], writes=[B_alT])
                S.op("act", lambda e: e.activation(out=alT[:], in_=alT[:], func=AF.Exp, scale=-1.0 / 16), reads=[B_alT], writes=[B_alT])
                for bb in range(4):
                    for hh in range(2):
                        h = 2 * p + hh
                        S.op("pe", lambda e: e.matmul(ps[4][hh * 64:(hh + 1) * 64, 0:128], lhsT=sel4[:, bb, 0:64], rhs=gv_tm[:, h * 128:(h + 1) * 128],
                                                      start=True, stop=True), reads=[B_sel4, B_gv_tm], writes=[Bps[4]])
                    S.op("dve", lambda e: e.tensor_scalar(out=kvt[:], in0=ps[4][:, 0:128], scalar1=zsT[:, 14 + p, bb:bb + 1], scalar2=None, op0=ALU.mult),
                         reads=[Bps[4], B_zsT], writes=[B_kvt])
                    S.op("dve", lambda e: e.scalar_tensor_tensor(out=Snew[:, bb, :], in0=Sin[:, bb, :], scalar=alT[:, bb:bb + 1], in1=kvt[:],
                                                                 op0=ALU.mult, op1=ALU.add), reads=[B_Sin, B_alT, B_kvt], writes=[B_Snew])
                    for hh in range(2):
                        h = 2 * p + hh
                        rs = slice(hh * 64, (hh + 1) * 64)
                        S.op("pe", lambda e: e.matmul(ps[6][:, h * 4 + bb:h * 4 + bb + 1], lhsT=Snew[rs, bb, :], rhs=zsT[rs, 12 + p, bb:bb + 1],
                                                      start=True, stop=True), reads=[B_Snew, B_zsT], writes=[Bps[6]])
                S.dma("sp", lambda e: e.dma_start(out=o_glas[p], in_=Snew[:]), reads=[B_Snew], writes=[B_oglas])
            S.op("act", lambda e: e.mul(out=ogs[:, 0, :], in_=ps[6][:, 0:16], mul=0.125), reads=[Bps[6]], writes=[B_ogs])
            ogq = sb.tile([128, 1, 16], BF16, "ogq"); B_ogq = Buf("ogq")
            rso = sb.tile([128, 16], F32, "rso"); B_rso = Buf("rso")
            sgs = sb.tile([128, 4, 4], F32, "sgs"); B_sgs = Buf("sgs")
            t1s = sb.tile([128, 4, 4], F32, "t1s"); B_t1s = Buf("t1s")
            rms_rstd(ogs, B_ogs, 16, ogq, B_ogq, rso, B_rso, 5, nfeat=1, dim=128.0)
            S.op("act", lambda e: e.activation(out=sgs[:], in_=zsT[:, 20:24, :], func=AF.Exp, scale=-1.0), reads=[B_zsT], writes=[B_sgs])
            S.op("dve", lambda e: e.tensor_scalar(out=sgs[:], in0=sgs[:], scalar1=1.0, scalar2=None, op0=ALU.add), reads=[B_sgs], writes=[B_sgs])
            S.op("dve", lambda e: e.reciprocal(out=sgs[:], in_=sgs[:]), reads=[B_sgs], writes=[B_sgs])
            S.op("dve", lambda e: e.tensor_tensor(out=sgs[:], in0=zsT[:, 20:24, :], in1=sgs[:], op=ALU.mult), reads=[B_zsT, B_sgs], writes=[B_sgs])
            S.op("dve", lambda e: e.scalar_tensor_tensor(out=t1s[:].rearrange("p h b -> p (h b)"), in0=ogs[:, 0, :], scalar=gn[:, 24:25], in1=rso[:],
                                                         op0=ALU.mult, op1=ALU.mult), reads=[B_ogs, B_gn, B_rso], writes=[B_t1s])
            S.op("dve", lambda e: e.tensor_tensor(out=mix[:, 4:8, 2048:2052], in0=t1s[:], in1=sgs[:], op=ALU.mult), reads=[B_t1s, B_sgs], writes=[B_mix])
            qk = sb.tile([4, 8, 64], F32, "qk"); B_qk = Buf("qk")
            enew = sb.tile([4, 8], F32, "enew"); B_enew = Buf("enew")
            S.op("dve", lambda e: e.tensor_tensor(out=qk[:].rearrange("p h d -> p (h d)"), in0=fq_tm[:], in1=fk_tm[:], op=ALU.mult),
                 reads=[B_fq_tm, B_fk_tm], writes=[B_qk])
            S.op("dve", lambda e: e.tensor_reduce(out=enew[:], in_=qk[:], axis=AX.X, op=ALU.add), reads=[B_qk], writes=[B_enew])
            S.op("act", lambda e: e.activation(out=enew[:], in_=enew[:], func=AF.Exp, scale=0.125), reads=[B_enew], writes=[B_enew])
            qb = sb.tile([128, 4, 512], F32, "qb"); B_qb = Buf("qb")
            for bb in range(4):
                S.op("pe", lambda e: e.matmul(ps[2][:, :], lhsT=sel4[:, bb, :], rhs=fq_tm[:], start=True, stop=True), reads=[B_sel4, B_fq_tm], writes=[Bps[2]])
                S.op("act", lambda e: e.mul(out=qb[:, bb, :], in_=ps[2][:, :], mul=0.125), reads=[Bps[2]], writes=[B_qb])
            lfb = sb.tile([128, 4, 8], F32, "lfb"); B_lfb = Buf("lfb")
            for bb in range(4):
                S.op("pe", lambda e: e.matmul(ps[2][:, 0:8], lhsT=sel4[:, bb, :], rhs=lfs_tm[:], start=True, stop=True), reads=[B_sel4, B_lfs], writes=[Bps[2]])
                S.op("act", lambda e: e.copy(out=lfb[:, bb, :], in_=ps[2][:, 0:8]), reads=[Bps[2]], writes=[B_lfb])
            S.barrier()
        m64 = sb.tile([128, 512], F32, "m64"); B_m64 = Buf("m64")
        Iscan = sb.tile([128, 512], F32, "Iscan"); B_I = Buf("I")
        S.op("pool", lambda e: e.memset(m64[:], 1.0), writes=[B_m64])
        S.op("pool", lambda e: e.memset(m64[:].rearrange("p (h n) -> p h n", n=64)[:, :, 0:1], 0.0), writes=[B_m64])
        lf2 = lf_tab[:].rearrange("p h n -> p (h n)")
        S.op("dve", lambda e: e.tensor_tensor_scan(out=Iscan[:], data0=m64[:], data1=lf2, initial=0.0, op0=ALU.mult, op1=ALU.add),
             reads=[B_m64, B_lf], writes=[B_I])
        S.op("dve", lambda e: e.tensor_tensor(out=Iscan[:], in0=Iscan[:], in1=lf2, op=ALU.subtract), reads=[B_I, B_lf], writes=[B_I])
        S.op("pe", lambda e: e.matmul(ps[0][:, :], lhsT=U_f[:], rhs=lf2, start=True, stop=False), reads=[B_U, B_lf], writes=[Bps[0]])
        S.op("pe", lambda e: e.matmul(ps[0][:, :], lhsT=ones_f[:], rhs=Iscan[:], start=False, stop=True), reads=[B_ones_f, B_I], writes=[Bps[0]])
        S.op("act", lambda e: e.copy(out=C_tab[:].rearrange("p h n -> p (h n)"), in_=ps[0][:, :]), reads=[Bps[0]], writes=[B_C])
        oh = sb.tile([128, 4, 64], F32, "oh"); B_oh = Buf("oh")
        S.dma("sp", lambda e: e.dma_start(out=oh[:], in_=onehot.rearrange("s p n -> p s n")), writes=[B_oh])
        cref = sb.tile([128, 4, H], F32, "cref"); B_cref = Buf("cref")
        ctmp = sb.tile([128, H, NB], F32, "ctmp"); B_ctmp = Buf("ctmp")
        cpart = sb.tile([128, H], F32, "cpart"); B_cpart = Buf("cpart")
        for s_ in range(4):
            S.op("dve", lambda e: e.tensor_tensor(out=ctmp[:], in0=C_tab[:], in1=oh[:, s_:s_ + 1, :].to_broadcast([128, H, NB]), op=ALU.mult),
                 reads=[B_C, B_oh], writes=[B_ctmp])
            S.op("dve", lambda e: e.tensor_reduce(out=cpart[:], in_=ctmp[:], axis=AX.X, op=ALU.add), reads=[B_ctmp], writes=[B_cpart])
            S.op("pe", lambda e: e.matmul(ps[0][:, 0:H], lhsT=ones_f[:], rhs=cpart[:], start=True, stop=True), reads=[B_ones_f, B_cpart], writes=[Bps[0]])
            S.op("act", lambda e: e.copy(out=cref[:, s_, :], in_=ps[0][:, 0:H]), reads=[Bps[0]], writes=[B_cref])
        qown = [sb.tile([128, 2048], BF16, "qown%d" % p) for p in range(4)]; B_qown = [Buf("qown%d" % p) for p in range(4)]
        m_proj = sb.mark()
        w2_bf = sb.tile([128, 8, 1024], BF16, "w2_bf"); B_w2 = Buf("w2")
        wst2 = [sb.tile([128, 1024], F32, "wst2_%d" % i) for i in range(2)]; B_wst2 = [Buf("wst2_0"), Buf("wst2_1")]
        for kc in range(8):
            i = kc % 2
            S.dma("sp", lambda e: e.dma_start(out=wst2[i][:], in_=w2[kc * 128:(kc + 1) * 128, :]), writes=[B_wst2[i]])
            S.op("dve" if i == 0 else "pool", lambda e: e.tensor_copy(out=w2_bf[:, kc, :], in_=wst2[i][:]), reads=[B_wst2[i]], writes=[B_w2])
        xo = sb.tile([128, 8, 512], F32, "xo"); B_xo = Buf("xo")
        sq2 = sb.tile([128, 8, 512], BF16, "sq2"); B_sq2 = Buf("sq2")
        xn2 = sb.tile([128, 8, 512], BF16, "xn2"); B_xn2 = Buf("xn2")
        rstd2 = sb.tile([128, 512], F32, "rstd2"); B_rstd2 = Buf("rstd2")
        og2 = [sb.tile([128, 512], BF16, "og%d" % i) for i in range(2)]; B_og2 = [Buf("og0"), Buf("og1")]
        ogsq = sb.tile([128, 1, 512], BF16, "ogsq"); B_ogsq = Buf("ogsq")
        rstdo = sb.tile([128, 512], F32, "rstdo"); B_rstdo = Buf("rstdo")
        sg = sb.tile([128, 512], F32, "sg"); B_sg = Buf("sg")
        t1 = sb.tile([128, 512], F32, "t1"); B_t1 = Buf("t1")
        idxo = sb.tile([128, 16], I32, "idxo"); B_idxo = Buf("idxo")
        S.dma("sp", lambda e: e.dma_start(out=idxo[:], in_=idx_o), writes=[B_idxo])
        for s_ in range(4):
            cs = slice(s_ * 512, (s_ + 1) * 512)
            S.dma("sp", lambda e: e.dma_start(out=xo[:], in_=xT_own_v[:, :, cs]), writes=[B_xo])
            rms_rstd(xo, B_xo, 512, sq2, B_sq2, rstd2, B_rstd2, 0)
            for kc in range(8):
                S.op("dve", lambda e: e.scalar_tensor_tensor(out=xn2[:, kc, :], in0=xo[:, kc, :], scalar=gn[:, kc:kc + 1],
                                                             in1=rstd2[:], op0=ALU.mult, op1=ALU.mult),
                     reads=[B_xo, B_gn, B_rstd2], writes=[B_xn2])
            for p in range(4):
                pb = 1 + (p % 2)
                for kc in range(8):
                    S.op("pe", lambda e: e.matmul(ps[pb][:, :], lhsT=w2_bf[:, kc, p * 128:(p + 1) * 128], rhs=xn2[:, kc, :],
                                                  start=(kc == 0), stop=(kc == 7)), reads=[B_w2, B_xn2], writes=[Bps[pb]])
                S.op("act", lambda e: e.mul(out=qown[p][:, cs], in_=ps[pb][:, :], mul=0.125), reads=[Bps[pb]], writes=[B_qown[p]])
            if STAGE >= 4:
                for h in range(GH):
                    pb = 3 + (h % 2)
                    for kc in range(8):
                        S.op("pe", lambda e: e.matmul(ps[pb][:, :], lhsT=w2_bf[:, kc, 512 + h * 128:512 + (h + 1) * 128], rhs=xn2[:, kc, :],
                                                      start=(kc == 0), stop=(kc == 7)), reads=[B_w2, B_xn2], writes=[Bps[pb]])
                    S.dma("pool", lambda e: e.indirect_dma_start(out=og2[h % 2][:], out_offset=None, in_=og_d,
                                                                  in_offset=bass.IndirectOffsetOnAxis(ap=idxo[:, s_ * 4 + h:s_ * 4 + h + 1], axis=0)),
                          reads=[B_ogd, B_idxo], writes=[B_og2[h % 2]])
                    rms_rstd(og2[h % 2][:].rearrange("p (o n) -> p o n", o=1), B_og2[h % 2], 512, ogsq, B_ogsq, rstdo, B_rstdo, 5, nfeat=1, dim=128.0)
                    S.op("act", lambda e: e.activation(out=sg[:], in_=ps[pb][:, :], func=AF.Exp, scale=-1.0), reads=[Bps[pb]], writes=[B_sg])
                    S.op("dve", lambda e: e.tensor_scalar(out=sg[:], in0=sg[:], scalar1=1.0, scalar2=None, op0=ALU.add), reads=[B_sg], writes=[B_sg])
                    S.op("dve", lambda e: e.reciprocal(out=sg[:], in_=sg[:]), reads=[B_sg], writes=[B_sg])
                    S.op("dve", lambda e: e.tensor_tensor(out=sg[:], in0=ps[pb][:, :], in1=sg[:], op=ALU.mult), reads=[Bps[pb], B_sg], writes=[B_sg])
                    S.op("dve", lambda e: e.scalar_tensor_tensor(out=t1[:], in0=og2[h % 2][:], scalar=gn[:, 24:25], in1=rstdo[:], op0=ALU.mult, op1=ALU.mult),
                         reads=[B_og2[h % 2], B_gn, B_rstdo], writes=[B_t1])
                    S.op("dve", lambda e: e.tensor_tensor(out=mix[:, 4 + h, cs], in0=t1[:], in1=sg[:], op=ALU.mult), reads=[B_t1, B_sg], writes=[B_mix])
        S.barrier()
        sb.release(m_proj)
        if STAGE >= 6:
            CKT = 4
            NCK = 128 // CKT
            TOP2 = 229312 - 49 * 1024
            sbt2 = SbAlloc(nc, base=TOP2, top=229312, prefix="U")
            kc_t = [sbt2.tile([128, CKT * 512], F32, "kc_t%d" % i) for i in range(2)]; B_kc = [Buf("kc0"), Buf("kc1")]
            vc_t = [sbt2.tile([128, CKT * 512], F32, "vc_t%d" % i) for i in range(2)]; B_vc = [Buf("vc0"), Buf("vc1")]
            tmpk = sbt2.tile([128, CKT * 512], F32, "tmpk"); B_tmpk = Buf("tmpk")
            lfp = tmpk[:, 0:1024]; B_lfp = B_tmpk
            Ipre = sbt2.tile([128, 8, 128], F32, "Ipre"); B_Ipre = Buf("Ipre")
            scs = sbt2.tile([128, 128, 8], F32, "scs"); B_scs = Buf("scs")
            ps7 = psT_bf[:].bitcast(F32); Bps7 = B_psT
            totc = sb.tile([128, 8], F32, "totc"); B_totc = Buf("totc")
            tb = sb.tile([128, 8], F32, "tb"); B_tb = Buf("tb")
            dsum = sb.tile([128, 8], F32, "dsum"); B_dsum = Buf("dsum")
            tmpd = sb.tile([8, 8, 64], F32, "tmpd"); B_tmpd = Buf("tmpd")
            Od = sb.tile([8, 64], F32, "Od"); B_Od = Buf("Od")
            den8 = sb.tile([8, 1], F32, "den8"); B_den8 = Buf("den8")
            scr_o = dscr("scr_o", [4, 512], F32); B_scro = Buf("scr_o")
            scr_d = dscr("scr_d", [4, 8], F32); B_scrd = Buf("scr_d")
            idxk = sb.tile([128, 4, NCK], I32, "idxk"); B_idxk = Buf("idxk")
            for ck in range(NCK):
                S.op("dve", lambda e: e.tensor_scalar(out=idxk[:, :, ck], in0=pt_t[:], scalar1=float(NCK), scalar2=float(ck), op0=ALU.mult, op1=ALU.add),
                     reads=[B_pt], writes=[B_idxk])

            def decode_gen():
              for bb in range(4):
                  S.dma("pool", lambda e: e.indirect_dma_start(out=lfp, out_offset=None, in_=cache_lf,
                                                                in_offset=bass.IndirectOffsetOnAxis(ap=pt_t[:, bb:bb + 1], axis=0)),
                        reads=[B_pt], writes=[B_lfp])
                  S.op("dve", lambda e: e.tensor_tensor_scan(out=Ipre[:].rearrange("p h i -> p (h i)"), data0=m128[:], data1=lfp, initial=0.0,
                                                             op0=ALU.mult, op1=ALU.add), reads=[B_m128, B_lfp], writes=[B_Ipre])
                  S.op("dve", lambda e: e.tensor_copy(out=totc[:], in_=Ipre[:, :, 127]), reads=[B_Ipre], writes=[B_totc])
                  S.op("pe", lambda e: e.matmul(ps[0][:, 0:8], lhsT=SL_f[:], rhs=totc[:], start=True, stop=True), reads=[B_SL, B_totc], writes=[Bps[0]])
                  S.op("dve", lambda e: e.tensor_tensor(out=tb[:], in0=ps[0][:, 0:8], in1=totc[:], op=ALU.add), reads=[Bps[0], B_totc], writes=[B_tb])
                  S.op("dve", lambda e: e.tensor_tensor(out=tb[:], in0=tb[:], in1=lfb[:, bb, :], op=ALU.add), reads=[B_tb, B_lfb], writes=[B_tb])
                  S.op("dve", lambda e: e.tensor_tensor(out=Ipre[:], in0=tb[:].unsqueeze(2).to_broadcast([128, 8, 128]), in1=Ipre[:], op=ALU.subtract),
                       reads=[B_tb, B_Ipre], writes=[B_Ipre])
                  for ck in range(NCK):
                      i = ck % 2
                      S.dma("pool", lambda e: e.indirect_dma_start(out=kc_t[i][:], out_offset=None, in_=cache_k2,
                                                                    in_offset=bass.IndirectOffsetOnAxis(ap=idxk[:, bb, ck:ck + 1], axis=0)),
                            reads=[B_idxk], writes=[B_kc[i]])
                      S.op("dve", lambda e: e.tensor_tensor(out=tmpk[:].rearrange("p (t f) -> p t f", f=512), in0=kc_t[i][:].rearrange("p (t f) -> p t f", f=512),
                                                            in1=qb[:, bb:bb + 1, :].to_broadcast([128, CKT, 512]), op=ALU.mult),
                           reads=[B_kc[i], B_qb], writes=[B_tmpk])
                      S.op("dve", lambda e: e.tensor_reduce(out=scs[:, ck * CKT:(ck + 1) * CKT, :].rearrange("p t h -> p (t h)"),
                                                            in_=tmpk[:].rearrange("p (g d) -> p g d", d=64), axis=AX.X, op=ALU.add),
                           reads=[B_tmpk], writes=[B_scs])
                      yield
                  S.op("dve", lambda e: e.tensor_tensor(out=scs[:], in0=scs[:], in1=Ipre[:].rearrange("p h i -> p i h"), op=ALU.add),
                       reads=[B_scs, B_Ipre], writes=[B_scs])
                  S.op("act", lambda e: e.activation(out=scs[:], in_=scs[:], func=AF.Exp), reads=[B_scs], writes=[B_scs])
                  S.op("dve", lambda e: e.tensor_reduce(out=dsum[:], in_=scs[:].rearrange("p i h -> p h i"), axis=AX.X, op=ALU.add), reads=[B_scs], writes=[B_dsum])
                  for ck in range(NCK):
                      i = ck % 2
                      S.dma("pool", lambda e: e.indirect_dma_start(out=vc_t[i][:], out_offset=None, in_=cache_v2,
                                                                    in_offset=bass.IndirectOffsetOnAxis(ap=idxk[:, bb, ck:ck + 1], axis=0)),
                            reads=[B_idxk], writes=[B_vc[i]])
                      for il in range(CKT):
                          ii = ck * CKT + il
                          S.op("pe", lambda e: e.matmul(ps7[0:8, :], lhsT=scs[:, ii, :], rhs=vc_t[i][:, il * 512:(il + 1) * 512],
                                                        start=(ii == 0), stop=(ii == NCK * CKT - 1)), reads=[B_scs, B_vc[i]], writes=[Bps7])
                      yield
                  S.op("pe", lambda e: e.matmul(ps[0][0:8, 8:9], lhsT=dsum[:], rhs=ones_f[:, 0:1], start=True, stop=True), reads=[B_dsum, B_ones_f], writes=[Bps[0]])
                  S.op("dve", lambda e: e.tensor_tensor(out=tmpd[:].rearrange("p a d -> p (a d)"), in0=ps7[0:8, :], in1=bm[:].rearrange("p a d -> p (a d)"), op=ALU.mult),
                       reads=[Bps7, B_bm], writes=[B_tmpd])
                  S.op("dve", lambda e: e.tensor_reduce(out=Od[:], in_=tmpd[:].rearrange("p a d -> p d a"), axis=AX.X, op=ALU.add), reads=[B_tmpd], writes=[B_Od])
                  S.op("act", lambda e: e.copy(out=den8[:], in_=ps[0][0:8, 8:9]), reads=[Bps[0]], writes=[B_den8])
                  S.dma("sp", lambda e: e.dma_start(out=scr_o[bb:bb + 1, :].rearrange("o (h d) -> (o h) d", d=64), in_=Od[:]), reads=[B_Od], writes=[B_scro])
                  S.dma("sp", lambda e: e.dma_start(out=scr_d[bb:bb + 1, :].rearrange("o h -> h o"), in_=den8[:]), reads=[B_den8], writes=[B_scrd])
        kts = [sb.tile([128, SEQ], BF16, "kts%d" % i) for i in range(1)]; B_kts = [Buf("kts0")]
        vts = [sb.tile([128, NB, 65], BF16, "vts%d" % i) for i in range(2)]; B_vts = [Buf("vts0"), Buf("vts1")]
        mk = sb.tile([128, 16, 512], BF16, "mk"); B_mk = Buf("mk")
        bias_t = [sb.tile([128, NB], F32, "bias%d" % i) for i in range(2)]; B_bias = [Buf("bias0"), Buf("bias1")]
        NPT = 4
        Pt = [sb.tile([128, 512], BF16, "Pt%d" % i) for i in range(NPT)]; B_Pt = [Buf("Pt%d" % i) for i in range(NPT)]
        rcs = sb.tile([128, 512], F32, "rcs"); B_rcs = Buf("rcs")
        Osb = sb.tile([64, 512], F32, "Osb"); B_Osb = Buf("Osb")
        dec_it = decode_gen() if STAGE >= 6 else iter(())
        DEC_EVERY = 2
        SBANK = [1, 2, 4]
        OBANK = [3, 6]
        LA = 2
        unit = 0
        hcount = 0
        pcount = 0
        mb_t = sb.tile([128, 64], F32, "mb_t"); B_mb = Buf("mb")
        S.dma("sp", lambda e: e.dma_start(out=mb_t[:], in_=mbias), writes=[B_mb])
        for s_ in range(4):
            nk = 16 * s_ + 16
            cs = slice(s_ * 512, (s_ + 1) * 512)
            S.dma("sp", lambda e: e.dma_start(out=mk[:], in_=masks[s_].rearrange("j p q -> p j q")), writes=[B_mk])
            for p in range(4):
                kb_ = pcount % len(kts)
                pcount += 1
                S.dma("sp", lambda e: e.dma_start(out=kts[kb_][:, 0:nk * 128], in_=kT_d[p * 128:(p + 1) * 128, 0:nk * 128]),
                      reads=[B_kTd], writes=[B_kts[kb_]])
                for hh in range(2):
                    h = 2 * p + hh
                    vb = hcount % 2
                    ob = OBANK[hcount % 2]
                    hcount += 1
                    rs = slice(hh * 64, (hh + 1) * 64)
                    S.dma("sp", lambda e: e.dma_start(out=vts[vb][:, 0:nk, :], in_=v_d[h, :, 0:nk, :]), reads=[B_vd], writes=[B_vts[vb]])
                    S.op("dve", lambda e: e.tensor_scalar(out=bias_t[vb][:, 0:nk], in0=C_tab[:, h, 0:nk], scalar1=-1.0,
                                                          scalar2=cref[:, s_, h:h + 1], op0=ALU.mult, op1=ALU.add),
                         reads=[B_C, B_cref], writes=[B_bias[vb]])
                    S.op("dve", lambda e: e.tensor_tensor(out=bias_t[vb][:, 16 * s_:16 * s_ + 16], in0=bias_t[vb][:, 16 * s_:16 * s_ + 16],
                                                          in1=mb_t[:, 16 * s_:16 * s_ + 16], op=ALU.add),
                         reads=[B_bias[vb], B_mb], writes=[B_bias[vb]])
                    slot_of = {}
                    for n2 in range(nk + LA):
                        if n2 < nk:
                            n = n2
                            pb = SBANK[unit % 3]
                            pt = unit % NPT
                            unit += 1
                            if unit % DEC_EVERY == 0:
                                next(dec_it, None)
                            slot_of[n] = (pb, pt)
                            S.op("pe", lambda e: e.matmul(ps[pb][:, :], lhsT=kts[kb_][rs, n * 128:(n + 1) * 128], rhs=qown[p][rs, cs], start=True, stop=True),
                                 reads=[B_kts[kb_], B_qown[p]], writes=[Bps[pb]])
                        if n2 - LA >= 0:
                            n = n2 - LA
                            pb, pt = slot_of[n]
                            S.op("act", lambda e: e.activation(out=Pt[pt][:], in_=ps[pb][:, :], func=AF.Exp, bias=bias_t[vb][:, n:n + 1], scale=1.0),
                                 reads=[Bps[pb], B_bias[vb]], writes=[B_Pt[pt]])
                            if n >= 16 * s_:
                                S.op("dve", lambda e: e.tensor_tensor(out=Pt[pt][:], in0=Pt[pt][:], in1=mk[:, n - 16 * s_, :], op=ALU.mult),
                                     reads=[B_Pt[pt], B_mk], writes=[B_Pt[pt]])
                            S.op("pe", lambda e: e.matmul(ps[ob][0:65, :], lhsT=vts[vb][:, n, :], rhs=Pt[pt][:], start=(n == 0), stop=(n == nk - 1)),
                                 reads=[B_vts[vb], B_Pt[pt]], writes=[Bps[ob]])
                    S.op("dve", lambda e: e.reciprocal(out=rcs[64:65, :], in_=ps[ob][64:65, :]), reads=[Bps[ob]], writes=[B_rcs])
                    S.op("act", lambda e: e.copy(out=Osb[:], in_=ps[ob][0:64, :]), reads=[Bps[ob]], writes=[B_Osb])
                    S.op("pe", lambda e: e.matmul(ps[5][0:64, :], lhsT=ones_f[64:65, 0:64], rhs=rcs[64:65, :], start=True, stop=True),
                         reads=[B_ones_f, B_rcs], writes=[Bps[5]])
                    S.op("dve", lambda e: e.tensor_tensor(out=mix[rs, p, cs], in0=Osb[:], in1=ps[5][0:64, :], op=ALU.mult),
                         reads=[B_Osb, Bps[5]], writes=[B_mix])

        if STAGE >= 6:
            for _ in dec_it:
                pass
            S.dma("sp", lambda e: e.dma_start(out=Otm[:], in_=scr_o), reads=[B_scro], writes=[B_Otm])
            S.dma("sp", lambda e: e.dma_start(out=dentm[:], in_=scr_d), reads=[B_scrd], writes=[B_dentm])
            S.op("dve", lambda e: e.tensor_tensor(out=qk[:], in0=fv_tm[:].rearrange("p (h d) -> p h d", d=64), in1=enew[:].unsqueeze(2).to_broadcast([4, 8, 64]), op=ALU.mult),
                 reads=[B_fv_tm, B_enew], writes=[B_qk])
            S.op("dve", lambda e: e.tensor_tensor(out=Otm[:], in0=Otm[:], in1=qk[:].rearrange("p h d -> p (h d)"), op=ALU.add), reads=[B_Otm, B_qk], writes=[B_Otm])
            S.op("dve", lambda e: e.tensor_tensor(out=dentm[:], in0=dentm[:], in1=enew[:], op=ALU.add), reads=[B_dentm, B_enew], writes=[B_dentm])
            S.op("dve", lambda e: e.reciprocal(out=dentm[:], in_=dentm[:]), reads=[B_dentm], writes=[B_dentm])
            S.op("dve", lambda e: e.tensor_tensor(out=Otm[:].rearrange("p (h d) -> p h d", d=64), in0=Otm[:].rearrange("p (h d) -> p h d", d=64),
                                                  in1=dentm[:].unsqueeze(2).to_broadcast([4, 8, 64]), op=ALU.mult), reads=[B_Otm, B_dentm], writes=[B_Otm])
            for pr in range(4):
                S.op("pe", lambda e: e.transpose(out=ps[5][:, 0:4], in_=Otm[:, pr * 128:(pr + 1) * 128], identity=ident_f[0:4, 0:4]),
                     reads=[B_Otm, B_ident], writes=[Bps[5]])
                S.op("act", lambda e: e.copy(out=mix[:, pr, 2048:2052], in_=ps[5][:, 0:4]), reads=[Bps[5]], writes=[B_mix])
        assert sb.off <= TOP2, (sb.off, TOP2)
    if STAGE >= 3 and 'M' in DBG:
        o_mix = dout("o_mix", [128, 8, NOWN], BF16)
        B_omix = Buf("o_mix"); out_bufs.append(B_omix)
        S.dma("sp", lambda e: e.dma_start(out=o_mix, in_=mix[:]), reads=[B_mix], writes=[B_omix])
    if STAGE >= 5:
        S.barrier()
        sb.release(m_p2)
        wo_bf = sb.tile([128, 8, 1024], BF16, "wo_bf"); B_wo = Buf("wo")
        wup_bf = sb.tile([128, 8, 4096], BF16, "wup_bf"); B_wup = Buf("wup")
        wdn_bf = sb.tile([128, 32, 1024], BF16, "wdn_bf"); B_wdn = Buf("wdn")
        wst3 = [sb.tile([128, 512], F32, "wst3_%d" % i) for i in range(2)]; B_wst3 = [Buf("wst3_0"), Buf("wst3_1")]
        cnt3 = 0
        jobs = [(w_o[kc * 128:(kc + 1) * 128, q * 512:(q + 1) * 512], wo_bf[:, kc, q * 512:(q + 1) * 512], B_wo) for kc in range(8) for q in range(2)]
        jobs += [(w_up[kc * 128:(kc + 1) * 128, q * 512:(q + 1) * 512], wup_bf[:, kc, q * 512:(q + 1) * 512], B_wup)
                 for kc in range(8) for q in range(8)]
        jobs += [(w_down[f * 128:(f + 1) * 128, q * 512:(q + 1) * 512], wdn_bf[:, f, q * 512:(q + 1) * 512], B_wdn)
                 for f in range(32) for q in range(2)]
        for src, dst, Bd in jobs:
            i = cnt3 % 2
            cnt3 += 1
            S.dma("sp", lambda e: e.dma_start(out=wst3[i][:], in_=src), writes=[B_wst3[i]])
            eng3 = ["dve", "act"][cnt3 % 2]
            if eng3 == "act":
                S.op("act", lambda e: e.copy(out=dst, in_=wst3[i][:]), reads=[B_wst3[i]], writes=[Bd])
            else:
                S.op(eng3, lambda e: e.tensor_copy(out=dst, in_=wst3[i][:]), reads=[B_wst3[i]], writes=[Bd])
        NT3 = NOWN // P3T
        x3 = sb.tile([128, 8, P3T], F32, "x3"); B_x3 = Buf("x3")
        hT = sb.tile([128, 8, P3T], F32, "hT"); B_hT = Buf("hT")
        sq3 = sb.tile([128, 8, P3T], BF16, "sq3"); B_sq3 = Buf("sq3")
        hn = sb.tile([128, 8, P3T], BF16, "hn"); B_hn = Buf("hn")
        rs3 = sb.tile([128, P3T], F32, "rs3"); B_rs3 = Buf("rs3")
        rr = [sb.tile([128, P3T], F32, "rr%d" % i) for i in range(2)]; B_rr = [Buf("rr0"), Buf("rr1")]
        uT = sb.tile([128, 32, P3T], BF16, "uT"); B_uT = Buf("uT")
        yT = x3; B_yT = B_x3
        o_y_v = o_y.rearrange("(m p) t -> p m t", p=128)
        for tt in range(NT3):
            cs = slice(tt * P3T, (tt + 1) * P3T)
            S.dma("sp", lambda e: e.dma_start(out=x3[:], in_=xT_own_v[:, :, cs]), writes=[B_x3])
            for m in range(8):
                pb = 1 + (m % 2)
                for kc in range(8):
                    S.op("pe", lambda e: e.matmul(ps[pb][:, 0:P3T], lhsT=wo_bf[:, kc, m * 128:(m + 1) * 128], rhs=mix[:, kc, cs],
                                                  start=(kc == 0), stop=(kc == 7)), reads=[B_wo, B_mix], writes=[Bps[pb]])
                S.op("dve", lambda e: e.tensor_tensor(out=hT[:, m, :], in0=ps[pb][:, 0:P3T], in1=x3[:, m, :], op=ALU.add),
                     reads=[Bps[pb], B_x3], writes=[B_hT])
            rms_rstd(hT, B_hT, P3T, sq3, B_sq3, rs3, B_rs3, 0)
            for kc in range(8):
                S.op("dve", lambda e: e.scalar_tensor_tensor(out=hn[:, kc, :], in0=hT[:, kc, :], scalar=gn[:, 8 + kc:9 + kc], in1=rs3[:],
                                                             op0=ALU.mult, op1=ALU.mult), reads=[B_hT, B_gn, B_rs3], writes=[B_hn])
            for f in range(32):
                pb = 3 + (f % 2)
                for kc in range(8):
                    S.op("pe", lambda e: e.matmul(ps[pb][:, 0:P3T], lhsT=wup_bf[:, kc, f * 128:(f + 1) * 128], rhs=hn[:, kc, :],
                                                  start=(kc == 0), stop=(kc == 7)), reads=[B_wup, B_hn], writes=[Bps[pb]])
                j = f % 2
                S.op("act", lambda e: e.activation(out=rr[j][:], in_=ps[pb][:, 0:P3T], func=AF.Relu), reads=[Bps[pb]], writes=[B_rr[j]])
                S.op("dve" if j == 0 else "pool", lambda e: e.tensor_tensor(out=uT[:, f, :], in0=rr[j][:], in1=rr[j][:], op=ALU.mult),
                     reads=[B_rr[j]], writes=[B_uT])
            for m in range(8):
                pb = 1 + (m % 2)
                for f in range(32):
                    S.op("pe", lambda e: e.matmul(ps[pb][:, 0:P3T], lhsT=wdn_bf[:, f, m * 128:(m + 1) * 128], rhs=uT[:, f, :],
                                                  start=(f == 0), stop=(f == 31)), reads=[B_wdn, B_uT], writes=[Bps[pb]])
                S.op("dve", lambda e: e.tensor_tensor(out=hT[:, m, :], in0=ps[pb][:, 0:P3T], in1=hT[:, m, :], op=ALU.add),
                     reads=[Bps[pb], B_hT], writes=[B_hT])
            rms_rstd(hT, B_hT, P3T, sq3, B_sq3, rs3, B_rs3, 0)
            for m in range(8):
                S.op("dve", lambda e: e.scalar_tensor_tensor(out=yT[:, m, :], in0=hT[:, m, :], scalar=gn[:, 16 + m:17 + m], in1=rs3[:],
                                                             op0=ALU.mult, op1=ALU.mult), reads=[B_hT, B_gn, B_rs3], writes=[B_yT])
            S.dma("sp", lambda e: e.dma_start(out=o_y_v[:, :, cs], in_=yT[:]), reads=[B_yT], writes=[B_oy])

    k.out_bufs = out_bufs
    k.locals = locals()
    return k


def finish_program(k):
    S = k.S
    S.wait_all("sp", k.out_bufs)
    S.emit()
    return k.nc


def kernel(x_prompt, x_sample, cache_k, cache_v, cache_logf, state_gla, page_table,
           norm1_g, w_in, fox_b_f, gla_w_gate_up, gla_b_gate, gla_norm_g, w_o,
           norm2_g, w_up, w_down, final_g):
    f32 = np.float32
    import ml_dtypes
    x_prompt = np.asarray(x_prompt, f32); x_sample = np.asarray(x_sample, f32)
    w_in0 = np.asarray(w_in, f32)[0]
    k = build_program()
    nc = finish_program(k)
    cols1 = np.concatenate([np.arange(C_FK, C_FK + 512), np.arange(C_GQ, C_GQ + 256), np.arange(C_GK, C_GK + 256),
                            np.arange(C_GLR, C_GLR + 16), np.arange(C_FV, C_FV + 512), np.arange(C_GV, C_GV + 512),
                            np.arange(C_FF, C_FF + 8)])
    w1 = np.ascontiguousarray(w_in0[:, cols1])
    cols2 = np.concatenate([np.arange(C_FQ, C_FQ + 512), np.arange(C_GG, C_GG + 512)])
    w2 = np.ascontiguousarray(w_in0[:, cols2])
    gains = np.zeros((128, 40), f32)
    gains[:, 0:8] = np.asarray(norm1_g, f32)[0].reshape(8, 128).T
    gains[:, 8:16] = np.asarray(norm2_g, f32)[0].reshape(8, 128).T
    gains[:, 16:24] = np.asarray(final_g, f32).reshape(8, 128).T
    gains[:, 24] = np.asarray(gla_norm_g, f32)[0]
    gains[:, 25:27] = -np.asarray(gla_b_gate, f32)[0].reshape(2, 128).T
    bfrep = np.broadcast_to(np.asarray(fox_b_f, f32)[0][None, :], (128, 8)).copy()
    wg = np.asarray(gla_w_gate_up, f32)[0]
    perm_s = np.concatenate([np.arange(C_FQ, C_FQ + 512), np.arange(C_FK, C_FK + 512), np.arange(C_FV, C_FV + 512),
                             np.arange(C_GQ, C_GQ + 256), np.arange(C_GK, C_GK + 256), np.arange(C_GV, C_GV + 512),
                             np.arange(C_GG, C_GG + 512), np.arange(C_GLR, C_GLR + 16), np.arange(C_FF, C_FF + 8)])
    w_s = np.ascontiguousarray(w_in0[:, perm_s])
    if STAGE >= 6:
        ck2 = np.asarray(cache_k, f32)[0].reshape(-1, 2048)
        cv2 = np.asarray(cache_v, f32)[0].reshape(-1, 2048)
        clf = np.ascontiguousarray(np.asarray(cache_logf, f32)[0].transpose(0, 2, 1)).reshape(-1, 1024)
        pt_all = np.asarray(page_table, np.int32)
        sg_all = np.asarray(state_gla, f32)[0]
    in_maps = []
    for c in range(8):
        b, r = c // 4, c % 4
        tiles = own_tiles(r)
        xo = np.concatenate([x_prompt[b, t * 512:(t + 1) * 512, :] for t in tiles] + [x_sample[4 * c:4 * c + 4, 0, :]], axis=0)
        masks = np.zeros((4, 16, 128, 512), ml_dtypes.bfloat16)
        onehot = np.zeros((4, 128, 64), f32)
        idx_o = np.zeros((128, 16), np.int32)
        mbias = np.zeros((128, 64), f32)
        kp = np.arange(128)[:, None]; qp = np.arange(512)[None, :]
        for s_, t in enumerate(tiles):
            for j in range(16):
                kb = 16 * s_ + j
                if kb < 4 * t:
                    masks[s_, j] = 1
                elif kb < 4 * t + 4:
                    masks[s_, j] = ((kb - 4 * t) * 128 + kp <= qp)
                else:
                    mbias[:, s_ * 16 + j] = -30000.0
            if t > 0:
                onehot[s_, 127, 4 * t - 1] = 1.0
            for h in range(GH):
                idx_o[:, s_ * 4 + h] = (t * GH + h) * 128 + np.arange(128)
        m = {"xT_full": np.ascontiguousarray(x_prompt[b].T), "xT_own": np.ascontiguousarray(xo.T),
             "w1": w1, "w2": w2, "w_o": np.asarray(w_o, f32)[0], "w_up": np.asarray(w_up, f32)[0],
             "w_down": np.asarray(w_down, f32)[0], "gains": gains, "bfrep": bfrep, "wg": wg,
             "masks": masks, "onehot": onehot, "idx_o": idx_o, "mbias": mbias, "w_s": w_s}
        if STAGE >= 6:
            m["ptab"] = np.ascontiguousarray(pt_all[4 * c:4 * c + 4, :].T)
            m["cache_k2"] = ck2; m["cache_v2"] = cv2; m["cache_lf"] = clf
            sg = sg_all[4 * c:4 * c + 4].reshape(4, 2, 2, 64, 128).transpose(1, 2, 3, 0, 4).reshape(2, 128, 4, 128)
            m["sgla"] = np.ascontiguousarray(sg)
        in_maps.append(m)
    used = set(k.used_inputs)
    in_maps = [{n: v for n, v in m.items() if n in used} for m in in_maps]
    res = run_bass_kernel_spmd(nc, in_maps, core_ids=list(range(8)))
    R = res.results
    y_prompt = np.zeros((2, SEQ, D), f32); y_sample = np.zeros((32, 1, D), f32)
    new_k = np.zeros((1, 2, SEQ, H, 64), f32); new_v = np.zeros((1, 2, SEQ, H, 64), f32)
    new_lf = np.zeros((1, 2, SEQ, H), f32); new_gla = np.zeros((1, 2, GH, 64, 128), f32)
    ks = np.zeros((1, 32, 1, H, 64), f32); vs = np.zeros((1, 32, 1, H, 64), f32)
    lfs = np.zeros((1, 32, 1, H), f32); glas = np.zeros((1, 32, GH, 64, 128), f32)
    for c in range(8):
        b, r = c // 4, c % 4
        o = R[c]
        if r == 0:
            new_k[0, b] = o["o_k"].T.reshape(SEQ, H, 64)
            new_v[0, b] = o["o_v"].reshape(SEQ, H, 64)
            new_lf[0, b] = o["o_lf"].reshape(128, H, NB).transpose(2, 0, 1).reshape(SEQ, H)
            new_gla[0, b] = o["o_gla"].reshape(GH, 64, 128)
        if "o_y" in o and STAGE >= 5:
            yT = o["o_y"]
            for s_, t in enumerate(own_tiles(r)):
                y_prompt[b, t * 512:(t + 1) * 512] = yT[:, s_ * 512:(s_ + 1) * 512].T
            y_sample[4 * c:4 * c + 4, 0] = yT[:, 2048:2052].T
        if STAGE >= 6:
            ks[0, 4 * c:4 * c + 4, 0] = o["o_ks"].reshape(4, H, 64)
            vs[0, 4 * c:4 * c + 4, 0] = o["o_vs"].reshape(4, H, 64)
            lfs[0, 4 * c:4 * c + 4, 0] = o["o_lfs"]
            glas[0, 4 * c:4 * c + 4] = o["o_glas"].reshape(2, 2, 64, 4, 128).transpose(3, 0, 1, 2, 4).reshape(4, GH, 64, 128)
    kernel.last_results = R
    return (y_prompt, y_sample, new_k, new_v, new_lf, new_gla, ks, vs, lfs, glas)
```

```python
import numpy as np
import concourse.bass as bass
import concourse.mybir as mybir
from concourse.bass_utils import run_bass_kernel_spmd

F32 = mybir.dt.float32
F32R = mybir.dt.float32r
BF16 = mybir.dt.bfloat16
I32 = mybir.dt.int32
ALU = mybir.AluOpType
AF = mybir.ActivationFunctionType
AX = mybir.AxisListType


import types


def _freeze(fn):
    if fn is None or fn.__closure__ is None:
        return fn
    cells = []
    for c in fn.__closure__:
        try:
            cells.append(types.CellType(c.cell_contents))
        except ValueError:
            cells.append(c)
    return types.FunctionType(fn.__code__, fn.__globals__, fn.__name__, fn.__defaults__, tuple(cells))


class Buf:
    __slots__ = ("name", "w", "r", "excl")

    def __init__(self, name, excl=False):
        self.name = name
        self.excl = excl
        self.w = None
        self.r = {}


class Sched:
    COMPUTE = ("pe", "act", "dve", "pool")

    def __init__(self, nc, n_dma_sems=40):
        self.nc = nc
        self.engs = {"pe": nc.tensor, "act": nc.scalar, "dve": nc.vector, "pool": nc.gpsimd, "sp": nc.sync}
        self.prog = {e: [] for e in self.engs}
        self.sems = {}
        for e in self.COMPUTE:
            self.sems[e] = nc.alloc_semaphore(name="c_" + e)
        self.cnt = {e: 0 for e in self.COMPUTE}
        self.dsem = [nc.alloc_semaphore(name="d%d" % i) for i in range(n_dma_sems)]
        self.dcnt = [0] * n_dma_sems
        self.drr = 0
        self.waited = {e: {} for e in self.engs}
        self.n_ops = 0

    def _sem(self, key):
        return self.sems[key] if isinstance(key, str) else self.dsem[key]

    def _deps(self, reads, writes, eng=None):
        deps = {}

        def add(tok):
            if tok is None:
                return
            k, v = tok
            if deps.get(k, 0) < v:
                deps[k] = v

        for b in reads:
            add(b.w)
            if b.excl:
                for k, v in b.r.items():
                    if k != eng:
                        add((k, v))
        for b in writes:
            add(b.w)
            for k, v in b.r.items():
                add((k, v))
        return deps

    def _emit_waits(self, eng, deps):
        waits = []
        wd = self.waited[eng]
        for k, v in deps.items():
            if k == "pe" and eng == "pe":
                continue
            if wd.get(k, 0) >= v:
                continue
            wd[k] = v
            waits.append((self._sem(k), v))
        return waits

    def _commit(self, tok, reads, writes):
        k, v = tok
        for b in writes:
            b.w = tok
            b.r = {}
        for b in reads:
            if b.r.get(k, 0) < v:
                b.r[k] = v

    def op(self, eng, fn, reads=(), writes=()):
        deps = self._deps(reads, writes, eng)
        waits = self._emit_waits(eng, deps)
        self.cnt[eng] += 1
        tok = (eng, self.cnt[eng])
        sem = self.sems[eng]
        self.prog[eng].append((waits, _freeze(fn), sem, 1))
        self._commit(tok, reads, writes)
        self.n_ops += 1
        return tok

    def dma(self, eng, fn, reads=(), writes=()):
        deps = self._deps(reads, writes)
        k = self.drr
        self.drr = (self.drr + 1) % len(self.dsem)
        if self.dcnt[k] > 0:
            deps[k] = max(deps.get(k, 0), 16 * self.dcnt[k])
        waits = self._emit_waits(eng, deps)
        self.dcnt[k] += 1
        tok = (k, 16 * self.dcnt[k])
        self.prog[eng].append((waits, _freeze(fn), self.dsem[k], 16))
        self._commit(tok, reads, writes)
        self.n_ops += 1
        return tok

    def barrier(self):
        deps = {e: self.cnt[e] for e in self.COMPUTE if self.cnt[e] > 0}
        for k_, c_ in enumerate(self.dcnt):
            if c_ > 0:
                deps[k_] = 16 * c_
        for e in self.engs:
            waits = self._emit_waits(e, dict(deps))
            self.prog[e].append((waits, None, None, 0))

    def wait_all(self, eng, bufs):
        deps = self._deps(bufs, ())
        waits = self._emit_waits(eng, deps)
        self.prog[eng].append((waits, None, None, 0))

    def emit(self):
        nc = self.nc
        with nc.Block() as block:
            def mk(ename):
                def body(e):
                    for waits, fn, sem, inc in self.prog[ename]:
                        for s, v in waits:
                            e.wait_ge(s, v)
                        if fn is not None:
                            fn(e).then_inc(sem, inc)
                return body

            block.sync(mk("sp"))
            block.tensor(mk("pe"))
            block.scalar(mk("act"))
            block.vector(mk("dve"))
            block.gpsimd(mk("pool"))


class SbAlloc:
    def __init__(self, nc, base=16512, top=229312, prefix=""):
        self.prefix = prefix
        self.nc = nc
        self.off = base
        self.top = top
        self.n = 0
        self.peak = 0

    def mark(self):
        return self.off

    def release(self, mark):
        self.off = mark

    def tile(self, shape, dtype, name=None):
        esz = {F32: 4, F32R: 4, BF16: 2, I32: 4}[dtype]
        free = 1
        for s in shape[1:]:
            free *= s
        nbytes = (free * esz + 63) // 64 * 64
        off = self.off
        self.off += nbytes
        self.peak = max(self.peak, self.off)
        assert self.off <= self.top, "SBUF overflow: %d > %d (%s)" % (self.off, self.top, name)
        self.n += 1
        nm = "%s%s_%d" % (self.prefix, name or "t", self.n)
        return self.nc.alloc_sbuf_tensor_at(nm, list(shape), dtype, offset=off)


D = 1024
SEQ = 8192
NT = 16
NB = 64
H = 8
GH = 4
EPS = 1e-6
NOWN = 2052
P3T = 108
import os
STAGE = int(os.environ.get('K_STAGE', '6'))
DBG_NT = int(os.environ.get('K_NT', '16'))
DBG = os.environ.get('K_DBG', '')

C_FQ, C_FK, C_FV, C_FF = 0, 512, 1024, 1536
C_GQ, C_GK, C_GV, C_GLR, C_GG = 1544, 1800, 2056, 2568, 2584


def own_tiles(r):
    return [r, 7 - r, 8 + r, 15 - r]


class K:
    pass


def build_program():
    nc = bass.Bass("TRN2", target_bir_lowering=False)
    S = Sched(nc)
    sb = SbAlloc(nc)
    k = K()
    k.nc, k.S, k.sb = nc, S, sb

    k.used_inputs = []

    def din(name, shape, dt=F32):
        k.used_inputs.append(name)
        return nc.dram_tensor(name, list(shape), dt, kind="ExternalInput").ap()

    def dout(name, shape, dt=F32):
        return nc.dram_tensor(name, list(shape), dt, kind="ExternalOutput").ap()

    def dscr(name, shape, dt):
        return nc.dram_tensor(name, list(shape), dt, kind="Internal").ap()

    xT_full = din("xT_full", [D, SEQ])
    xT_own = din("xT_own", [D, NOWN])
    w1 = din("w1", [D, 2072])
    w2 = din("w2", [D, 1024])
    w_o = din("w_o", [D, D])
    w_up = din("w_up", [D, 4096])
    w_down = din("w_down", [4096, D])
    gains = din("gains", [128, 40])
    bfrep = din("bfrep", [128, 8])
    wg = din("wg", [16, 256])
    masks = din("masks", [4, 16, 128, 512], BF16)
    onehot = din("onehot", [4, 128, 64])
    idx_o = din("idx_o", [128, 16], I32)
    mbias = din("mbias", [128, 64])

    NPHYS = int(os.environ.get("K_NPHYS", "5120"))
    w_s = din("w_s", [D, 3096])
    ptab = din("ptab", [128, 4], I32)
    cache_k2 = din("cache_k2", [NPHYS * 32, 2048])
    cache_v2 = din("cache_v2", [NPHYS * 32, 2048])
    cache_lf = din("cache_lf", [NPHYS, 1024])
    sgla = din("sgla", [2, 128, 4, 128])
    o_ks = dout("o_ks", [4, 512]); o_vs = dout("o_vs", [4, 512]); o_lfs = dout("o_lfs", [4, 8])
    o_glas = dout("o_glas", [2, 128, 4, 128])
    o_y = dout("o_y", [D, NOWN])
    o_k = dout("o_k", [512, SEQ])
    o_v = dout("o_v", [SEQ, 512])
    o_lf = dout("o_lf", [128, NB * H])
    o_gla = dout("o_gla", [256, 128])

    kT_d = dscr("kT_d", [512, SEQ], BF16)
    v_d = dscr("v_d", [H, 128, NB, 65], BF16)
    og_d = dscr("og_d", [NT * GH * 128, 512], BF16)

    def cbuf(name):
        return Buf(name)

    ones_bf = sb.tile([128, 128], BF16, "ones_bf"); B_ones_bf = cbuf("ones_bf")
    ones_f = sb.tile([128, 128], F32, "ones_f"); B_ones_f = cbuf("ones_f")
    U_f = sb.tile([128, 128], F32, "U_f"); B_U = cbuf("U")
    tri_bf = sb.tile([128, 128], BF16, "tri_bf"); B_tri = cbuf("tri")
    gn = sb.tile([128, 40], F32, "gains"); B_gn = cbuf("gains")
    bfr = sb.tile([128, 8], F32, "bfr"); B_bfr = cbuf("bfr")
    wg_f = sb.tile([16, 256], F32, "wg_f"); wg_bf = sb.tile([16, 256], BF16, "wg_bf"); B_wg = cbuf("wg")
    lf_tab = sb.tile([128, H, NB], F32, "lf_tab"); B_lf = cbuf("lf")
    C_tab = sb.tile([128, H, NB], F32, "C_tab"); B_C = cbuf("C")

    S.op("pool", lambda e: e.memset(ones_bf[:], 1.0), writes=[B_ones_bf])
    S.op("pool", lambda e: e.memset(ones_f[:], 1.0), writes=[B_ones_f])
    S.op("pool", lambda e: e.memset(U_f[:], 1.0), writes=[B_U])
    S.op("pool", lambda e: e.affine_select(out=U_f[:], in_=U_f[:], pattern=[[1, 128]], compare_op=ALU.is_ge,
                                           fill=0.0, base=0, channel_multiplier=-1), writes=[B_U])
    S.op("pool", lambda e: e.tensor_copy(out=tri_bf[:], in_=U_f[:]), reads=[B_U], writes=[B_tri])
    S.dma("sp", lambda e: e.dma_start(out=gn[:], in_=gains), writes=[B_gn])
    S.dma("sp", lambda e: e.dma_start(out=bfr[:], in_=bfrep), writes=[B_bfr])
    S.dma("sp", lambda e: e.dma_start(out=wg_f[:], in_=wg), writes=[B_wg])
    S.op("dve", lambda e: e.tensor_copy(out=wg_bf[:], in_=wg_f[:]), reads=[B_wg], writes=[B_wg])

    ps = [nc.alloc_psum_tensor("ps%d" % i, [128, 512], F32) for i in range(7)]
    Bps = [Buf("ps%d" % i, excl=True) for i in range(7)]
    psT_bf = nc.alloc_psum_tensor("psT_bf", [128, 1024], BF16); B_psT = Buf("psT", excl=True)
    ident_bf = sb.tile([128, 128], BF16, "ident_bf"); B_ident = Buf("ident")
    ident_f = sb.tile([128, 128], F32, "ident_f")
    S.op("pool", lambda e: e.affine_select(out=ident_f[:], in_=U_f[:], pattern=[[-1, 128]], compare_op=ALU.is_ge,
                                           fill=0.0, base=0, channel_multiplier=1), reads=[B_U], writes=[B_ident])
    S.op("pool", lambda e: e.tensor_copy(out=ident_bf[:], in_=ident_f[:]), reads=[B_ident], writes=[B_ident])

    out_bufs = []

    def rms_rstd(xt, B_xt, n, sq, B_sq, rstd, B_rstd, pbank, nfeat=8, dim=1024.0):
        S.op("act", lambda e: e.activation(out=sq[:, 0:nfeat, 0:n], in_=xt[:, 0:nfeat, 0:n], func=AF.Square),
             reads=[B_xt], writes=[B_sq])
        for kc in range(nfeat):
            S.op("pe", lambda e, kc=kc: e.matmul(ps[pbank][:, 0:n], lhsT=ones_bf[:], rhs=sq[:, kc, 0:n],
                                                  start=(kc == 0), stop=(kc == nfeat - 1)),
                 reads=[B_sq, B_ones_bf], writes=[Bps[pbank]])
        S.op("act", lambda e: e.activation(out=rstd[:, 0:n], in_=ps[pbank][:, 0:n], func=AF.Ln, scale=1.0 / dim, bias=EPS),
             reads=[Bps[pbank]], writes=[B_rstd])
        S.op("act", lambda e: e.activation(out=rstd[:, 0:n], in_=rstd[:, 0:n], func=AF.Exp, scale=-0.5),
             reads=[B_rstd], writes=[B_rstd])

    m_p1 = sb.mark()
    rmask = sb.tile([128, 512], F32, "rmask"); B_rmask = cbuf("rmask")
    S.op("pool", lambda e: e.memset(rmask[:], 1.0), writes=[B_rmask])
    S.op("pool", lambda e: e.memset(rmask[:].rearrange("p (c t) -> p c t", t=128)[:, :, 0:1], 0.0), writes=[B_rmask])
    w1_bf = sb.tile([128, 8, 2072], BF16, "w1_bf"); B_w1 = Buf("w1")
    wst = [sb.tile([128, 2072], F32, "wst%d" % i) for i in range(2)]; B_wst = [Buf("wst0"), Buf("wst1")]
    for kc in range(8):
        i = kc % 2
        S.dma("sp", lambda e, kc=kc, i=i: e.dma_start(out=wst[i][:], in_=w1[kc * 128:(kc + 1) * 128, :]), writes=[B_wst[i]])
        if kc % 2 == 0:
            S.op("dve", lambda e, kc=kc, i=i: e.tensor_copy(out=w1_bf[:, kc, :], in_=wst[i][:]), reads=[B_wst[i]], writes=[B_w1])
        else:
            S.op("act", lambda e, kc=kc, i=i: e.copy(out=w1_bf[:, kc, :], in_=wst[i][:]), reads=[B_wst[i]], writes=[B_w1])
    W_FK, W_GQ, W_GK, W_GLR, W_FV, W_GV, W_FF = 0, 512, 768, 1024, 1040, 1552, 2064

    xt = [sb.tile([128, 8, 512], F32, "xt%d" % i) for i in range(2)]; B_xt = [Buf("xt0"), Buf("xt1")]
    sq = sb.tile([128, 8, 512], BF16, "sq"); B_sq = Buf("sq")
    xn = sb.tile([128, 8, 512], BF16, "xn"); B_xn = Buf("xn")
    rstd = sb.tile([128, 512], F32, "rstd"); B_rstd = Buf("rstd")
    kst = [sb.tile([128, 512], F32, "kst%d" % i) for i in range(2)]; B_kst = [Buf("kst0"), Buf("kst1")]
    kbf = [sb.tile([128, 512], F32 if "F" in DBG else BF16, "kbf%d" % i) for i in range(2)]; B_kbf = [Buf("kbf0"), Buf("kbf1")]
    vst = [sb.tile([128, 512], F32, "vst%d" % i) for i in range(2)]; B_vst = [Buf("vst0"), Buf("vst1")]
    vaug = [sb.tile([128, H, 4, 65], BF16, "vaug%d" % i) for i in range(2)]; B_vaug = [Buf("vaug0"), Buf("vaug1")]
    for i in range(2):
        S.op("pool", lambda e, i=i: e.memset(vaug[i][:], 1.0), writes=[B_vaug[i]])
    zff = sb.tile([128, 8], F32, "zff"); B_zff = Buf("zff")
    gq_f = [sb.tile([128, 512], F32, "gq%d" % i) for i in range(2)]; B_gq = [Buf("gq0"), Buf("gq1")]
    gk_f = [sb.tile([128, 512], F32, "gk%d" % i) for i in range(2)]; B_gk = [Buf("gk0"), Buf("gk1")]
    glr_bf = sb.tile([16, 512], BF16, "glr"); B_glr = Buf("glr")
    Lb = [sb.tile([128, 512], F32, "L%d" % i) for i in range(2)]; B_L = [Buf("L0"), Buf("L1")]
    Bc = [sb.tile([128, 512], F32, "Bc%d" % i) for i in range(2)]; B_Bc = [Buf("Bc0"), Buf("Bc1")]
    dd = [sb.tile([128, 512], F32, "dd%d" % i) for i in range(2)]; B_dd = [Buf("dd0"), Buf("dd1")]
    ee = [sb.tile([128, 512], F32, "ee%d" % i) for i in range(2)]; B_ee = [Buf("ee0"), Buf("ee1")]
    e3 = [sb.tile([128, 512], F32, "e3%d" % i) for i in range(2)]; B_e3 = [Buf("e30"), Buf("e31")]
    qt_ = [sb.tile([128, 512], BF16, "qt%d" % i) for i in range(2)]; B_qt = [Buf("qt0"), Buf("qt1")]
    kt_ = [sb.tile([128, 512], BF16, "kt%d" % i) for i in range(2)]; B_kt = [Buf("kt0"), Buf("kt1")]
    qh_ = [sb.tile([128, 512], BF16, "qh%d" % i) for i in range(2)]; B_qh = [Buf("qh0"), Buf("qh1")]
    kh_ = [sb.tile([128, 512], BF16, "kh%d" % i) for i in range(2)]; B_kh = [Buf("kh0"), Buf("kh1")]
    gvb = [sb.tile([128, 512], BF16, "gvb%d" % i) for i in range(4)]; B_gvb = [Buf("gvb%d" % i) for i in range(4)]
    khT = [sb.tile([128, 128], BF16, "khT%d" % i) for i in range(2)]; B_khT = [Buf("khT0"), Buf("khT1")]
    AT = [sb.tile([128, 128], BF16, "AT%d" % i) for i in range(2)]; B_AT = [Buf("AT0"), Buf("AT1")]
    ost = [sb.tile([128, 512], BF16, "ost%d" % i) for i in range(4)]; B_ost = [Buf("ost%d" % i) for i in range(4)]
    Sst = [sb.tile([128, 128], F32, "Sst%d" % i) for i in range(2)]; B_Sst = [Buf("Sst0"), Buf("Sst1")]
    Sbf = [sb.tile([128, 128], BF16, "Sbf%d" % i) for i in range(2)]; B_Sbf = [Buf("Sbf0"), Buf("Sbf1")]
    for i in range(2):
        S.op("pool", lambda e, i=i: e.memset(Sst[i][:], 0.0), writes=[B_Sst[i]])
        S.op("pool", lambda e, i=i: e.memset(Sbf[i][:], 0.0), writes=[B_Sbf[i]])

    B_kTd = Buf("kT_d"); B_vd = Buf("v_d"); B_ogd = Buf("og_d")
    B_ok = Buf("o_k"); B_ov = Buf("o_v"); B_olf = Buf("o_lf"); B_ogla = Buf("o_gla"); B_oy = Buf("o_y")
    out_bufs += [B_ok, B_ov, B_olf, B_ogla, B_oy]
    xT_full_v = xT_full.rearrange("(kc p) t -> p kc t", p=128)

    DO_GLA = STAGE >= 2

    def load_x(ti):
        i = ti % 2
        S.dma("sp", lambda e: e.dma_start(out=xt[i][:], in_=xT_full_v[:, :, ti * 512:(ti + 1) * 512]), writes=[B_xt[i]])

    load_x(0)
    for ti in range(DBG_NT):
        i = ti % 2
        if ti + 1 < DBG_NT:
            load_x(ti + 1)
        t0 = ti * 512
        if 'r' not in DBG:
            rms_rstd(xt[i], B_xt[i], 512, sq, B_sq, rstd, B_rstd, 0)
        else:
            S.op('pool', lambda e: e.memset(rstd[:], 1.0), writes=[B_rstd])
        for kc in range(0 if 'n' in DBG else 8):
            S.op("dve", lambda e, kc=kc: e.scalar_tensor_tensor(out=xn[:, kc, :], in0=xt[i][:, kc, :], scalar=gn[:, kc:kc + 1],
                                                                in1=rstd[:], op0=ALU.mult, op1=ALU.mult),
                 reads=[B_xt[i], B_gn, B_rstd], writes=[B_xn])
        for p in range(0 if 'a' in DBG else 4):
            pb = 1 + (p % 2)
            for kc in range(8):
                S.op("pe", lambda e, kc=kc, p=p, pb=pb: e.matmul(ps[pb][:, :], lhsT=w1_bf[:, kc, W_FK + p * 128:W_FK + (p + 1) * 128],
                                                                  rhs=xn[:, kc, :], start=(kc == 0), stop=(kc == 7)),
                     reads=[B_w1, B_xn], writes=[Bps[pb]])
            j = p % 2
            if 'e' in DBG:
                continue
            S.op("act", lambda e, pb=pb, j=j: e.copy(out=kst[j][:], in_=ps[pb][:, :]), reads=[Bps[pb]], writes=[B_kst[j]])
            if 'f' not in DBG:
                S.op("dve", lambda e, pb=pb, j=j: e.tensor_copy(out=kbf[j][:], in_=ps[pb][:, :]), reads=[Bps[pb]], writes=[B_kbf[j]])
            if 'g' not in DBG:
                S.dma("sp", lambda e, p=p, j=j: e.dma_start(out=o_k[p * 128:(p + 1) * 128, t0:t0 + 512], in_=kst[j][:]),
                      reads=[B_kst[j]], writes=[B_ok])
            if 'k' not in DBG:
                S.dma("sp", lambda e, p=p, j=j: e.dma_start(out=kT_d[p * 128:(p + 1) * 128, t0:t0 + 512], in_=kbf[j][:]),
                      reads=[B_kbf[j]], writes=[B_kTd])
        vi = ti % 2
        for blk in range(0 if 'b' in DBG else 4):
            pb = 3 + (blk % 2)
            for kc in range(8):
                S.op("pe", lambda e, kc=kc, blk=blk, pb=pb: e.matmul(ps[pb][:, :], lhsT=xn[:, kc, blk * 128:(blk + 1) * 128],
                                                                      rhs=w1_bf[:, kc, W_FV:W_FV + 512], start=(kc == 0), stop=(kc == 7)),
                     reads=[B_w1, B_xn], writes=[Bps[pb]])
            j = blk % 2
            S.op("act", lambda e, pb=pb, j=j: e.copy(out=vst[j][:], in_=ps[pb][:, :]), reads=[Bps[pb]], writes=[B_vst[j]])
            S.op("dve", lambda e, pb=pb, blk=blk: e.tensor_copy(out=vaug[vi][:, :, blk, 0:64],
                                                                in_=ps[pb][:, :].rearrange("p (h d) -> p h d", d=64)),
                 reads=[Bps[pb]], writes=[B_vaug[vi]])
            S.dma("sp", lambda e, blk=blk, j=j: e.dma_start(out=o_v[t0 + blk * 128:t0 + (blk + 1) * 128, :], in_=vst[j][:]),
                  reads=[B_vst[j]], writes=[B_ov])
            for kc in range(8):
                S.op("pe", lambda e, kc=kc, blk=blk: e.matmul(ps[5][:, 0:8], lhsT=xn[:, kc, blk * 128:(blk + 1) * 128],
                                                              rhs=w1_bf[:, kc, W_FF:W_FF + 8], start=(kc == 0), stop=(kc == 7)),
                     reads=[B_w1, B_xn], writes=[Bps[5]])
            nb = ti * 4 + blk
            S.op("dve", lambda e: e.tensor_tensor(out=zff[:], in0=ps[5][:, 0:8], in1=bfr[:], op=ALU.add),
                 reads=[Bps[5], B_bfr], writes=[B_zff])
            S.op("act", lambda e: e.activation(out=zff[:], in_=zff[:], func=AF.Exp, scale=-1.0), reads=[B_zff], writes=[B_zff])
            S.op("act", lambda e: e.activation(out=zff[:], in_=zff[:], func=AF.Ln, bias=1.0), reads=[B_zff], writes=[B_zff])
            S.op("dve", lambda e, nb=nb: e.tensor_scalar(out=lf_tab[:, :, nb], in0=zff[:], scalar1=-1.0, scalar2=None, op0=ALU.mult),
                 reads=[B_zff], writes=[B_lf])
        for h in range(0 if 'c' in DBG else H):
            S.dma("sp", lambda e, h=h: e.dma_start(out=v_d[h, :, ti * 4:(ti + 1) * 4, :], in_=vaug[vi][:, h, :, :]),
                  reads=[B_vaug[vi]], writes=[B_vd])
        if not DO_GLA:
            continue
        for kc in range(8):
            S.op("pe", lambda e, kc=kc: e.matmul(ps[5][0:16, :], lhsT=w1_bf[:, kc, W_GLR:W_GLR + 16], rhs=xn[:, kc, :],
                                                  start=(kc == 0), stop=(kc == 7)), reads=[B_w1, B_xn], writes=[Bps[5]])
        S.op("act", lambda e: e.copy(out=glr_bf[:], in_=ps[5][0:16, :]), reads=[Bps[5]], writes=[B_glr])
        for p in range(2):
            for which, (wofs, dst, Bd) in enumerate([(W_GQ, gq_f, B_gq), (W_GK, gk_f, B_gk)]):
                pb = 1 + which
                for kc in range(8):
                    S.op("pe", lambda e, kc=kc, p=p, wofs=wofs, pb=pb: e.matmul(ps[pb][:, :], lhsT=w1_bf[:, kc, wofs + p * 128:wofs + (p + 1) * 128],
                                                                                 rhs=xn[:, kc, :], start=(kc == 0), stop=(kc == 7)),
                         reads=[B_w1, B_xn], writes=[Bps[pb]])
                S.op("act", lambda e, pb=pb, dst=dst, p=p: e.copy(out=dst[p][:], in_=ps[pb][:, :]), reads=[Bps[pb]], writes=[Bd[p]])
            S.op("pe", lambda e, p=p: e.matmul(ps[5][:, :], lhsT=wg_bf[:, p * 128:(p + 1) * 128], rhs=glr_bf[:], start=True, stop=True),
                 reads=[B_wg, B_glr], writes=[Bps[5]])
            S.op("act", lambda e, p=p: e.activation(out=Lb[p][:], in_=ps[5][:, :], func=AF.Exp, scale=-1.0, bias=gn[:, 25 + p:26 + p]),
                 reads=[Bps[5], B_gn], writes=[B_L[p]])
            S.op("act", lambda e, p=p: e.activation(out=Lb[p][:], in_=Lb[p][:], func=AF.Ln, bias=1.0), reads=[B_L[p]], writes=[B_L[p]])
            S.op("dve", lambda e, p=p: e.tensor_tensor_scan(out=Bc[p][:], data0=rmask[:], data1=Lb[p][:], initial=0.0,
                                                            op0=ALU.mult, op1=ALU.add), reads=[B_rmask, B_L[p]], writes=[B_Bc[p]])
            Bc3 = Bc[p][:].rearrange("p (c t) -> p c t", t=128)
            dd3 = dd[p][:].rearrange("p (c t) -> p c t", t=128)
            S.op("dve", lambda e, Bc3=Bc3, dd3=dd3: e.tensor_tensor(out=dd3, in0=Bc3, in1=Bc3[:, :, 64:65].to_broadcast([128, 4, 128]),
                                                                    op=ALU.subtract), reads=[B_Bc[p]], writes=[B_dd[p]])
            S.op("act", lambda e, p=p: e.activation(out=ee[p][:], in_=dd[p][:], func=AF.Exp, scale=-1.0 / 16), reads=[B_dd[p]], writes=[B_ee[p]])
            S.op("dve", lambda e, p=p: e.scalar_tensor_tensor(out=qt_[p][:], in0=gq_f[p][:], scalar=0.125, in1=ee[p][:], op0=ALU.mult, op1=ALU.mult),
                 reads=[B_gq[p], B_ee[p]], writes=[B_qt[p]])
            S.op("act", lambda e, p=p: e.activation(out=ee[p][:], in_=dd[p][:], func=AF.Exp, scale=1.0 / 16), reads=[B_dd[p]], writes=[B_ee[p]])
            S.op("dve", lambda e, p=p: e.tensor_tensor(out=kt_[p][:], in0=gk_f[p][:], in1=ee[p][:], op=ALU.mult),
                 reads=[B_gk[p], B_ee[p]], writes=[B_kt[p]])
            S.op("act", lambda e, p=p: e.activation(out=e3[p][:], in_=Bc[p][:], func=AF.Exp, scale=-1.0 / 16), reads=[B_Bc[p]], writes=[B_e3[p]])
            S.op("dve", lambda e, p=p: e.scalar_tensor_tensor(out=qh_[p][:], in0=gq_f[p][:], scalar=0.125, in1=e3[p][:], op0=ALU.mult, op1=ALU.mult),
                 reads=[B_gq[p], B_e3[p]], writes=[B_qh[p]])
            S.op("dve", lambda e, Bc3=Bc3, dd3=dd3: e.tensor_tensor(out=dd3, in0=Bc3, in1=Bc3[:, :, 127:128].to_broadcast([128, 4, 128]),
                                                                    op=ALU.subtract), reads=[B_Bc[p]], writes=[B_dd[p]])
            S.op("act", lambda e, p=p: e.activation(out=ee[p][:], in_=dd[p][:], func=AF.Exp, scale=1.0 / 16), reads=[B_dd[p]], writes=[B_ee[p]])
            S.op("dve", lambda e, p=p: e.tensor_tensor(out=kh_[p][:], in0=gk_f[p][:], in1=ee[p][:], op=ALU.mult),
                 reads=[B_gk[p], B_ee[p]], writes=[B_kh[p]])
        for blk in range(4):
            pb = 3 + (blk % 2)
            for kc in range(8):
                S.op("pe", lambda e, kc=kc, blk=blk, pb=pb: e.matmul(ps[pb][:, :], lhsT=xn[:, kc, blk * 128:(blk + 1) * 128],
                                                                      rhs=w1_bf[:, kc, W_GV:W_GV + 512], start=(kc == 0), stop=(kc == 7)),
                     reads=[B_w1, B_xn], writes=[Bps[pb]])
            S.op("act", lambda e, pb=pb, blk=blk: e.copy(out=gvb[blk][:], in_=ps[pb][:, :]), reads=[Bps[pb]], writes=[B_gvb[blk]])
        for c in range(4):
            cs = slice(c * 128, (c + 1) * 128)
            for p in range(2):
                S.op("pe", lambda e, p=p, cs=cs: e.transpose(out=psT_bf[:, 0:128], in_=kh_[p][:, cs], identity=ident_bf[:]),
                     reads=[B_kh[p], B_ident], writes=[B_psT])
                S.op("act", lambda e, p=p: e.copy(out=khT[p][:], in_=psT_bf[:, 0:128]), reads=[B_psT], writes=[B_khT[p]])
                for hh in range(2):
                    h = p * 2 + hh
                    rs = slice(hh * 64, (hh + 1) * 64)
                    S.op("pe", lambda e, p=p, rs=rs, cs=cs: e.matmul(ps[6][:, 0:128], lhsT=kt_[p][rs, cs], rhs=qt_[p][rs, cs], start=True, stop=True),
                         reads=[B_kt[p], B_qt[p]], writes=[Bps[6]])
                    S.op("dve", lambda e, hh=hh: e.tensor_tensor(out=AT[hh][:], in0=ps[6][:, 0:128], in1=tri_bf[:], op=ALU.mult),
                         reads=[Bps[6], B_tri], writes=[B_AT[hh]])
                    S.op("pe", lambda e, h=h, hh=hh, c=c: e.matmul(ps[2][:, 0:128], lhsT=gvb[c][:, h * 128:(h + 1) * 128], rhs=AT[hh][:], start=True, stop=False),
                         reads=[B_gvb[c], B_AT[hh]], writes=[Bps[2]])
                    S.op("pe", lambda e, p=p, rs=rs, cs=cs: e.matmul(ps[2][:, 0:128], lhsT=Sbf[p][rs, :], rhs=qh_[p][rs, cs], start=False, stop=True),
                         reads=[B_Sbf[p], B_qh[p]], writes=[Bps[2]])
                    S.op("act", lambda e, h=h, cs=cs: e.copy(out=ost[h][:, cs], in_=ps[2][:, 0:128]), reads=[Bps[2]], writes=[B_ost[h]])
                for hh in range(2):
                    h = p * 2 + hh
                    S.op("pe", lambda e, p=p, hh=hh, h=h, c=c: e.matmul(ps[5][hh * 64:(hh + 1) * 64, 0:128], lhsT=khT[p][:, hh * 64:(hh + 1) * 64],
                                                                         rhs=gvb[c][:, h * 128:(h + 1) * 128], start=True, stop=True),
                         reads=[B_khT[p], B_gvb[c]], writes=[Bps[5]])
                S.op("dve", lambda e, p=p, c=c: e.scalar_tensor_tensor(out=Sst[p][:], in0=Sst[p][:], scalar=e3[p][:, c * 128 + 127:c * 128 + 128],
                                                                      in1=ps[5][:, 0:128], op0=ALU.mult, op1=ALU.add),
                     reads=[B_Sst[p], B_e3[p], Bps[5]], writes=[B_Sst[p]])
                S.op("act", lambda e, p=p: e.copy(out=Sbf[p][:], in_=Sst[p][:]), reads=[B_Sst[p]], writes=[B_Sbf[p]])
        for h in range(GH):
            S.dma("sp", lambda e, h=h: e.dma_start(out=og_d[(ti * GH + h) * 128:(ti * GH + h + 1) * 128, :], in_=ost[h][:]),
                  reads=[B_ost[h]], writes=[B_ogd])

    if DO_GLA:
        for p in range(2):
            S.dma("sp", lambda e, p=p: e.dma_start(out=o_gla[p * 128:(p + 1) * 128, :], in_=Sst[p][:]), reads=[B_Sst[p]], writes=[B_ogla])
    S.dma("sp", lambda e: e.dma_start(out=o_lf, in_=lf_tab[:].rearrange("p h n -> p (h n)")), reads=[B_lf], writes=[B_olf])


    if STAGE >= 3:
        S.barrier()
        sb.release(m_p1)
        mix = sb.tile([128, 8, NOWN], BF16, "mix"); B_mix = Buf("mix")
        S.op("pool", lambda e: e.memset(mix[:], 0.0), writes=[B_mix])
        m_p2 = sb.mark()
        xT_own_v = xT_own.rearrange("(kc p) t -> p kc t", p=128)
        if STAGE >= 6:
            B_oks, B_ovs, B_olfs, B_oglas = Buf("o_ks"), Buf("o_vs"), Buf("o_lfs"), Buf("o_glas")
            out_bufs += [B_oks, B_ovs, B_olfs, B_oglas]
            zsT = sb.tile([128, 25, 4], F32, "zsT"); B_zsT = Buf("zsT")
            xs = sb.tile([128, 8, 4], F32, "xs"); B_xs = Buf("xs")
            sqs = sb.tile([128, 8, 4], BF16, "sqs"); B_sqs = Buf("sqs")
            rss = sb.tile([128, 4], F32, "rss"); B_rss = Buf("rss")
            xns = sb.tile([128, 8, 4], F32, "xns"); B_xns = Buf("xns")
            fq_tm = sb.tile([4, 512], F32, "fq_tm"); B_fq_tm = Buf("fq_tm")
            fk_tm = sb.tile([4, 512], F32, "fk_tm"); B_fk_tm = Buf("fk_tm")
            fv_tm = sb.tile([4, 512], F32, "fv_tm"); B_fv_tm = Buf("fv_tm")
            gv_tm = sb.tile([4, 512], F32, "gv_tm"); B_gv_tm = Buf("gv_tm")
            lfs_tm = sb.tile([4, 8], F32, "lfs_tm"); B_lfs = Buf("lfs_tm")
            sel4 = sb.tile([4, 4, 128], F32, "sel4"); B_sel4 = Buf("sel4")
            bm = sb.tile([8, 8, 64], F32, "bm"); B_bm = Buf("bm")
            SL_f = sb.tile([128, 128], F32, "SL_f"); B_SL = Buf("SL")
            m128 = sb.tile([128, 1024], F32, "m128"); B_m128 = Buf("m128")
            pt_t = sb.tile([128, 4], I32, "pt_t"); B_pt = Buf("pt")
            Otm = sb.tile([4, 512], F32, "Otm"); B_Otm = Buf("Otm")
            dentm = sb.tile([4, 8], F32, "dentm"); B_dentm = Buf("dentm")
            S.op("dve", lambda e: e.tensor_copy(out=sel4[:], in_=ident_f[0:4, 0:4].unsqueeze(2).to_broadcast([4, 4, 128])), reads=[B_ident], writes=[B_sel4])
            S.op("dve", lambda e: e.tensor_copy(out=bm[:], in_=ident_f[0:8, 0:8].unsqueeze(2).to_broadcast([8, 8, 64])), reads=[B_ident], writes=[B_bm])
            S.op("dve", lambda e: e.tensor_tensor(out=SL_f[:], in0=ones_f[:], in1=U_f[:], op=ALU.subtract), reads=[B_ones_f, B_U], writes=[B_SL])
            S.op("pool", lambda e: e.memset(m128[:], 1.0), writes=[B_m128])
            S.op("pool", lambda e: e.memset(m128[:].rearrange("p (h i) -> p h i", i=128)[:, :, 0:1], 0.0), writes=[B_m128])
            S.dma("sp", lambda e: e.dma_start(out=pt_t[:], in_=ptab), writes=[B_pt])
            TOPBASE = 229312 - 100 * 1024
            sbt = SbAlloc(nc, base=TOPBASE, top=229312, prefix="T")
            win_f = sbt.tile([128, 8, 3096], F32, "win_f"); B_win = Buf("win_f")
            for kc in range(8):
                S.dma("sp", lambda e: e.dma_start(out=win_f[:, kc, :], in_=w_s[kc * 128:(kc + 1) * 128, :]), writes=[B_win])
            S.dma("sp", lambda e: e.dma_start(out=xs[:], in_=xT_own_v[:, :, 2048:2052]), writes=[B_xs])
            rms_rstd(xs, B_xs, 4, sqs, B_sqs, rss, B_rss, 0)
            for kc in range(8):
                S.op("dve", lambda e: e.scalar_tensor_tensor(out=xns[:, kc, :], in0=xs[:, kc, :], scalar=gn[:, kc:kc + 1], in1=rss[:],
                                                             op0=ALU.mult, op1=ALU.mult), reads=[B_xs, B_gn, B_rss], writes=[B_xns])
            for mc in range(25):
                mcols = 128 if mc < 24 else 24
                for kc in range(8):
                    S.op("pe", lambda e: e.matmul(ps[0][0:mcols, mc * 4:(mc + 1) * 4], lhsT=win_f[:, kc, mc * 128:mc * 128 + mcols], rhs=xns[:, kc, :],
                                                  start=(kc == 0), stop=(kc == 7)), reads=[B_win, B_xns], writes=[Bps[0]])
            S.op("act", lambda e: e.copy(out=zsT[:, 0:24, :], in_=ps[0][:, 0:96].rearrange("p (m t) -> p m t", t=4)), reads=[Bps[0]], writes=[B_zsT])
            S.op("act", lambda e: e.copy(out=zsT[0:24, 24, :], in_=ps[0][0:24, 96:100]), reads=[Bps[0]], writes=[B_zsT])
            def proj_tm(c0, n, dst, Bd, pb):
                for kc in range(8):
                    S.op("pe", lambda e: e.matmul(ps[pb][0:4, 0:n], lhsT=xns[:, kc, :], rhs=win_f[:, kc, c0:c0 + n], start=(kc == 0), stop=(kc == 7)),
                         reads=[B_win, B_xns], writes=[Bps[pb]])
                if dst is not None:
                    S.op("act", lambda e: e.copy(out=dst[:, 0:n], in_=ps[pb][0:4, 0:n]), reads=[Bps[pb]], writes=[Bd])
            proj_tm(0, 512, fq_tm, B_fq_tm, 1)
            proj_tm(512, 512, fk_tm, B_fk_tm, 2)
            proj_tm(1024, 512, fv_tm, B_fv_tm, 1)
            proj_tm(2048, 512, gv_tm, B_gv_tm, 2)
            proj_tm(3088, 8, None, None, 1)
            S.op("dve", lambda e: e.tensor_tensor(out=lfs_tm[:], in0=ps[1][0:4, 0:8], in1=bfr[0:4, :], op=ALU.add), reads=[Bps[1], B_bfr], writes=[B_lfs])
            S.op("act", lambda e: e.activation(out=lfs_tm[:], in_=lfs_tm[:], func=AF.Exp, scale=-1.0), reads=[B_lfs], writes=[B_lfs])
            S.op("act", lambda e: e.activation(out=lfs_tm[:], in_=lfs_tm[:], func=AF.Ln, bias=1.0), reads=[B_lfs], writes=[B_lfs])
            S.op("dve", lambda e: e.tensor_scalar(out=lfs_tm[:], in0=lfs_tm[:], scalar1=-1.0, scalar2=None, op0=ALU.mult), reads=[B_lfs], writes=[B_lfs])
            S.dma("sp", lambda e: e.dma_start(out=o_ks, in_=fk_tm[:]), reads=[B_fk_tm], writes=[B_oks])
            S.dma("sp", lambda e: e.dma_start(out=o_vs, in_=fv_tm[:]), reads=[B_fv_tm], writes=[B_ovs])
            S.dma("sp", lambda e: e.dma_start(out=o_lfs, in_=lfs_tm[:]), reads=[B_lfs], writes=[B_olfs])
            Sin = sb.tile([128, 4, 128], F32, "Sin"); B_Sin = Buf("Sin")
            Snew = sb.tile([128, 4, 128], F32, "Snew"); B_Snew = Buf("Snew")
            alT = sb.tile([128, 4], F32, "alT"); B_alT = Buf("alT")
            kvt = sb.tile([128, 128], F32, "kvt"); B_kvt = Buf("kvt")
            ogs = sb.tile([128, 1, 16], F32, "ogs"); B_ogs = Buf("ogs")
            for p in range(2):
                S.dma("sp", lambda e: e.dma_start(out=Sin[:], in_=sgla[p]), writes=[B_Sin])
                S.op("pe", lambda e: e.matmul(ps[5][:, 0:4], lhsT=wg_f[:, p * 128:(p + 1) * 128],
                                              rhs=zsT[0:16, 24, :], start=True, stop=True), reads=[B_wg, B_zsT], writes=[Bps[5]])
                S.op("act", lambda e: e.activation(out=alT[:], in_=ps[5][:, 0:4], func=AF.Exp, scale=-1.0, bias=gn[:, 25 + p:26 + p]),
                     reads=[Bps[5], B_gn], writes=[B_alT])
                S.op("act", lambda e: e.activation(out=alT[:], in_=alT[:], func=AF.Ln, bias=1.0), reads=[B_alT], writes=[B_alT])
                S.op("act", lambda e: e.activation(out=alT[:], in_=alT[:], func=AF.Exp, scale=-1.0 / 16), reads=[B_alT], writes=[B_alT])
                for bb in range(4):
                    for hh in range(2):
                        h = 2 * p + hh
                        S.op("pe", lambda e: e.matmul(ps[4][hh * 64:(hh + 1) * 64, 0:128], lhsT=sel4[:, bb, 0:64], rhs=gv_tm[:, h * 128:(h + 1) * 128],
                                                      start=True, stop=True), reads=[B_sel4, B_gv_tm], writes=[Bps[4]])
                    S.op("dve", lambda e: e.tensor_scalar(out=kvt[:], in0=ps[4][:, 0:128], scalar1=zsT[:, 14 + p, bb:bb + 1], scalar2=None, op0=ALU.mult),
                         reads=[Bps[4], B_zsT], writes=[B_kvt])
                    S.op("dve", lambda e: e.scalar_tensor_tensor(out=Snew[:, bb, :], in0=Sin[:, bb, :], scalar=alT[:, bb:bb + 1], in1=kvt[:],
                                                                 op0=ALU.mult, op1=ALU.add), reads=[B_Sin, B_alT, B_kvt], writes=[B_Snew])
                    for hh in range(2):
                        h = 2 * p + hh
                        rs = slice(hh * 64, (hh + 1) * 64)
                        S.op("pe", lambda e: e.matmul(ps[6][:, h * 4 + bb:h * 4 + bb + 1], lhsT=Snew[rs, bb, :], rhs=zsT[rs, 12 + p, bb:bb + 1],
                                                      start=True, stop=True), reads=[B_Snew, B_zsT], writes=[Bps[6]])
                S.dma("sp", lambda e: e.dma_start(out=o_glas[p], in_=Snew[:]), reads=[B_Snew], writes=[B_oglas])
            S.op("act", lambda e: e.mul(out=ogs[:, 0, :], in_=ps[6][:, 0:16], mul=0.125), reads=[Bps[6]], writes=[B_ogs])
            ogq = sb.tile([128, 1, 16], BF16, "ogq"); B_ogq = Buf("ogq")
            rso = sb.tile([128, 16], F32, "rso"); B_rso = Buf("rso")
            sgs = sb.tile([128, 4, 4], F32, "sgs"); B_sgs = Buf("sgs")
            t1s = sb.tile([128, 4, 4], F32, "t1s"); B_t1s = Buf("t1s")
            rms_rstd(ogs, B_ogs, 16, ogq, B_ogq, rso, B_rso, 5, nfeat=1, dim=128.0)
            S.op("act", lambda e: e.activation(out=sgs[:], in_=zsT[:, 20:24, :], func=AF.Exp, scale=-1.0), reads=[B_zsT], writes=[B_sgs])
            S.op("dve", lambda e: e.tensor_scalar(out=sgs[:], in0=sgs[:], scalar1=1.0, scalar2=None, op0=ALU.add), reads=[B_sgs], writes=[B_sgs])
            S.op("dve", lambda e: e.reciprocal(out=sgs[:], in_=sgs[:]), reads=[B_sgs], writes=[B_sgs])
            S.op("dve", lambda e: e.tensor_tensor(out=sgs[:], in0=zsT[:, 20:24, :], in1=sgs[:], op=ALU.mult), reads=[B_zsT, B_sgs], writes=[B_sgs])
            S.op("dve", lambda e: e.scalar_tensor_tensor(out=t1s[:].rearrange("p h b -> p (h b)"), in0=ogs[:, 0, :], scalar=gn[:, 24:25], in1=rso[:],
                                                         op0=ALU.mult, op1=ALU.mult), reads=[B_ogs, B_gn, B_rso], writes=[B_t1s])
            S.op("dve", lambda e: e.tensor_tensor(out=mix[:, 4:8, 2048:2052], in0=t1s[:], in1=sgs[:], op=ALU.mult), reads=[B_t1s, B_sgs], writes=[B_mix])
            qk = sb.tile([4, 8, 64], F32, "qk"); B_qk = Buf("qk")
            enew = sb.tile([4, 8], F32, "enew"); B_enew = Buf("enew")
            S.op("dve", lambda e: e.tensor_tensor(out=qk[:].rearrange("p h d -> p (h d)"), in0=fq_tm[:], in1=fk_tm[:], op=ALU.mult),
                 reads=[B_fq_tm, B_fk_tm], writes=[B_qk])
            S.op("dve", lambda e: e.tensor_reduce(out=enew[:], in_=qk[:], axis=AX.X, op=ALU.add), reads=[B_qk], writes=[B_enew])
            S.op("act", lambda e: e.activation(out=enew[:], in_=enew[:], func=AF.Exp, scale=0.125), reads=[B_enew], writes=[B_enew])
            qb = sb.tile([128, 4, 512], F32, "qb"); B_qb = Buf("qb")
            for bb in range(4):
                S.op("pe", lambda e: e.matmul(ps[2][:, :], lhsT=sel4[:, bb, :], rhs=fq_tm[:], start=True, stop=True), reads=[B_sel4, B_fq_tm], writes=[Bps[2]])
                S.op("act", lambda e: e.mul(out=qb[:, bb, :], in_=ps[2][:, :], mul=0.125), reads=[Bps[2]], writes=[B_qb])
            lfb = sb.tile([128, 4, 8], F32, "lfb"); B_lfb = Buf("lfb")
            for bb in range(4):
                S.op("pe", lambda e: e.matmul(ps[2][:, 0:8], lhsT=sel4[:, bb, :], rhs=lfs_tm[:], start=True, stop=True), reads=[B_sel4, B_lfs], writes=[Bps[2]])
                S.op("act", lambda e: e.copy(out=lfb[:, bb, :], in_=ps[2][:, 0:8]), reads=[Bps[2]], writes=[B_lfb])
            S.barrier()
        m64 = sb.tile([128, 512], F32, "m64"); B_m64 = Buf("m64")
        Iscan = sb.tile([128, 512], F32, "Iscan"); B_I = Buf("I")
        S.op("pool", lambda e: e.memset(m64[:], 1.0), writes=[B_m64])
        S.op("pool", lambda e: e.memset(m64[:].rearrange("p (h n) -> p h n", n=64)[:, :, 0:1], 0.0), writes=[B_m64])
        lf2 = lf_tab[:].rearrange("p h n -> p (h n)")
        S.op("dve", lambda e: e.tensor_tensor_scan(out=Iscan[:], data0=m64[:], data1=lf2, initial=0.0, op0=ALU.mult, op1=ALU.add),
             reads=[B_m64, B_lf], writes=[B_I])
        S.op("dve", lambda e: e.tensor_tensor(out=Iscan[:], in0=Iscan[:], in1=lf2, op=ALU.subtract), reads=[B_I, B_lf], writes=[B_I])
        S.op("pe", lambda e: e.matmul(ps[0][:, :], lhsT=U_f[:], rhs=lf2, start=True, stop=False), reads=[B_U, B_lf], writes=[Bps[0]])
        S.op("pe", lambda e: e.matmul(ps[0][:, :], lhsT=ones_f[:], rhs=Iscan[:], start=False, stop=True), reads=[B_ones_f, B_I], writes=[Bps[0]])
        S.op("act", lambda e: e.copy(out=C_tab[:].rearrange("p h n -> p (h n)"), in_=ps[0][:, :]), reads=[Bps[0]], writes=[B_C])
        oh = sb.tile([128, 4, 64], F32, "oh"); B_oh = Buf("oh")
        S.dma("sp", lambda e: e.dma_start(out=oh[:], in_=onehot.rearrange("s p n -> p s n")), writes=[B_oh])
        cref = sb.tile([128, 4, H], F32, "cref"); B_cref = Buf("cref")
        ctmp = sb.tile([128, H, NB], F32, "ctmp"); B_ctmp = Buf("ctmp")
        cpart = sb.tile([128, H], F32, "cpart"); B_cpart = Buf("cpart")
        for s_ in range(4):
            S.op("dve", lambda e: e.tensor_tensor(out=ctmp[:], in0=C_tab[:], in1=oh[:, s_:s_ + 1, :].to_broadcast([128, H, NB]), op=ALU.mult),
                 reads=[B_C, B_oh], writes=[B_ctmp])
            S.op("dve", lambda e: e.tensor_reduce(out=cpart[:], in_=ctmp[:], axis=AX.X, op=ALU.add), reads=[B_ctmp], writes=[B_cpart])
            S.op("pe", lambda e: e.matmul(ps[0][:, 0:H], lhsT=ones_f[:], rhs=cpart[:], start=True, stop=True), reads=[B_ones_f, B_cpart], writes=[Bps[0]])
            S.op("act", lambda e: e.copy(out=cref[:, s_, :], in_=ps[0][:, 0:H]), reads=[Bps[0]], writes=[B_cref])
        qown = [sb.tile([128, 2048], BF16, "qown%d" % p) for p in range(4)]; B_qown = [Buf("qown%d" % p) for p in range(4)]
        m_proj = sb.mark()
        w2_bf = sb.tile([128, 8, 1024], BF16, "w2_bf"); B_w2 = Buf("w2")
        wst2 = [sb.tile([128, 1024], F32, "wst2_%d" % i) for i in range(2)]; B_wst2 = [Buf("wst2_0"), Buf("wst2_1")]
        for kc in range(8):
            i = kc % 2
            S.dma("sp", lambda e: e.dma_start(out=wst2[i][:], in_=w2[kc * 128:(kc + 1) * 128, :]), writes=[B_wst2[i]])
            if i == 0:
                S.op("dve", lambda e: e.tensor_copy(out=w2_bf[:, kc, :], in_=wst2[i][:]), reads=[B_wst2[i]], writes=[B_w2])
            else:
                S.op("act", lambda e: e.copy(out=w2_bf[:, kc, :], in_=wst2[i][:]), reads=[B_wst2[i]], writes=[B_w2])
        xo = sb.tile([128, 8, 512], F32, "xo"); B_xo = Buf("xo")
        sq2 = sb.tile([128, 8, 512], BF16, "sq2"); B_sq2 = Buf("sq2")
        xn2 = sb.tile([128, 8, 512], BF16, "xn2"); B_xn2 = Buf("xn2")
        rstd2 = sb.tile([128, 512], F32, "rstd2"); B_rstd2 = Buf("rstd2")
        og2 = [sb.tile([128, 512], BF16, "og%d" % i) for i in range(2)]; B_og2 = [Buf("og0"), Buf("og1")]
        ogsq = sb.tile([128, 1, 512], BF16, "ogsq"); B_ogsq = Buf("ogsq")
        rstdo = sb.tile([128, 512], F32, "rstdo"); B_rstdo = Buf("rstdo")
        sg = sb.tile([128, 512], F32, "sg"); B_sg = Buf("sg")
        t1 = sb.tile([128, 512], F32, "t1"); B_t1 = Buf("t1")
        idxo = sb.tile([128, 16], I32, "idxo"); B_idxo = Buf("idxo")
        S.dma("sp", lambda e: e.dma_start(out=idxo[:], in_=idx_o), writes=[B_idxo])
        for s_ in range(4):
            cs = slice(s_ * 512, (s_ + 1) * 512)
            S.dma("sp", lambda e: e.dma_start(out=xo[:], in_=xT_own_v[:, :, cs]), writes=[B_xo])
            rms_rstd(xo, B_xo, 512, sq2, B_sq2, rstd2, B_rstd2, 0)
            for kc in range(8):
                S.op("dve", lambda e: e.scalar_tensor_tensor(out=xn2[:, kc, :], in0=xo[:, kc, :], scalar=gn[:, kc:kc + 1],
                                                             in1=rstd2[:], op0=ALU.mult, op1=ALU.mult),
                     reads=[B_xo, B_gn, B_rstd2], writes=[B_xn2])
            for p in range(4):
                pb = 1 + (p % 2)
                for kc in range(8):
                    S.op("pe", lambda e: e.matmul(ps[pb][:, :], lhsT=w2_bf[:, kc, p * 128:(p + 1) * 128], rhs=xn2[:, kc, :],
                                                  start=(kc == 0), stop=(kc == 7)), reads=[B_w2, B_xn2], writes=[Bps[pb]])
                S.op("act", lambda e: e.mul(out=qown[p][:, cs], in_=ps[pb][:, :], mul=0.125), reads=[Bps[pb]], writes=[B_qown[p]])
            if STAGE >= 4:
                for h in range(GH):
                    pb = 3 + (h % 2)
                    for kc in range(8):
                        S.op("pe", lambda e: e.matmul(ps[pb][:, :], lhsT=w2_bf[:, kc, 512 + h * 128:512 + (h + 1) * 128], rhs=xn2[:, kc, :],
                                                      start=(kc == 0), stop=(kc == 7)), reads=[B_w2, B_xn2], writes=[Bps[pb]])
                    S.dma("pool", lambda e: e.indirect_dma_start(out=og2[h % 2][:], out_offset=None, in_=og_d,
                                                                  in_offset=bass.IndirectOffsetOnAxis(ap=idxo[:, s_ * 4 + h:s_ * 4 + h + 1], axis=0)),
                          reads=[B_ogd, B_idxo], writes=[B_og2[h % 2]])
                    rms_rstd(og2[h % 2][:].rearrange("p (o n) -> p o n", o=1), B_og2[h % 2], 512, ogsq, B_ogsq, rstdo, B_rstdo, 5, nfeat=1, dim=128.0)
                    S.op("act", lambda e: e.activation(out=sg[:], in_=ps[pb][:, :], func=AF.Exp, scale=-1.0), reads=[Bps[pb]], writes=[B_sg])
                    S.op("dve", lambda e: e.tensor_scalar(out=sg[:], in0=sg[:], scalar1=1.0, scalar2=None, op0=ALU.add), reads=[B_sg], writes=[B_sg])
                    S.op("dve", lambda e: e.reciprocal(out=sg[:], in_=sg[:]), reads=[B_sg], writes=[B_sg])
                    S.op("dve", lambda e: e.tensor_tensor(out=sg[:], in0=ps[pb][:, :], in1=sg[:], op=ALU.mult), reads=[Bps[pb], B_sg], writes=[B_sg])
                    S.op("dve", lambda e: e.scalar_tensor_tensor(out=t1[:], in0=og2[h % 2][:], scalar=gn[:, 24:25], in1=rstdo[:], op0=ALU.mult, op1=ALU.mult),
                         reads=[B_og2[h % 2], B_gn, B_rstdo], writes=[B_t1])
                    S.op("dve", lambda e: e.tensor_tensor(out=mix[:, 4 + h, cs], in0=t1[:], in1=sg[:], op=ALU.mult), reads=[B_t1, B_sg], writes=[B_mix])
        S.barrier()
        sb.release(m_proj)
        if STAGE >= 6:
            CKT = 4
            NCK = 128 // CKT
            TOP2 = 229312 - 49 * 1024
            sbt2 = SbAlloc(nc, base=TOP2, top=229312, prefix="U")
            kc_t = [sbt2.tile([128, CKT * 512], F32, "kc_t%d" % i) for i in range(2)]; B_kc = [Buf("kc0"), Buf("kc1")]
            vc_t = [sbt2.tile([128, CKT * 512], F32, "vc_t%d" % i) for i in range(2)]; B_vc = [Buf("vc0"), Buf("vc1")]
            tmpk = sbt2.tile([128, CKT * 512], F32, "tmpk"); B_tmpk = Buf("tmpk")
            lfp = tmpk[:, 0:1024]; B_lfp = B_tmpk
            Ipre = sbt2.tile([128, 8, 128], F32, "Ipre"); B_Ipre = Buf("Ipre")
            scs = sbt2.tile([128, 128, 8], F32, "scs"); B_scs = Buf("scs")
            ps7 = psT_bf[:].bitcast(F32); Bps7 = B_psT
            totc = sb.tile([128, 8], F32, "totc"); B_totc = Buf("totc")
            tb = sb.tile([128, 8], F32, "tb"); B_tb = Buf("tb")
            dsum = sb.tile([128, 8], F32, "dsum"); B_dsum = Buf("dsum")
            tmpd = sb.tile([8, 8, 64], F32, "tmpd"); B_tmpd = Buf("tmpd")
            Od = sb.tile([8, 64], F32, "Od"); B_Od = Buf("Od")
            den8 = sb.tile([8, 1], F32, "den8"); B_den8 = Buf("den8")
            scr_o = dscr("scr_o", [4, 512], F32); B_scro = Buf("scr_o")
            scr_d = dscr("scr_d", [4, 8], F32); B_scrd = Buf("scr_d")
            idxk = sb.tile([128, 4, NCK], I32, "idxk"); B_idxk = Buf("idxk")
            for ck in range(NCK):
                S.op("dve", lambda e: e.tensor_scalar(out=idxk[:, :, ck], in0=pt_t[:], scalar1=float(NCK), scalar2=float(ck), op0=ALU.mult, op1=ALU.add),
                     reads=[B_pt], writes=[B_idxk])

            def decode_gen():
              for bb in range(4):
                  S.dma("pool", lambda e: e.indirect_dma_start(out=lfp, out_offset=None, in_=cache_lf,
                                                                in_offset=bass.IndirectOffsetOnAxis(ap=pt_t[:, bb:bb + 1], axis=0)),
                        reads=[B_pt], writes=[B_lfp])
                  S.op("dve", lambda e: e.tensor_tensor_scan(out=Ipre[:].rearrange("p h i -> p (h i)"), data0=m128[:], data1=lfp, initial=0.0,
                                                             op0=ALU.mult, op1=ALU.add), reads=[B_m128, B_lfp], writes=[B_Ipre])
                  S.op("dve", lambda e: e.tensor_copy(out=totc[:], in_=Ipre[:, :, 127]), reads=[B_Ipre], writes=[B_totc])
                  S.op("pe", lambda e: e.matmul(ps[0][:, 0:8], lhsT=SL_f[:], rhs=totc[:], start=True, stop=True), reads=[B_SL, B_totc], writes=[Bps[0]])
                  S.op("dve", lambda e: e.tensor_tensor(out=tb[:], in0=ps[0][:, 0:8], in1=totc[:], op=ALU.add), reads=[Bps[0], B_totc], writes=[B_tb])
                  S.op("dve", lambda e: e.tensor_tensor(out=tb[:], in0=tb[:], in1=lfb[:, bb, :], op=ALU.add), reads=[B_tb, B_lfb], writes=[B_tb])
                  S.op("dve", lambda e: e.tensor_tensor(out=Ipre[:], in0=tb[:].unsqueeze(2).to_broadcast([128, 8, 128]), in1=Ipre[:], op=ALU.subtract),
                       reads=[B_tb, B_Ipre], writes=[B_Ipre])
                  for ck in range(NCK):
                      i = ck % 2
                      S.dma("pool", lambda e: e.indirect_dma_start(out=kc_t[i][:], out_offset=None, in_=cache_k2,
                                                                    in_offset=bass.IndirectOffsetOnAxis(ap=idxk[:, bb, ck:ck + 1], axis=0)),
                            reads=[B_idxk], writes=[B_kc[i]])
                      S.op("dve", lambda e: e.tensor_tensor(out=tmpk[:].rearrange("p (t f) -> p t f", f=512), in0=kc_t[i][:].rearrange("p (t f) -> p t f", f=512),
                                                            in1=qb[:, bb:bb + 1, :].to_broadcast([128, CKT, 512]), op=ALU.mult),
                           reads=[B_kc[i], B_qb], writes=[B_tmpk])
                      S.op("dve", lambda e: e.tensor_reduce(out=scs[:, ck * CKT:(ck + 1) * CKT, :].rearrange("p t h -> p (t h)"),
                                                            in_=tmpk[:].rearrange("p (g d) -> p g d", d=64), axis=AX.X, op=ALU.add),
                           reads=[B_tmpk], writes=[B_scs])
                      yield
                  S.op("dve", lambda e: e.tensor_tensor(out=scs[:], in0=scs[:], in1=Ipre[:].rearrange("p h i -> p i h"), op=ALU.add),
                       reads=[B_scs, B_Ipre], writes=[B_scs])
                  S.op("act", lambda e: e.activation(out=scs[:], in_=scs[:], func=AF.Exp), reads=[B_scs], writes=[B_scs])
                  S.op("dve", lambda e: e.tensor_reduce(out=dsum[:], in_=scs[:].rearrange("p i h -> p h i"), axis=AX.X, op=ALU.add), reads=[B_scs], writes=[B_dsum])
                  for ck in range(NCK):
                      i = ck % 2
                      S.dma("pool", lambda e: e.indirect_dma_start(out=vc_t[i][:], out_offset=None, in_=cache_v2,
                                                                    in_offset=bass.IndirectOffsetOnAxis(ap=idxk[:, bb, ck:ck + 1], axis=0)),
                            reads=[B_idxk], writes=[B_vc[i]])
                      for il in range(CKT):
                          ii = ck * CKT + il
                          S.op("pe", lambda e: e.matmul(ps7[0:8, :], lhsT=scs[:, ii, :], rhs=vc_t[i][:, il * 512:(il + 1) * 512],
                                                        start=(ii == 0), stop=(ii == NCK * CKT - 1)), reads=[B_scs, B_vc[i]], writes=[Bps7])
                      yield
                  S.op("pe", lambda e: e.matmul(ps[0][0:8, 8:9], lhsT=dsum[:], rhs=ones_f[:, 0:1], start=True, stop=True), reads=[B_dsum, B_ones_f], writes=[Bps[0]])
                  S.op("dve", lambda e: e.tensor_tensor(out=tmpd[:].rearrange("p a d -> p (a d)"), in0=ps7[0:8, :], in1=bm[:].rearrange("p a d -> p (a d)"), op=ALU.mult),
                       reads=[Bps7, B_bm], writes=[B_tmpd])
                  S.op("dve", lambda e: e.tensor_reduce(out=Od[:], in_=tmpd[:].rearrange("p a d -> p d a"), axis=AX.X, op=ALU.add), reads=[B_tmpd], writes=[B_Od])
                  S.op("act", lambda e: e.copy(out=den8[:], in_=ps[0][0:8, 8:9]), reads=[Bps[0]], writes=[B_den8])
                  S.dma("sp", lambda e: e.dma_start(out=scr_o[bb:bb + 1, :].rearrange("o (h d) -> (o h) d", d=64), in_=Od[:]), reads=[B_Od], writes=[B_scro])
                  S.dma("sp", lambda e: e.dma_start(out=scr_d[bb:bb + 1, :].rearrange("o h -> h o"), in_=den8[:]), reads=[B_den8], writes=[B_scrd])
        kts = [sb.tile([128, SEQ], BF16, "kts%d" % i) for i in range(1)]; B_kts = [Buf("kts0")]
        vts = [sb.tile([128, NB, 65], BF16, "vts%d" % i) for i in range(2)]; B_vts = [Buf("vts0"), Buf("vts1")]
        mk = sb.tile([128, 16, 512], BF16, "mk"); B_mk = Buf("mk")
        bias_t = [sb.tile([128, NB], F32, "bias%d" % i) for i in range(2)]; B_bias = [Buf("bias0"), Buf("bias1")]
        NPT = 4
        Pt = [sb.tile([128, 512], BF16, "Pt%d" % i) for i in range(NPT)]; B_Pt = [Buf("Pt%d" % i) for i in range(NPT)]
        rcs = sb.tile([128, 512], F32, "rcs"); B_rcs = Buf("rcs")
        Osb = sb.tile([64, 512], F32, "Osb"); B_Osb = Buf("Osb")
        dec_it = decode_gen() if STAGE >= 6 else iter(())
        DEC_EVERY = 2
        SBANK = [1, 2, 4]
        OBANK = [3, 6]
        LA = 2
        unit = 0
        hcount = 0
        pcount = 0
        mb_t = sb.tile([128, 64], F32, "mb_t"); B_mb = Buf("mb")
        S.dma("sp", lambda e: e.dma_start(out=mb_t[:], in_=mbias), writes=[B_mb])
        for s_ in range(4):
            nk = 16 * s_ + 16
            cs = slice(s_ * 512, (s_ + 1) * 512)
            S.dma("sp", lambda e: e.dma_start(out=mk[:], in_=masks[s_].rearrange("j p q -> p j q")), writes=[B_mk])
            for p in range(4):
                kb_ = pcount % len(kts)
                pcount += 1
                S.dma("sp", lambda e: e.dma_start(out=kts[kb_][:, 0:nk * 128], in_=kT_d[p * 128:(p + 1) * 128, 0:nk * 128]),
                      reads=[B_kTd], writes=[B_kts[kb_]])
                for hh in range(2):
                    h = 2 * p + hh
                    vb = hcount % 2
                    ob = OBANK[hcount % 2]
                    hcount += 1
                    rs = slice(hh * 64, (hh + 1) * 64)
                    S.dma("sp", lambda e: e.dma_start(out=vts[vb][:, 0:nk, :], in_=v_d[h, :, 0:nk, :]), reads=[B_vd], writes=[B_vts[vb]])
                    S.op("dve", lambda e: e.tensor_scalar(out=bias_t[vb][:, 0:nk], in0=C_tab[:, h, 0:nk], scalar1=-1.0,
                                                          scalar2=cref[:, s_, h:h + 1], op0=ALU.mult, op1=ALU.add),
                         reads=[B_C, B_cref], writes=[B_bias[vb]])
                    S.op("dve", lambda e: e.tensor_tensor(out=bias_t[vb][:, 16 * s_:16 * s_ + 16], in0=bias_t[vb][:, 16 * s_:16 * s_ + 16],
                                                          in1=mb_t[:, 16 * s_:16 * s_ + 16], op=ALU.add),
                         reads=[B_bias[vb], B_mb], writes=[B_bias[vb]])
                    slot_of = {}
                    for n2 in range(nk + LA):
                        if n2 < nk:
                            n = n2
                            pb = SBANK[unit % 3]
                            pt = unit % NPT
                            unit += 1
                            if unit % DEC_EVERY == 0:
                                next(dec_it, None)
                            slot_of[n] = (pb, pt)
                            S.op("pe", lambda e: e.matmul(ps[pb][:, :], lhsT=kts[kb_][rs, n * 128:(n + 1) * 128], rhs=qown[p][rs, cs], start=True, stop=True),
                                 reads=[B_kts[kb_], B_qown[p]], writes=[Bps[pb]])
                        if n2 - LA >= 0:
                            n = n2 - LA
                            pb, pt = slot_of[n]
                            S.op("act", lambda e: e.activation(out=Pt[pt][:], in_=ps[pb][:, :], func=AF.Exp, bias=bias_t[vb][:, n:n + 1], scale=1.0),
                                 reads=[Bps[pb], B_bias[vb]], writes=[B_Pt[pt]])
                            if n >= 16 * s_:
                                S.op("dve", lambda e: e.tensor_tensor(out=Pt[pt][:], in0=Pt[pt][:], in1=mk[:, n - 16 * s_, :], op=ALU.mult),
                                     reads=[B_Pt[pt], B_mk], writes=[B_Pt[pt]])
                            S.op("pe", lambda e: e.matmul(ps[ob][0:65, :], lhsT=vts[vb][:, n, :], rhs=Pt[pt][:], start=(n == 0), stop=(n == nk - 1)),
                                 reads=[B_vts[vb], B_Pt[pt]], writes=[Bps[ob]])
                    S.op("dve", lambda e: e.reciprocal(out=rcs[64:65, :], in_=ps[ob][64:65, :]), reads=[Bps[ob]], writes=[B_rcs])
                    S.op("act", lambda e: e.copy(out=Osb[:], in_=ps[ob][0:64, :]), reads=[Bps[ob]], writes=[B_Osb])
                    S.op("pe", lambda e: e.matmul(ps[5][0:64, :], lhsT=ones_f[64:65, 0:64], rhs=rcs[64:65, :], start=True, stop=True),
                         reads=[B_ones_f, B_rcs], writes=[Bps[5]])
                    S.op("dve", lambda e: e.tensor_tensor(out=mix[rs, p, cs], in0=Osb[:], in1=ps[5][0:64, :], op=ALU.mult),
                         reads=[B_Osb, Bps[5]], writes=[B_mix])

        if STAGE >= 6:
            for _ in dec_it:
                pass
            S.dma("sp", lambda e: e.dma_start(out=Otm[:], in_=scr_o), reads=[B_scro], writes=[B_Otm])
            S.dma("sp", lambda e: e.dma_start(out=dentm[:], in_=scr_d), reads=[B_scrd], writes=[B_dentm])
            S.op("dve", lambda e: e.tensor_tensor(out=qk[:], in0=fv_tm[:].rearrange("p (h d) -> p h d", d=64), in1=enew[:].unsqueeze(2).to_broadcast([4, 8, 64]), op=ALU.mult),
                 reads=[B_fv_tm, B_enew], writes=[B_qk])
            S.op("dve", lambda e: e.tensor_tensor(out=Otm[:], in0=Otm[:], in1=qk[:].rearrange("p h d -> p (h d)"), op=ALU.add), reads=[B_Otm, B_qk], writes=[B_Otm])
            S.op("dve", lambda e: e.tensor_tensor(out=dentm[:], in0=dentm[:], in1=enew[:], op=ALU.add), reads=[B_dentm, B_enew], writes=[B_dentm])
            S.op("dve", lambda e: e.reciprocal(out=dentm[:], in_=dentm[:]), reads=[B_dentm], writes=[B_dentm])
            S.op("dve", lambda e: e.tensor_tensor(out=Otm[:].rearrange("p (h d) -> p h d", d=64), in0=Otm[:].rearrange("p (h d) -> p h d", d=64),
                                                  in1=dentm[:].unsqueeze(2).to_broadcast([4, 8, 64]), op=ALU.mult), reads=[B_Otm, B_dentm], writes=[B_Otm])
            for pr in range(4):
                S.op("pe", lambda e: e.transpose(out=ps[5][:, 0:4], in_=Otm[:, pr * 128:(pr + 1) * 128], identity=ident_f[0:4, 0:4]),
                     reads=[B_Otm, B_ident], writes=[Bps[5]])
                S.op("act", lambda e: e.copy(out=mix[:, pr, 2048:2052], in_=ps[5][:, 0:4]), reads=[Bps[5]], writes=[B_mix])
        assert sb.off <= TOP2, (sb.off, TOP2)
    if STAGE >= 3 and 'M' in DBG:
        o_mix = dout("o_mix", [128, 8, NOWN], BF16)
        B_omix = Buf("o_mix"); out_bufs.append(B_omix)
        S.dma("sp", lambda e: e.dma_start(out=o_mix, in_=mix[:]), reads=[B_mix], writes=[B_omix])
    if STAGE >= 5:
        S.barrier()
        sb.release(m_p2)
        wo_bf = sb.tile([128, 8, 1024], BF16, "wo_bf"); B_wo = Buf("wo")
        wup_bf = sb.tile([128, 8, 4096], BF16, "wup_bf"); B_wup = Buf("wup")
        wdn_bf = sb.tile([128, 32, 1024], BF16, "wdn_bf"); B_wdn = Buf("wdn")
        wst3 = [sb.tile([128, 512], F32, "wst3_%d" % i) for i in range(2)]; B_wst3 = [Buf("wst3_0"), Buf("wst3_1")]
        cnt3 = 0
        jobs = [(w_o[kc * 128:(kc + 1) * 128, q * 512:(q + 1) * 512], wo_bf[:, kc, q * 512:(q + 1) * 512], B_wo) for kc in range(8) for q in range(2)]
        jobs += [(w_up[kc * 128:(kc + 1) * 128, q * 512:(q + 1) * 512], wup_bf[:, kc, q * 512:(q + 1) * 512], B_wup)
                 for kc in range(8) for q in range(8)]
        jobs += [(w_down[f * 128:(f + 1) * 128, q * 512:(q + 1) * 512], wdn_bf[:, f, q * 512:(q + 1) * 512], B_wdn)
                 for f in range(32) for q in range(2)]
        for src, dst, Bd in jobs:
            i = cnt3 % 2
            cnt3 += 1
            S.dma("sp", lambda e: e.dma_start(out=wst3[i][:], in_=src), writes=[B_wst3[i]])
            eng3 = ["dve", "act"][cnt3 % 2]
            if eng3 == "act":
                S.op("act", lambda e: e.copy(out=dst, in_=wst3[i][:]), reads=[B_wst3[i]], writes=[Bd])
            else:
                S.op(eng3, lambda e: e.tensor_copy(out=dst, in_=wst3[i][:]), reads=[B_wst3[i]], writes=[Bd])
        NT3 = NOWN // P3T
        x3 = sb.tile([128, 8, P3T], F32, "x3"); B_x3 = Buf("x3")
        hT = sb.tile([128, 8, P3T], F32, "hT"); B_hT = Buf("hT")
        sq3 = sb.tile([128, 8, P3T], BF16, "sq3"); B_sq3 = Buf("sq3")
        hn = sb.tile([128, 8, P3T], BF16, "hn"); B_hn = Buf("hn")
        rs3 = sb.tile([128, P3T], F32, "rs3"); B_rs3 = Buf("rs3")
        rr = [sb.tile([128, P3T], F32, "rr%d" % i) for i in range(2)]; B_rr = [Buf("rr0"), Buf("rr1")]
        uT = sb.tile([128, 32, P3T], BF16, "uT"); B_uT = Buf("uT")
        yT = x3; B_yT = B_x3
        o_y_v = o_y.rearrange("(m p) t -> p m t", p=128)
        for tt in range(NT3):
            cs = slice(tt * P3T, (tt + 1) * P3T)
            S.dma("sp", lambda e: e.dma_start(out=x3[:], in_=xT_own_v[:, :, cs]), writes=[B_x3])
            for m in range(8):
                pb = 1 + (m % 2)
                for kc in range(8):
                    S.op("pe", lambda e: e.matmul(ps[pb][:, 0:P3T], lhsT=wo_bf[:, kc, m * 128:(m + 1) * 128], rhs=mix[:, kc, cs],
                                                  start=(kc == 0), stop=(kc == 7)), reads=[B_wo, B_mix], writes=[Bps[pb]])
                S.op("dve", lambda e: e.tensor_tensor(out=hT[:, m, :], in0=ps[pb][:, 0:P3T], in1=x3[:, m, :], op=ALU.add),
                     reads=[Bps[pb], B_x3], writes=[B_hT])
            rms_rstd(hT, B_hT, P3T, sq3, B_sq3, rs3, B_rs3, 0)
            for kc in range(8):
                S.op("dve", lambda e: e.scalar_tensor_tensor(out=hn[:, kc, :], in0=hT[:, kc, :], scalar=gn[:, 8 + kc:9 + kc], in1=rs3[:],
                                                             op0=ALU.mult, op1=ALU.mult), reads=[B_hT, B_gn, B_rs3], writes=[B_hn])
            for f in range(32):
                pb = 3 + (f % 2)
                for kc in range(8):
                    S.op("pe", lambda e: e.matmul(ps[pb][:, 0:P3T], lhsT=wup_bf[:, kc, f * 128:(f + 1) * 128], rhs=hn[:, kc, :],
                                                  start=(kc == 0), stop=(kc == 7)), reads=[B_wup, B_hn], writes=[Bps[pb]])
                j = f % 2
                S.op("act", lambda e: e.activation(out=rr[j][:], in_=ps[pb][:, 0:P3T], func=AF.Relu), reads=[Bps[pb]], writes=[B_rr[j]])
                S.op("dve" if j == 0 else "pool", lambda e: e.tensor_tensor(out=uT[:, f, :], in0=rr[j][:], in1=rr[j][:], op=ALU.mult),
                     reads=[B_rr[j]], writes=[B_uT])
            for m in range(8):
                pb = 1 + (m % 2)
                for f in range(32):
                    S.op("pe", lambda e: e.matmul(ps[pb][:, 0:P3T], lhsT=wdn_bf[:, f, m * 128:(m + 1) * 128], rhs=uT[:, f, :],
                                                  start=(f == 0), stop=(f == 31)), reads=[B_wdn, B_uT], writes=[Bps[pb]])
                S.op("dve", lambda e: e.tensor_tensor(out=hT[:, m, :], in0=ps[pb][:, 0:P3T], in1=hT[:, m, :], op=ALU.add),
                     reads=[Bps[pb], B_hT], writes=[B_hT])
            rms_rstd(hT, B_hT, P3T, sq3, B_sq3, rs3, B_rs3, 0)
            for m in range(8):
                S.op("dve", lambda e: e.scalar_tensor_tensor(out=yT[:, m, :], in0=hT[:, m, :], scalar=gn[:, 16 + m:17 + m], in1=rs3[:],
                                                             op0=ALU.mult, op1=ALU.mult), reads=[B_hT, B_gn, B_rs3], writes=[B_yT])
            S.dma("sp", lambda e: e.dma_start(out=o_y_v[:, :, cs], in_=yT[:]), reads=[B_yT], writes=[B_oy])

    k.out_bufs = out_bufs
    k.locals = locals()
    return k


def finish_program(k):
    S = k.S
    S.wait_all("sp", k.out_bufs)
    S.emit()
    return k.nc


def kernel(x_prompt, x_sample, cache_k, cache_v, cache_logf, state_gla, page_table,
           norm1_g, w_in, fox_b_f, gla_w_gate_up, gla_b_gate, gla_norm_g, w_o,
           norm2_g, w_up, w_down, final_g):
    f32 = np.float32
    import ml_dtypes
    x_prompt = np.asarray(x_prompt, f32); x_sample = np.asarray(x_sample, f32)
    w_in0 = np.asarray(w_in, f32)[0]
    k = build_program()
    nc = finish_program(k)
    cols1 = np.concatenate([np.arange(C_FK, C_FK + 512), np.arange(C_GQ, C_GQ + 256), np.arange(C_GK, C_GK + 256),
                            np.arange(C_GLR, C_GLR + 16), np.arange(C_FV, C_FV + 512), np.arange(C_GV, C_GV + 512),
                            np.arange(C_FF, C_FF + 8)])
    w1 = np.ascontiguousarray(w_in0[:, cols1])
    cols2 = np.concatenate([np.arange(C_FQ, C_FQ + 512), np.arange(C_GG, C_GG + 512)])
    w2 = np.ascontiguousarray(w_in0[:, cols2])
    gains = np.zeros((128, 40), f32)
    gains[:, 0:8] = np.asarray(norm1_g, f32)[0].reshape(8, 128).T
    gains[:, 8:16] = np.asarray(norm2_g, f32)[0].reshape(8, 128).T
    gains[:, 16:24] = np.asarray(final_g, f32).reshape(8, 128).T
    gains[:, 24] = np.asarray(gla_norm_g, f32)[0]
    gains[:, 25:27] = -np.asarray(gla_b_gate, f32)[0].reshape(2, 128).T
    bfrep = np.broadcast_to(np.asarray(fox_b_f, f32)[0][None, :], (128, 8)).copy()
    wg = np.asarray(gla_w_gate_up, f32)[0]
    perm_s = np.concatenate([np.arange(C_FQ, C_FQ + 512), np.arange(C_FK, C_FK + 512), np.arange(C_FV, C_FV + 512),
                             np.arange(C_GQ, C_GQ + 256), np.arange(C_GK, C_GK + 256), np.arange(C_GV, C_GV + 512),
                             np.arange(C_GG, C_GG + 512), np.arange(C_GLR, C_GLR + 16), np.arange(C_FF, C_FF + 8)])
    w_s = np.ascontiguousarray(w_in0[:, perm_s])
    if STAGE >= 6:
        ck2 = np.asarray(cache_k, f32)[0].reshape(-1, 2048)
        cv2 = np.asarray(cache_v, f32)[0].reshape(-1, 2048)
        clf = np.ascontiguousarray(np.asarray(cache_logf, f32)[0].transpose(0, 2, 1)).reshape(-1, 1024)
        pt_all = np.asarray(page_table, np.int32)
        sg_all = np.asarray(state_gla, f32)[0]
    in_maps = []
    for c in range(8):
        b, r = c // 4, c % 4
        tiles = own_tiles(r)
        xo = np.concatenate([x_prompt[b, t * 512:(t + 1) * 512, :] for t in tiles] + [x_sample[4 * c:4 * c + 4, 0, :]], axis=0)
        masks = np.zeros((4, 16, 128, 512), ml_dtypes.bfloat16)
        onehot = np.zeros((4, 128, 64), f32)
        idx_o = np.zeros((128, 16), np.int32)
        mbias = np.zeros((128, 64), f32)
        kp = np.arange(128)[:, None]; qp = np.arange(512)[None, :]
        for s_, t in enumerate(tiles):
            for j in range(16):
                kb = 16 * s_ + j
                if kb < 4 * t:
                    masks[s_, j] = 1
                elif kb < 4 * t + 4:
                    masks[s_, j] = ((kb - 4 * t) * 128 + kp <= qp)
                else:
                    mbias[:, s_ * 16 + j] = -30000.0
            if t > 0:
                onehot[s_, 127, 4 * t - 1] = 1.0
            for h in range(GH):
                idx_o[:, s_ * 4 + h] = (t * GH + h) * 128 + np.arange(128)
        m = {"xT_full": np.ascontiguousarray(x_prompt[b].T), "xT_own": np.ascontiguousarray(xo.T),
             "w1": w1, "w2": w2, "w_o": np.asarray(w_o, f32)[0], "w_up": np.asarray(w_up, f32)[0],
             "w_down": np.asarray(w_down, f32)[0], "gains": gains, "bfrep": bfrep, "wg": wg,
             "masks": masks, "onehot": onehot, "idx_o": idx_o, "mbias": mbias, "w_s": w_s}
        if STAGE >= 6:
            m["ptab"] = np.ascontiguousarray(pt_all[4 * c:4 * c + 4, :].T)
            m["cache_k2"] = ck2; m["cache_v2"] = cv2; m["cache_lf"] = clf
            sg = sg_all[4 * c:4 * c + 4].reshape(4, 2, 2, 64, 128).transpose(1, 2, 3, 0, 4).reshape(2, 128, 4, 128)
            m["sgla"] = np.ascontiguousarray(sg)
        in_maps.append(m)
    used = set(k.used_inputs)
    in_maps = [{n: v for n, v in m.items() if n in used} for m in in_maps]
    res = run_bass_kernel_spmd(nc, in_maps, core_ids=list(range(8)))
    R = res.results
    y_prompt = np.zeros((2, SEQ, D), f32); y_sample = np.zeros((32, 1, D), f32)
    new_k = np.zeros((1, 2, SEQ, H, 64), f32); new_v = np.zeros((1, 2, SEQ, H, 64), f32)
    new_lf = np.zeros((1, 2, SEQ, H), f32); new_gla = np.zeros((1, 2, GH, 64, 128), f32)
    ks = np.zeros((1, 32, 1, H, 64), f32); vs = np.zeros((1, 32, 1, H, 64), f32)
    lfs = np.zeros((1, 32, 1, H), f32); glas = np.zeros((1, 32, GH, 64, 128), f32)
    for c in range(8):
        b, r = c // 4, c % 4
        o = R[c]
        if r == 0:
            new_k[0, b] = o["o_k"].T.reshape(SEQ, H, 64)
            new_v[0, b] = o["o_v"].reshape(SEQ, H, 64)
            new_lf[0, b] = o["o_lf"].reshape(128, H, NB).transpose(2, 0, 1).reshape(SEQ, H)
            new_gla[0, b] = o["o_gla"].reshape(GH, 64, 128)
        if "o_y" in o and STAGE >= 5:
            yT = o["o_y"]
            for s_, t in enumerate(own_tiles(r)):
                y_prompt[b, t * 512:(t + 1) * 512] = yT[:, s_ * 512:(s_ + 1) * 512].T
            y_sample[4 * c:4 * c + 4, 0] = yT[:, 2048:2052].T
        if STAGE >= 6:
            ks[0, 4 * c:4 * c + 4, 0] = o["o_ks"].reshape(4, H, 64)
            vs[0, 4 * c:4 * c + 4, 0] = o["o_vs"].reshape(4, H, 64)
            lfs[0, 4 * c:4 * c + 4, 0] = o["o_lfs"]
            glas[0, 4 * c:4 * c + 4] = o["o_glas"].reshape(2, 2, 64, 4, 128).transpose(3, 0, 1, 2, 4).reshape(4, GH, 64, 128)
    kernel.last_results = R
    return (y_prompt, y_sample, new_k, new_v, new_lf, new_gla, ks, vs, lfs, glas)
```
